# Optimizing a Trainium2 kernel written in Bass

```python
import jax, jax.numpy as jnp
from jax import lax
import numpy as np

D_MODEL = 1024
BATCH = 8
SEQ = 2048
DEPTH = 1

D_MIX = D_MODEL
ATT_HEADS = 4
ATT_HEAD_DIM = 128
ROPE_DIM = ATT_HEAD_DIM // 4
NOPE_DIM = ATT_HEAD_DIM - ROPE_DIM
KV_LATENT = 128
IDX_HEADS = 8
IDX_DIM = 64
IDX_ROPE_DIM = IDX_DIM // 4
IDX_TOPK_MAX = 256
Q_BLOCK = 128
LRU_WIDTH = D_MIX - ATT_HEADS * ATT_HEAD_DIM
LRU_BLOCKS = 8
LRU_BLOCK_DIM = LRU_WIDTH // LRU_BLOCKS
CONV_WIDTH = 4
LRU_C = 8.0
ROPE_THETA = 500000.0
PLE_DIM = 256
PEER_HEADS = 8
PEER_NKEYS = 128
PEER_EXPERTS = PEER_NKEYS * PEER_NKEYS
PEER_QDIM = 256
PEER_HALF = PEER_QDIM // 2
PEER_TOPK = 16
TOKEN_CHUNK = 128
EPS = 1e-6
IN_SIZES = (ATT_HEADS * ATT_HEAD_DIM, KV_LATENT, ROPE_DIM, IDX_HEADS * IDX_DIM, IDX_DIM, IDX_HEADS, LRU_WIDTH, LRU_WIDTH)
IN_TOTAL = sum(IN_SIZES)

kernel_name = 'hybrid_dsa_rglru_peer_block'


def rmsnorm(x, g):
    xf = x.astype(jnp.float32)
    y = xf * lax.rsqrt(jnp.mean(xf * xf, axis=-1, keepdims=True) + EPS)
    return (y * g.astype(jnp.float32)).astype(x.dtype)


def rope_angles(positions, dim):
    inv = ROPE_THETA ** (-jnp.arange(0, dim, 2, dtype=jnp.float32) / dim)
    ang = positions.astype(jnp.float32)[..., None] * inv
    return jnp.cos(ang), jnp.sin(ang)


def apply_rope(x, cos, sin):
    x1, x2 = jnp.split(x.astype(jnp.float32), 2, axis=-1)
    return jnp.concatenate([x1 * cos - x2 * sin, x2 * cos + x1 * sin], axis=-1).astype(x.dtype)


def partial_rope(x, cos, sin, rot):
    return jnp.concatenate([apply_rope(x[..., :rot], cos, sin), x[..., rot:]], axis=-1)


def dsa_attention(q, c_kv, k_rope, iq, ik, iw, g_kv, w_uk, w_uv, cos_a, sin_a, cos_i, sin_i):
    B, S = q.shape[0], q.shape[1]
    q = q.reshape(B, S, ATT_HEADS, ATT_HEAD_DIM)
    q_rope = apply_rope(q[..., :ROPE_DIM], cos_a[:, :, None], sin_a[:, :, None])
    q_lat = jnp.einsum('bshn,hcn->bshc', q[..., ROPE_DIM:], w_uk)
    q_cat = jnp.concatenate([q_lat, q_rope], axis=-1)
    kv_cat = jnp.concatenate([rmsnorm(c_kv, g_kv), apply_rope(k_rope, cos_a, sin_a)], axis=-1)
    iq = partial_rope(iq.reshape(B, S, IDX_HEADS, IDX_DIM), cos_i[:, :, None], sin_i[:, :, None], IDX_ROPE_DIM)
    ik = partial_rope(ik, cos_i, sin_i, IDX_ROPE_DIM)
    iw = iw * (IDX_HEADS ** -0.5 * IDX_DIM ** -0.5)
    top_k = min(IDX_TOPK_MAX, S // 4)
    n_blocks = S // Q_BLOCK
    scale = ATT_HEAD_DIM ** -0.5
    key_pos = jnp.arange(S)

    def to_blocks(a):
        return a.reshape(B, n_blocks, Q_BLOCK, *a.shape[2:]).swapaxes(0, 1)

    def block(args):
        qc, iqb, iwb, start = args
        t_pos = start + jnp.arange(Q_BLOCK)
        rel = jax.nn.relu(jnp.einsum('bthd,bsd->bths', iqb, ik))
        score = jnp.einsum('bths,bth->bts', rel, iwb).astype(jnp.float32)
        causal = key_pos[None, :] <= t_pos[:, None]
        score = jnp.where(causal[None], score, -jnp.inf)
        _, sel = lax.top_k(score, top_k)
        kv_sel = jax.vmap(lambda kv, i: kv[i])(kv_cat, sel)
        logits = jnp.einsum('bthc,btkc->bthk', qc, kv_sel).astype(jnp.float32) * scale
        valid = sel <= t_pos[None, :, None]
        logits = jnp.where(valid[:, :, None, :], logits, -jnp.inf)
        probs = jax.nn.softmax(logits, axis=-1).astype(qc.dtype)
        o_lat = jnp.einsum('bthk,btkc->bthc', probs, kv_sel[..., :KV_LATENT])
        return jnp.einsum('bthc,hcd->bthd', o_lat, w_uv).reshape(B, Q_BLOCK, ATT_HEADS * ATT_HEAD_DIM)

    starts = jnp.arange(n_blocks, dtype=jnp.int32) * Q_BLOCK
    out = lax.map(block, (to_blocks(q_cat), to_blocks(iq), to_blocks(iw), starts))
    return out.swapaxes(0, 1).reshape(B, S, ATT_HEADS * ATT_HEAD_DIM)


def rg_lru_branch(xb, gate, conv_w, conv_b, w_rg, b_rg, w_ig, b_ig, lam):
    B, S, C = xb.shape
    xc = lax.conv_general_dilated(xb, conv_w[:, None, :], window_strides=(1,), padding=((CONV_WIDTH - 1, 0),),
                                  dimension_numbers=('NWC', 'WIO', 'NWC'), feature_group_count=C) + conv_b
    xblk = xc.reshape(B, S, LRU_BLOCKS, LRU_BLOCK_DIM)
    r = jax.nn.sigmoid(jnp.einsum('bsni,nio->bsno', xblk, w_rg).reshape(B, S, C) + b_rg)
    i = jax.nn.sigmoid(jnp.einsum('bsni,nio->bsno', xblk, w_ig).reshape(B, S, C) + b_ig)
    log_a = -LRU_C * r.astype(jnp.float32) * jax.nn.softplus(-lam.astype(jnp.float32))
    a = jnp.exp(log_a)
    b = jnp.sqrt(1.0 - jnp.exp(2.0 * log_a)) * (i * xc).astype(jnp.float32)

    def combine(left, right):
        a1, b1 = left
        a2, b2 = right
        return a1 * a2, a2 * b1 + b2

    _, h = lax.associative_scan(combine, (a, b), axis=1)
    return h.astype(xb.dtype) * jax.nn.gelu(gate)


def peer_ffn(h, w_pq, k1, k2, u_tab, v_tab):
    B, S, D = h.shape
    q = (h @ w_pq).reshape(B, S, PEER_HEADS, 2, PEER_HALF)
    s1 = jnp.einsum('bshd,hnd->bshn', q[..., 0, :], k1)
    s2 = jnp.einsum('bshd,hnd->bshn', q[..., 1, :], k2)
    v1, i1 = lax.top_k(s1, PEER_TOPK)
    v2, i2 = lax.top_k(s2, PEER_TOPK)
    cand = (v1[..., :, None] + v2[..., None, :]).reshape(B, S, PEER_HEADS, PEER_TOPK * PEER_TOPK)
    vals, flat = lax.top_k(cand, PEER_TOPK)
    e1 = jnp.take_along_axis(i1, flat // PEER_TOPK, axis=-1)
    e2 = jnp.take_along_axis(i2, flat % PEER_TOPK, axis=-1)
    experts = e1 * PEER_NKEYS + e2
    gates = jax.nn.softmax(vals.astype(jnp.float32), axis=-1).astype(h.dtype)
    n_chunks = (B * S) // TOKEN_CHUNK
    hc = h.reshape(n_chunks, TOKEN_CHUNK, D)
    ec = experts.reshape(n_chunks, TOKEN_CHUNK, PEER_HEADS * PEER_TOPK)
    gc = gates.reshape(n_chunks, TOKEN_CHUNK, PEER_HEADS * PEER_TOPK)

    def chunk(args):
        hx, e, g = args
        act = jax.nn.gelu(jnp.einsum('ckd,cd->ck', u_tab[e], hx), approximate=False)
        return jnp.einsum('ck,ckd->cd', g * act, v_tab[e])

    return lax.map(chunk, (hc, ec, gc)).reshape(B, S, D)


def setup_inputs(seed: int = 0) -> dict:
    key = jax.random.key(seed)
    ks = jax.random.split(key, 32)
    f32 = jnp.float32
    nrm = lambda k, shape, s: jax.random.normal(k, shape, f32) * s
    u = jax.random.uniform(ks[14], (DEPTH, LRU_WIDTH), f32, 0.9, 0.999)
    sg = u ** (1.0 / LRU_C)
    lam = jnp.log(sg) - jnp.log1p(-sg)
    return {
        'x': nrm(ks[0], (BATCH, SEQ, D_MODEL), 1.0),
        'p': nrm(ks[1], (DEPTH, BATCH, SEQ, PLE_DIM), 1.0),
        'positions': jnp.broadcast_to(jnp.arange(SEQ, dtype=jnp.int32), (BATCH, SEQ)),
        'g_mix': 1.0 + nrm(ks[2], (DEPTH, D_MODEL), 0.01),
        'w_in': nrm(ks[3], (DEPTH, D_MODEL, IN_TOTAL), D_MODEL ** -0.5),
        'g_kv': 1.0 + nrm(ks[4], (DEPTH, KV_LATENT), 0.01),
        'w_uk': nrm(ks[5], (DEPTH, ATT_HEADS, KV_LATENT, NOPE_DIM), KV_LATENT ** -0.5),
        'w_uv': nrm(ks[6], (DEPTH, ATT_HEADS, KV_LATENT, ATT_HEAD_DIM), KV_LATENT ** -0.5),
        'conv_w': nrm(ks[7], (DEPTH, CONV_WIDTH, LRU_WIDTH), CONV_WIDTH ** -0.5),
        'conv_b': nrm(ks[8], (DEPTH, LRU_WIDTH), 0.01),
        'w_rg': nrm(ks[9], (DEPTH, LRU_BLOCKS, LRU_BLOCK_DIM, LRU_BLOCK_DIM), LRU_BLOCK_DIM ** -0.5),
        'b_rg': nrm(ks[10], (DEPTH, LRU_WIDTH), 0.01),
        'w_ig': nrm(ks[11], (DEPTH, LRU_BLOCKS, LRU_BLOCK_DIM, LRU_BLOCK_DIM), LRU_BLOCK_DIM ** -0.5),
        'b_ig': nrm(ks[12], (DEPTH, LRU_WIDTH), 0.01),
        'lru_lambda': lam,
        'w_out': nrm(ks[13], (DEPTH, D_MIX, D_MODEL), D_MIX ** -0.5),
        'g_ffn': 1.0 + nrm(ks[15], (DEPTH, D_MODEL), 0.01),
        'w_pq': nrm(ks[16], (DEPTH, D_MODEL, PEER_HEADS * PEER_QDIM), D_MODEL ** -0.5),
        'peer_k1': nrm(ks[17], (DEPTH, PEER_HEADS, PEER_NKEYS, PEER_HALF), PEER_HALF ** -0.5),
        'peer_k2': nrm(ks[18], (DEPTH, PEER_HEADS, PEER_NKEYS, PEER_HALF), PEER_HALF ** -0.5),
        'peer_u': nrm(ks[19], (DEPTH, PEER_EXPERTS, D_MODEL), D_MODEL ** -0.5),
        'peer_v': nrm(ks[20], (DEPTH, PEER_EXPERTS, D_MODEL), (PEER_HEADS * PEER_TOPK) ** -0.5),
        'w_ple': nrm(ks[21], (DEPTH, PLE_DIM, D_MODEL), PLE_DIM ** -0.5),
        'w_ple_gate': nrm(ks[22], (DEPTH, D_MODEL, D_MODEL), D_MODEL ** -0.5),
        'g_final': 1.0 + nrm(ks[23], (D_MODEL,), 0.01),
    }


def reference(x, p, positions, g_mix, w_in, g_kv, w_uk, w_uv, conv_w, conv_b, w_rg, b_rg, w_ig, b_ig,
              lru_lambda, w_out, g_ffn, w_pq, peer_k1, peer_k2, peer_u, peer_v, w_ple, w_ple_gate, g_final):
    cos_a, sin_a = rope_angles(positions, ROPE_DIM)
    cos_i, sin_i = rope_angles(positions, IDX_ROPE_DIM)
    splits = [int(s) for s in np.cumsum(IN_SIZES)[:-1]]
    for i in range(DEPTH):
        h = rmsnorm(x, g_mix[i])
        q, c_kv, k_rope, iq, ik, iw, xb, gate = jnp.split(h @ w_in[i], splits, axis=-1)
        att = dsa_attention(q, c_kv, k_rope, iq, ik, iw, g_kv[i], w_uk[i], w_uv[i], cos_a, sin_a, cos_i, sin_i)
        rec = rg_lru_branch(xb, gate, conv_w[i], conv_b[i], w_rg[i], b_rg[i], w_ig[i], b_ig[i], lru_lambda[i])
        x = x + jnp.concatenate([att, rec], axis=-1) @ w_out[i]
        x = x + peer_ffn(rmsnorm(x, g_ffn[i]), w_pq[i], peer_k1[i], peer_k2[i], peer_u[i], peer_v[i])
        x = x + jax.nn.sigmoid(x @ w_ple_gate[i]) * (p[i] @ w_ple[i])
    return rmsnorm(x, g_final)
```

```python
import math
import numpy as np
import concourse.bass as bass
import concourse.mybir as mybir
from concourse.bass_utils import run_bass_kernel_spmd

F32 = mybir.dt.float32
BF16 = mybir.dt.bfloat16
I32 = mybir.dt.int32
U32 = mybir.dt.uint32
U8 = mybir.dt.uint8
AF = mybir.ActivationFunctionType
OP = mybir.AluOpType
AX = mybir.AxisListType

S = 2048
D = 1024
NT = 16
EPS = 1e-6
NEG = -1.0e30
NCORES = 8
ROPE_THETA = 500000.0
ATT_SCALE = 128 ** -0.5
IW_SCALE = (8 ** -0.5) * (64 ** -0.5)

CH_Q = [0, 1, 2, 3]
CH_QP = [4, 5, 6, 7]
CH_K, CH_KP = 8, 9
CH_IQ = [10, 11, 12, 13]
CH_IQP = [14, 15, 16, 17]
CH_IK, CH_IKP = 18, 19
CH_XB = [20, 21, 22, 23]
CH_GT = [24, 25, 26, 27]
NCH = 28


def _esz(dt):
    return 2 if dt == BF16 else (1 if dt == U8 else 4)


class Sched:
    PHYS = {'pe': 'pe', 'act': 'act', 'dve': 'dve', 'pool': 'pool', 'pq': 'pool', 'sp': 'sp'}
    UNIT = {'pe': 1, 'act': 1, 'dve': 1, 'pool': 1, 'pq': 16, 'sp': 16}
    EPOCH = {'pe': 30000, 'act': 30000, 'dve': 30000, 'pool': 30000}
    DMA = ('pq', 'sp')
    NSLOT = 16

    def __init__(self, nc):
        self.nc = nc
        self.streams = {'pe': [], 'act': [], 'dve': [], 'pool': [], 'sp': []}
        self.cnt = {e: 0 for e in self.PHYS}
        self.wr = {}
        self.rd = {}
        self.seen = {p: {} for p in self.streams}
        self.seen_d = {p: {e: set() for e in self.DMA} for p in self.streams}
        self.sems = {}
        self.fence_cnt = {}

    def fence(self):
        self.fence_cnt = dict(self.cnt)

    def sem(self, eng, c):
        if eng in self.DMA:
            slot = (c - 1) % self.NSLOT
            key = (eng, 'slot', slot)
            val = ((c - 1) // self.NSLOT + 1) * 16
        else:
            c = self.rank[eng][c]
            ep = (c - 1) // self.EPOCH[eng]
            key = (eng, ep)
            val = c - ep * self.EPOCH[eng]
        if key not in self.sems:
            self.sems[key] = self.nc.alloc_semaphore("s_" + "_".join(str(x) for x in key))
        return self.sems[key], val

    def op(self, eng, fn, r=(), w=()):
        phys = self.PHYS[eng]
        waits_c = {}
        waits_d = set()

        def need(ec):
            e, c = ec
            if e in self.DMA:
                waits_d.add((e, c))
            elif c > waits_c.get(e, 0):
                waits_c[e] = c
        for e, c in self.fence_cnt.items():
            if c <= 0:
                continue
            if e in self.DMA:
                for cc in range(max(1, c - self.NSLOT + 1), c + 1):
                    need((e, cc))
            else:
                need((e, c))
        for k in r:
            if k in self.wr:
                need(self.wr[k])
        for k in w:
            if k in self.wr:
                need(self.wr[k])
            for e, c in self.rd.get(k, {}).items():
                need((e, c))
        self.cnt[eng] += 1
        me = self.cnt[eng]
        if eng in self.DMA and me > self.NSLOT:
            waits_d.add((eng, me - self.NSLOT))
        final = []
        for e, c in waits_c.items():
            if e == 'pe' and eng == 'pe':
                continue
            if self.seen[phys].get(e, 0) >= c:
                continue
            self.seen[phys][e] = c
            final.append((e, c))
        for (e, c) in sorted(waits_d):
            if c in self.seen_d[phys][e]:
                continue
            self.seen_d[phys][e].add(c)
            final.append((e, c))
        self.streams[phys].append((eng, me, fn, final))
        for k in r:
            self.rd.setdefault(k, {})[eng] = me
        for k in w:
            self.wr[k] = (eng, me)
            self.rd[k] = {}

    def emit(self, block):
        final_waits = []
        for e, c in self.cnt.items():
            if c <= 0:
                continue
            if e in self.DMA:
                final_waits += [(e, cc) for cc in range(max(1, c - self.NSLOT + 1), c + 1)]
            else:
                final_waits.append((e, c))
        targets = {e: set() for e in self.PHYS if e not in self.DMA}
        for phys in self.streams:
            for (leng, me, fn, waits) in self.streams[phys]:
                for (e, c) in waits:
                    if e not in self.DMA:
                        targets[e].add(c)
        for (e, c) in final_waits:
            if e not in self.DMA:
                targets[e].add(c)
        self.rank = {e: {c: i + 1 for i, c in enumerate(sorted(t))} for e, t in targets.items()}

        def body_for(phys):
            def body(eng):
                for (leng, me, fn, waits) in self.streams[phys]:
                    for (e, c) in waits:
                        sm, v = self.sem(e, c)
                        eng.wait_ge(sm, v)
                    ins = fn(eng)
                    if leng in self.DMA:
                        sm, v = self.sem(leng, me)
                        ins.then_inc(sm, 16)
                    elif me in targets[leng]:
                        sm, v = self.sem(leng, me)
                        ins.then_inc(sm, 1)
                if phys == 'sp':
                    for (e, c) in final_waits:
                        sm, v = self.sem(e, c)
                        eng.wait_ge(sm, v)
            return body
        block.tensor(body_for('pe'))
        block.scalar(body_for('act'))
        block.vector(body_for('dve'))
        block.gpsimd(body_for('pool'))
        block.sync(body_for('sp'))


class Arena:
    def __init__(self, nc, nbytes, ap=None, base=0):
        self.ap = nc.alloc_sbuf_tensor("arena", [128, nbytes], U8).ap() if ap is None else ap
        self.n = base + nbytes
        self.off = base
        self.peak = 0

    def alloc(self, shape, dt):
        n = 1
        for d in shape[1:]:
            n *= d
        b = n * _esz(dt)
        b32 = (b + 63) // 64 * 64
        assert self.off + b32 <= self.n, f"arena overflow {self.off}+{b32}>{self.n}"
        a = self.ap[:, self.off:self.off + b].bitcast(dt)
        self.off += b32
        self.peak = max(self.peak, self.off)
        if len(shape) == 3:
            a = a.rearrange("p (a b) -> p a b", b=shape[2])
        elif len(shape) == 4:
            a = a.rearrange("p (a b c) -> p a b c", b=shape[2], c=shape[3])
        if shape[0] < 128:
            a = a[0:shape[0]]
        return a

    def mark(self):
        return self.off

    def reset(self, m):
        self.off = m


def bc_last(ap, n):
    return bass.AP(ap.tensor, ap.offset, [list(x) for x in ap.ap] + [[0, n]])


def bc_mid(ap, axis, n):
    l = [list(x) for x in ap.ap]
    return bass.AP(ap.tensor, ap.offset, l[:axis] + [[0, n]] + l[axis:])


def build(stop='all', dbg=None):
    try:
        return _build(stop, dbg)
    except _StopBuild as ex:
        return ex.nc


class _StopBuild(Exception):
    def __init__(self, nc):
        self.nc = nc


def _build(stop='all', dbg=None):
    nc = bass.Bass("TRN2", target_bir_lowering=False)
    dt_in = lambda name, shape, dt=F32: nc.dram_tensor(name, shape, dt, kind="ExternalInput").ap()
    x_d = dt_in("x", [S, D])
    pT_d = dt_in("pT", [256, S])
    pos_d = dt_in("pos", [1, S], I32)
    wfm_d = dt_in("wfm", [NCH, 128, 8, 128])
    wtm_d = dt_in("wtm", [128, 8, 136])
    gvec_d = dt_in("gvec", [128, 24])
    ropec_d = dt_in("ropec", [128, 4])
    ident_d = dt_in("ident", [128, 128])
    cmask_d = dt_in("cmask", [128, 128])
    iota_d = dt_in("iota", [128, 128])
    gkv_d = dt_in("gkv", [1, 128])
    wukT_d = dt_in("wukT", [128, 4, 128])
    wuv_d = dt_in("wuv", [128, 4, 128])
    lruv_d = dt_in("lruv", [128, 36])
    wbd_d = dt_in("wbd", [128, 8, 128])
    wout_d = dt_in("wout", [128, 8, 1024])
    wpq_d = dt_in("wpq", [16, 128, 8, 128])
    kT_d = dt_in("kT", [128, 16, 128])
    uT_d = dt_in("uT", [128, 128, 8, 128])
    v_d = dt_in("v", [16384, D])
    wple_d = dt_in("wple", [128, 2, 1024])
    wpg_d = dt_in("wpg", [128, 8, 1024])
    gfin_d = dt_in("gfin", [1, D])
    out_d = nc.dram_tensor("out", [S, D], F32, kind="ExternalOutput").ap()
    dbg_d = None
    if dbg is not None:
        dbg_d = nc.dram_tensor("dbg", list(dbg), F32, kind="ExternalOutput").ap()
    xmid_d = nc.dram_tensor("xmid", [S, D], F32).ap()
    gd_d = nc.dram_tensor("gd", [8, 128, 128, 256], BF16).ap()

    wc_d = nc.dram_tensor("wc16", [128, 128, 2048], BF16).ap()
    wpqb_d = nc.dram_tensor("wpq16", [16, 128, 8, 128], BF16).ap()
    sch = Sched(nc)
    ar = Arena(nc, 204 * 1024)
    ps = [nc.alloc_psum_tensor(f"ps{i}", [128, 512], F32).ap() for i in range(8)]
    psb = [p.bitcast(BF16) for p in ps]
    op = sch.op

    ident_f = ar.alloc([128, 128], F32)
    ident = ar.alloc([128, 128], BF16)
    cmask = ar.alloc([128, 128], F32)
    iota_f = ar.alloc([128, 128], F32)
    iota_b = ar.alloc([128, 128], BF16)
    gvec = ar.alloc([128, 24], F32)
    ropec = ar.alloc([128, 4], F32)
    lruv = ar.alloc([128, 36], F32)
    op('sp', lambda e: e.dma_start(out=ident_f, in_=ident_d), w=['ident_f'])
    op('sp', lambda e: e.dma_start(out=cmask, in_=cmask_d), w=['cmask'])
    op('sp', lambda e: e.dma_start(out=iota_f, in_=iota_d), w=['iota_f'])
    op('sp', lambda e: e.dma_start(out=gvec, in_=gvec_d), w=['gvec'])
    op('sp', lambda e: e.dma_start(out=ropec, in_=ropec_d), w=['ropec'])
    op('sp', lambda e: e.dma_start(out=lruv, in_=lruv_d), w=['lruv'])
    op('dve', lambda e: e.tensor_copy(out=ident, in_=ident_f), r=['ident_f'], w=['ident'])
    op('dve', lambda e: e.tensor_copy(out=iota_b, in_=iota_f), r=['iota_f'], w=['iota_b'])

    junk = ar.alloc([128, 2048], BF16)
    epsc = ar.alloc([128, 1], F32)
    op('dve', lambda e: e.memset(epsc, EPS), w=['epsc'])
    onec = ar.alloc([128, 1], F32)
    op('dve', lambda e: e.memset(onec, 1.0), w=['onec'])
    negb = ar.alloc([128, 1], F32)
    op('dve', lambda e: e.memset(negb, -3.141589), w=['negb'])
    stat = ar.alloc([128, 64], F32)
    h_off = ar.off
    hT = ar.alloc([128, 8, S], BF16)
    cat_off = ar.off
    catT = ar.alloc([128, 8, S], BF16)
    cat_alias = [ar.ap[:, cat_off + i * 8192:cat_off + (i + 1) * 8192].bitcast(F32) for i in range(2)]

    def rmsnorm_to_T(src_tile_fn, gcol, dstT, tag):
        m = ar.mark()
        xn = [ar.alloc([128, D], BF16) for _ in range(2)]
        for i in range(NT):
            xt, xkey = src_tile_fn(i)
            ss = stat[:, i:i + 1]
            rs = stat[:, 16 + i:17 + i]
            op('act', lambda e, xt=xt, ss=ss: e.activation(out=junk[:, 0:D], in_=xt, func=AF.Square, accum_out=ss),
               r=[xkey], w=['junk', (tag, 'ss', i)])
            op('act', lambda e, ss=ss, rs=rs: e.activation(out=rs, in_=ss, func=AF.Sqrt, scale=1.0 / D, bias=epsc),
               r=[(tag, 'ss', i), 'epsc'], w=[(tag, 'rs', i)])
            op('dve', lambda e, rs=rs: e.reciprocal(out=rs, in_=rs),
               r=[(tag, 'rs', i)], w=[(tag, 'rs', i)])
            xb = xn[i % 2]
            op('dve', lambda e, xb=xb, xt=xt, rs=rs: e.tensor_scalar(out=xb, in0=xt, scalar1=rs, scalar2=None,
                                                                      op0=OP.mult),
               r=[xkey, (tag, 'rs', i)], w=[(tag, 'xn', i % 2)])
            bank = i % 2
            for k in range(8):
                op('pe', lambda e, k=k, xb=xb, bank=bank: e.transpose(out=psb[bank][:, k * 128:(k + 1) * 128],
                                                                       in_=xb[:, k * 128:(k + 1) * 128],
                                                                       identity=ident),
                   r=[(tag, 'xn', i % 2), 'ident'], w=[('ps', bank)])
            g3 = bc_last(gvec[:, gcol:gcol + 8], 128)
            op('dve', lambda e, i=i, bank=bank, g3=g3: e.tensor_tensor(
                out=dstT[:, :, i * 128:(i + 1) * 128],
                in0=psb[bank].rearrange("p (k t) -> p k t", k=8), in1=g3, op=OP.mult),
               r=[('ps', bank), 'gvec'], w=[(tag, 'T', i)])
        ar.reset(m)

    mA = ar.mark()
    xts = [ar.alloc([128, D], F32) for _ in range(2)]

    def srcA(i):
        xt = xts[i % 2]
        op('sp', lambda e, xt=xt, i=i: e.dma_start(out=xt, in_=x_d[i * 128:(i + 1) * 128, :]), w=[('xt', i % 2)])
        return xt, ('xt', i % 2)
    rmsnorm_to_T(srcA, 0, hT, 'A')
    ar.reset(mA)
    sch.fence()
    hT_keys = [('A', 'T', i) for i in range(NT)]

    def finish_dbg(src_ap, keys, rows, cols, conv=None):
        op('sp', lambda e: e.dma_start(out=dbg_d[0:rows, 0:cols], in_=src_ap), r=keys, w=['dbg'])

    def dbg_point(name, ap, keys, rows=128, ncols=S):
        if stop != name:
            return
        op('pq', lambda e: e.dma_start(out=dbg_d[0:rows, 0:ncols], in_=ap), r=keys, w=['dbg'])
        with nc.Block() as block:
            sch.emit(block)
        raise _StopBuild(nc)

    def dbg_multi(name, items):
        if stop != name:
            return
        for (ap, keys, c0, ncol) in items:
            op('pq', lambda e, ap=ap, c0=c0, ncol=ncol: e.dma_start(out=dbg_d[0:128, c0:c0 + ncol], in_=ap), r=keys, w=[('dbg', c0)])
        with nc.Block() as block:
            sch.emit(block)
        raise _StopBuild(nc)

    if stop == 'A':
        m = ar.mark()
        tmp = ar.alloc([128, S], F32)
        op('dve', lambda e: e.tensor_copy(out=tmp, in_=hT[:, 3, :]), r=hT_keys, w=['tmp'])
        finish_dbg(tmp, ['tmp'], 128, S)
        with nc.Block() as block:
            sch.emit(block)
        return nc

    wb = [ar.alloc([128, 8, 128], BF16) for _ in range(4)]
    wctr = [0]

    def load_w(src_ap):
        slot = wctr[0] % 4
        wctr[0] += 1
        t = wb[slot]
        op('pq', lambda e: e.dma_start(out=t, in_=src_ap), w=[('wb', slot)])
        return t, ('wb', slot)
    pctr = [0]

    def next_bank(lo=0, n=4):
        b = lo + pctr[0] % n
        pctr[0] += 1
        return b

    def cols(n, w=512):
        return slice(n * w, (n + 1) * w)

    def proj_mm(wt, wkey, n, bank, M=128, srcT=None, skeys=None):
        srcT = hT if srcT is None else srcT
        skeys = hT_keys if skeys is None else skeys
        for k in range(8):
            op('pe', lambda e, k=k: e.matmul(ps[bank][0:M, :], wt[:, k, 0:M], srcT[:, k, cols(n)],
                                             start=(k == 0), stop=(k == 7)),
               r=[wkey] + skeys[n * 4:(n + 1) * 4], w=[('ps', bank)])

    mL = ar.mark()
    wbd = ar.alloc([128, 8, 128], BF16)
    op('pq', lambda e: e.dma_start(out=wbd, in_=wbd_d), w=['wbd'])
    spc = ar.alloc([128, 8], F32)
    op('act', lambda e: e.activation(out=spc[:, 0:4], in_=lruv[:, 28:32], func=AF.Exp, scale=-1.0),
       r=['lruv'], w=['spc'])
    op('act', lambda e: e.activation(out=spc[:, 0:4], in_=spc[:, 0:4], func=AF.Ln, bias=onec),
       r=['spc', 'onec'], w=['spc'])
    op('dve', lambda e: e.tensor_scalar(out=spc[:, 4:8], in0=spc[:, 0:4], scalar1=-16.0, scalar2=None, op0=OP.mult),
       r=['spc'], w=['spc'])
    op('dve', lambda e: e.tensor_scalar(out=spc[:, 0:4], in0=spc[:, 0:4], scalar1=-8.0, scalar2=None, op0=OP.mult),
       r=['spc'], w=['spc'])
    Lb = {nm: ar.alloc([128, S], F32) for nm in ['xb', 'gate', 'xc', 'r', 'i', 'a', 'b']}
    xcb = ar.alloc([128, S], BF16)
    for c in range(4):
        for nm, chs in (('xb', CH_XB), ('gate', CH_GT)):
            wt, wk = load_w(wfm_d[chs[c]])
            for n in range(4):
                bank = next_bank()
                proj_mm(wt, wk, n, bank)
                op('act', lambda e, nm=nm, n=n, bank=bank: e.activation(out=Lb[nm][:, cols(n)], in_=ps[bank],
                                                                         func=AF.Copy),
                   r=[('ps', bank)], w=[nm])
        xb_, gate_, xc_, r_, i_, a_, b_ = (Lb[k] for k in ['xb', 'gate', 'xc', 'r', 'i', 'a', 'b'])
        cw = lambda tap, c=c: lruv[:, c * 4 + tap:c * 4 + tap + 1]
        op('dve', lambda e, c=c, cw=cw: e.tensor_scalar(out=xc_, in0=xb_, scalar1=cw(3), scalar2=lruv[:, 16 + c:17 + c],
                                                        op0=OP.mult, op1=OP.add),
           r=['xb', 'lruv'], w=['xc'])
        for sft in (1, 2, 3):
            op('dve', lambda e, sft=sft, cw=cw: e.scalar_tensor_tensor(out=xc_[:, sft:], in0=xb_[:, :S - sft],
                                                                       scalar=cw(3 - sft), in1=xc_[:, sft:],
                                                                       op0=OP.mult, op1=OP.add),
               r=['xb', 'xc', 'lruv'], w=['xc'])
        if c == 1:
            dbg_point('C:xb', xb_, ['xb'])
            dbg_point('C:gate', gate_, ['gate'])
            dbg_point('C:xc', xc_, ['xc'])
        op('act', lambda e: e.activation(out=xcb, in_=xc_, func=AF.Copy), r=['xc'], w=['xcb'])
        for gi, (nm, bcol) in enumerate((('r', 20), ('i', 24))):
            for n in range(4):
                bank = next_bank()
                op('pe', lambda e, gi=gi, c=c, n=n, bank=bank: e.matmul(ps[bank], wbd[:, gi * 4 + c, :],
                                                                         xcb[:, cols(n)], start=True, stop=True),
                   r=['wbd', 'xcb'], w=[('ps', bank)])
                op('act', lambda e, nm=nm, n=n, bank=bank, bcol=bcol, c=c: e.activation(
                    out=Lb[nm][:, cols(n)], in_=ps[bank], func=AF.Sigmoid, bias=lruv[:, bcol + c:bcol + c + 1]),
                   r=[('ps', bank), 'lruv'], w=[nm])
        op('act', lambda e, c=c: e.activation(out=a_, in_=r_, func=AF.Exp, scale=spc[:, c:c + 1]),
           r=['r', 'spc'], w=['a'])
        op('act', lambda e, c=c: e.activation(out=b_, in_=r_, func=AF.Exp, scale=spc[:, 4 + c:5 + c]),
           r=['r', 'spc'], w=['b'])
        op('act', lambda e: e.activation(out=b_, in_=b_, func=AF.Sqrt, scale=-1.0, bias=onec),
           r=['b', 'onec'], w=['b'])
        if c == 1:
            dbg_point('C:r', r_, ['r'])
            dbg_point('C:i', i_, ['i'])
            dbg_point('C:a', a_, ['a'])
            dbg_point('C:m', b_, ['b'])
        op('pool', lambda e: e.tensor_tensor(out=i_, in0=i_, in1=xc_, op=OP.mult), r=['i', 'xc'], w=['i'])
        op('dve', lambda e: e.tensor_tensor(out=b_, in0=b_, in1=i_, op=OP.mult), r=['b', 'i'], w=['b'])
        op('dve', lambda e: e.tensor_tensor_scan(out=r_, data0=a_, data1=b_, initial=0.0, op0=OP.mult, op1=OP.add),
           r=['a', 'b'], w=['r'])
        if c == 1:
            dbg_point('C:h', r_, ['r'])
        op('act', lambda e: e.activation(out=a_, in_=gate_, func=AF.Square), r=['gate'], w=['a'])
        op('dve', lambda e: e.tensor_scalar(out=a_, in0=a_, scalar1=0.044715, scalar2=1.0, op0=OP.mult, op1=OP.add),
           r=['a'], w=['a'])
        op('dve', lambda e: e.tensor_tensor(out=a_, in0=a_, in1=gate_, op=OP.mult), r=['a', 'gate'], w=['a'])
        op('act', lambda e: e.activation(out=a_, in_=a_, func=AF.Sigmoid, scale=1.5957691216057308),
           r=['a'], w=['a'])
        op('pool', lambda e: e.tensor_tensor(out=b_, in0=r_, in1=gate_, op=OP.mult), r=['r', 'gate'], w=['b'])
        op('dve', lambda e, c=c: e.tensor_tensor(out=catT[:, 4 + c, :], in0=b_, in1=a_, op=OP.mult),
           r=['a', 'b'], w=[('cat', 4 + c)])
    if stop == 'C':
        tmp = ar.alloc([128, S], F32)
        op('dve', lambda e: e.tensor_copy(out=tmp, in_=catT[:, 5, :]), r=[('cat', 5)], w=['tmp'])
        finish_dbg(tmp, ['tmp'], 128, S)
        with nc.Block() as block:
            sch.emit(block)
        return nc
    ar.reset(mL)
    sch.fence()

    mAtt = ar.mark()
    q_rotT = ar.alloc([128, 4, S], BF16)
    q_latT = ar.alloc([128, 4, S], BF16)
    iq_rotT = ar.alloc([128, 4, S], BF16)
    k_rotT = ar.alloc([128, S], BF16)
    ik2T = ar.alloc([128, S], BF16)
    kv_latT = ar.alloc([128, S], BF16)
    kv1 = ar.alloc([128, 16, 130], BF16)
    iw_sb = ar.alloc([128, 16, 8], F32)
    wukT = ar.alloc([128, 4, 128], BF16)
    wuv = ar.alloc([128, 4, 128], BF16)
    gkv_bc = ar.alloc([128, 128], F32)
    wtm = ar.alloc([128, 8, 136], BF16)
    op('pq', lambda e: e.dma_start(out=wukT, in_=wukT_d), w=['wukT'])
    op('pq', lambda e: e.dma_start(out=wuv, in_=wuv_d), w=['wuv'])
    op('pq', lambda e: e.dma_start(out=wtm, in_=wtm_d), w=['wtm'])
    op('sp', lambda e: e.dma_start(out=gkv_bc, in_=bass.AP(gkv_d.tensor, 0, [[0, 128], [1, 128]])), w=['gkv'])
    mTab = ar.mark()
    posf = ar.alloc([128, S], F32)
    tabs = cat_alias + [ar.alloc([128, S], F32) for _ in range(2)]
    ttmp = ar.alloc([128, S], F32)
    op('pq', lambda e: e.dma_start(out=posf, in_=bass.AP(pos_d.tensor, 0, [[0, 128], [1, S]])), w=['posf'])
    SC = 0.999999
    MAGIC = 12582912.0
    ttmp2 = t2buf = None
    for (tc_, ts_, icol, scol) in ((tabs[0], tabs[1], 0, 1), (tabs[2], tabs[3], 2, 3)):
        for (dst, shift, sg) in ((ts_, 0.0, True), (tc_, 0.25, False)):
            op('dve', lambda e, icol=icol, shift=shift: e.tensor_scalar(out=ttmp, in0=posf,
                                                                        scalar1=ropec[:, icol:icol + 1],
                                                                        scalar2=shift, op0=OP.mult, op1=OP.add),
               r=['posf', 'ropec'], w=['ttmp'])
            op('dve', lambda e, dst=dst: e.tensor_scalar(out=dst, in0=ttmp, scalar1=MAGIC, scalar2=None, op0=OP.add),
               r=['ttmp'], w=['tabs'])
            op('dve', lambda e, dst=dst: e.tensor_scalar(out=dst, in0=dst, scalar1=-MAGIC, scalar2=None, op0=OP.add),
               r=['tabs'], w=['tabs'])
            op('dve', lambda e, dst=dst: e.tensor_tensor(out=ttmp, in0=ttmp, in1=dst, op=OP.subtract),
               r=['tabs', 'ttmp'], w=['ttmp'])
            op('act', lambda e, dst=dst: e.activation(out=dst, in_=ttmp, func=AF.Sin, scale=2 * math.pi * SC),
               r=['ttmp'], w=['tabs'])
            if sg:
                op('dve', lambda e, dst=dst, scol=scol: e.tensor_scalar(out=dst, in0=dst,
                                                                        scalar1=ropec[:, scol:scol + 1],
                                                                        scalar2=None, op0=OP.mult),
                   r=['tabs', 'ropec'], w=['tabs'])

    t1s = [ar.alloc([128, 512], F32) for _ in range(2)]
    t2s = [ar.alloc([128, 512], F32) for _ in range(2)]
    rctr = [0]

    def rope_proj(chX, chP, Tc, Ts, dst_fn, dkey, M=128):
        wX, kX = load_w(wfm_d[chX])
        wP, kP = load_w(wfm_d[chP])
        for n in range(4):
            bX = next_bank()
            proj_mm(wX, kX, n, bX, M)
            bP = next_bank()
            proj_mm(wP, kP, n, bP, M)
            sl = rctr[0] % 2
            rctr[0] += 1
            t1, t2 = t1s[sl], t2s[sl]
            op('dve', lambda e, bX=bX, n=n, t1=t1: e.tensor_tensor(out=t1[0:M], in0=ps[bX][0:M], in1=Tc[0:M, cols(n)],
                                                                   op=OP.mult),
               r=[('ps', bX), 'tabs'], w=[('t1', sl)])
            op('dve', lambda e, bP=bP, n=n, t2=t2: e.tensor_tensor(out=t2[0:M], in0=ps[bP][0:M], in1=Ts[0:M, cols(n)],
                                                                   op=OP.mult),
               r=[('ps', bP), 'tabs'], w=[('t2', sl)])
            op('pool', lambda e, n=n, t1=t1, t2=t2: e.tensor_tensor(out=dst_fn(n), in0=t1[0:M], in1=t2[0:M], op=OP.add),
               r=[('t1', sl), ('t2', sl)], w=[dkey])

    for h in range(4):
        rope_proj(CH_Q[h], CH_QP[h], tabs[0], tabs[1], lambda n, h=h: q_rotT[:, h, cols(n)], ('qrot', h))
        for n in range(4):
            bank = next_bank()
            op('pe', lambda e, h=h, n=n, bank=bank: e.matmul(ps[bank], wukT[:, h, :], q_rotT[:, h, cols(n)],
                                                             start=True, stop=True),
               r=['wukT', ('qrot', h)], w=[('ps', bank)])
            op('act', lambda e, h=h, n=n, bank=bank: e.activation(out=q_latT[:, h, cols(n)], in_=ps[bank],
                                                                  func=AF.Copy),
               r=[('ps', bank)], w=[('qlat', h)])
    rope_proj(CH_K, CH_KP, tabs[0], tabs[1], lambda n: k_rotT[0:32, cols(n)], 'krot', M=32)
    for m_ in range(4):
        rope_proj(CH_IQ[m_], CH_IQP[m_], tabs[2], tabs[3], lambda n, m_=m_: iq_rotT[:, m_, cols(n)], ('iqrot', m_))
    rope_proj(CH_IK, CH_IKP, tabs[2], tabs[3], lambda n: ik2T[:, cols(n)], 'ik2')
    op('pool', lambda e: e.memset(kv1[:, :, 128:130], 1.0), w=[('kv1', i) for i in range(NT)])
    for i in range(NT):
        bank = next_bank()
        for k in range(8):
            op('pe', lambda e, k=k, i=i, bank=bank: e.matmul(ps[bank][:, 0:136], hT[:, k, i * 128:(i + 1) * 128],
                                                             wtm[:, k, :], start=(k == 0), stop=(k == 7)),
               r=['wtm', hT_keys[i]], w=[('ps', bank)])
        ssq = stat[:, 32 + i:33 + i]
        rk = stat[:, 48 + i:49 + i]
        op('act', lambda e, bank=bank, ssq=ssq: e.activation(out=junk[:, 0:128], in_=ps[bank][:, 0:128],
                                                             func=AF.Square, accum_out=ssq),
           r=[('ps', bank)], w=['junk', ('kss', i)])
        op('act', lambda e, ssq=ssq, rk=rk: e.activation(out=rk, in_=ssq, func=AF.Sqrt, scale=1.0 / 128, bias=epsc),
           r=[('kss', i), 'epsc'], w=[('krs', i)])
        op('dve', lambda e, rk=rk: e.reciprocal(out=rk, in_=rk), r=[('krs', i)], w=[('krs', i)])
        op('dve', lambda e, bank=bank, rk=rk, i=i: e.scalar_tensor_tensor(out=kv1[:, i, 0:128], in0=ps[bank][:, 0:128],
                                                                          scalar=rk, in1=gkv_bc,
                                                                          op0=OP.mult, op1=OP.mult),
           r=[('ps', bank), ('krs', i), 'gkv'], w=[('kv1', i)])
        op('dve', lambda e, bank=bank, i=i: e.tensor_scalar(out=iw_sb[:, i, :], in0=ps[bank][:, 128:136],
                                                            scalar1=IW_SCALE, scalar2=None, op0=OP.mult),
           r=[('ps', bank)], w=[('iw', i)])
        bT = next_bank()
        op('pe', lambda e, i=i, bT=bT: e.transpose(out=psb[bT][:, 0:128], in_=kv1[:, i, 0:128], identity=ident),
           r=[('kv1', i), 'ident'], w=[('ps', bT)])
        op('act', lambda e, i=i, bT=bT: e.activation(out=kv_latT[:, i * 128:(i + 1) * 128], in_=psb[bT][:, 0:128],
                                                     func=AF.Copy),
           r=[('ps', bT)], w=[('kvT', i)])
    dbg_point('E:qrot', q_rotT[:, 1, :], [('qrot', 1)])
    dbg_point('E:qlat', q_latT[:, 1, :], [('qlat', 1)])
    dbg_point('E:krot', k_rotT[0:32, :], ['krot'], rows=32)
    dbg_point('E:iqrot', iq_rotT[:, 1, :], [('iqrot', 1)])
    dbg_point('E:ik2', ik2T, ['ik2'])
    dbg_point('E:kvT', kv_latT, [('kvT', i) for i in range(NT)])
    dbg_point('E:tabs', tabs[1], ['tabs'])
    ar.reset(mTab)
    sch.fence()

    o_nT = ar.alloc([128, 4, S], BF16)
    score = ar.alloc([128, S], F32)
    relu_sb = [ar.alloc([128, 512], F32) for _ in range(2)]
    maskb = ar.alloc([128, S], BF16)
    maskT = ar.alloc([128, 16, 128], BF16)
    Eb = [ar.alloc([128, 512], BF16) for _ in range(2)]
    PTb = [ar.alloc([128, 4, 128], BF16) for _ in range(2)]
    o_n = ar.alloc([128, 4, 128], BF16)
    bis = ar.alloc([128, 8], F32)
    recyc = ['tabs', 'posf', 'posi', 'ttmp', ('t1', 0), ('t1', 1), ('t2', 0), ('t2', 1)]
    NIT = 26
    RNG = 64.0
    ectr = [0]
    har = Arena(nc, 32768, ap=ar.ap, base=h_off)
    scoreB = [score, har.alloc([128, S], F32)]
    score2 = har.alloc([128, S], F32)
    maskbB = [maskb, har.alloc([128, S], BF16)]
    junkB = har.alloc([128, S], BF16)
    reluB = relu_sb + [har.alloc([128, 512], F32) for _ in range(2)]
    bisB = [bis, har.alloc([128, 8], F32)]
    rctr2 = [0]

    def att_scores(j, b):
        op('pq', lambda e: e.dma_start(out=wc_d[8 * j:8 * j + 8, :, 0:1024],
                                       in_=uT_d[8 * j:8 * j + 8].rearrange("c p k e -> c p (k e)")),
           w=[('ub', j)])
        op('pq', lambda e: e.dma_start(out=wc_d[:, 8 * j:8 * j + 8, 1024:2048],
                                       in_=v_d[1024 * j:1024 * (j + 1), :].rearrange("(e1 c) d -> c e1 d", c=128)),
           w=[('vbf', j)])
        if j == 0:
            op('pq', lambda e: e.dma_start(out=wpqb_d.rearrange("c p k e -> (c p) (k e)"),
                                           in_=wpq_d.rearrange("c p k e -> (c p) (k e)")), w=['wpqb'])
        Sj = (j + 1) * 128
        tq = slice(j * 128, (j + 1) * 128)
        sc_ = scoreB[b]
        skey = ('score', b)
        nch = (Sj + 511) // 512
        for cc in range(nch):
            w_ = min(512, Sj - cc * 512)
            cs = slice(cc * 512, cc * 512 + w_)
            for hh in range(8):
                pb = (hh % 2) * 64
                bank = next_bank(0, 2)
                op('pe', lambda e, hh=hh, pb=pb, bank=bank, cs=cs, w_=w_: e.matmul(
                    ps[bank][:, 0:w_], iq_rotT[pb:pb + 64, hh // 2, tq], ik2T[pb:pb + 64, cs], start=True, stop=True),
                   r=[('iqrot', hh // 2), 'ik2'], w=[('ps', bank)])
                ri = rctr2[0] % 4
                rctr2[0] += 1
                rl = reluB[ri]
                op('act', lambda e, bank=bank, rl=rl, w_=w_: e.activation(out=rl[:, 0:w_], in_=ps[bank][:, 0:w_],
                                                                          func=AF.Relu),
                   r=[('ps', bank)], w=[('relu', ri)])
                eng = 'dve'
                dst = sc_
                dkey = skey
                if hh == 0:
                    op(eng, lambda e, rl=rl, cs=cs, w_=w_, hh=hh, dst=dst: e.tensor_scalar(
                        out=dst[:, cs], in0=rl[:, 0:w_], scalar1=iw_sb[:, j, hh:hh + 1], scalar2=None, op0=OP.mult),
                       r=[('relu', ri), ('iw', j)], w=[dkey])
                elif eng == 'dve':
                    op(eng, lambda e, rl=rl, cs=cs, w_=w_, hh=hh, dst=dst: e.scalar_tensor_tensor(
                        out=dst[:, cs], in0=rl[:, 0:w_], scalar=iw_sb[:, j, hh:hh + 1], in1=dst[:, cs],
                        op0=OP.mult, op1=OP.add),
                       r=[('relu', ri), ('iw', j), dkey], w=[dkey])
                else:
                    op(eng, lambda e, rl=rl, w_=w_, hh=hh: e.tensor_scalar(
                        out=rl[:, 0:w_], in0=rl[:, 0:w_], scalar1=iw_sb[:, j, hh:hh + 1], scalar2=None, op0=OP.mult),
                       r=[('relu', ri), ('iw', j)], w=[('relu', ri)])
                    op(eng, lambda e, rl=rl, cs=cs, w_=w_, dst=dst: e.tensor_tensor(
                        out=dst[:, cs], in0=dst[:, cs], in1=rl[:, 0:w_], op=OP.add),
                       r=[('relu', ri), dkey], w=[dkey])
        op('dve', lambda e: e.tensor_tensor(out=sc_[:, tq], in0=sc_[:, tq], in1=cmask, op=OP.add),
           r=[skey, 'cmask'], w=[skey])

    def att_bisect_pair(jA, jB):
        A, B = bisB[0], bisB[1]
        SA, SB = (jA + 1) * 128, (jB + 1) * 128
        doA, doB = jA >= 2, jB >= 2
        if doA:
            op('dve', lambda e: e.memset(A[:, 0:1], 0.0), w=[('bis', 0)])
        else:
            op('dve', lambda e: e.memset(A[:, 3:4], -1.0e29), w=[('bis', 0)])
        if doB:
            op('dve', lambda e: e.memset(B[:, 0:1], 0.0), w=[('bis', 1)])
        else:
            op('dve', lambda e: e.memset(B[:, 3:4], -1.0e29), w=[('bis', 1)])
        for it in range(NIT):
            wd = RNG / (2 ** (it + 1))
            if doA:
                op('dve', lambda e: e.tensor_scalar(out=junk[:, 0:SA], in0=scoreB[0][:, 0:SA], scalar1=A[:, 0:1],
                                                    scalar2=None, op0=OP.is_ge, op1=OP.add, accum_out=A[:, 1:2]),
                   r=[('score', 0), ('bis', 0)], w=['junk', ('bis', 0)])
            if doB:
                op('act', lambda e: e.activation(out=junkB[:, 0:SB], in_=scoreB[1][:, 0:SB], func=AF.Sign,
                                                 bias=B[:, 0:1], accum_out=B[:, 1:2]),
                   r=[('score', 1), ('bis', 1)], w=['junkB', ('bis', 1)])
            if doA:
                op('dve', lambda e, wd=wd: e.tensor_scalar(out=A[:, 2:3], in0=A[:, 1:2], scalar1=255.5, scalar2=2.0 * wd,
                                                           op0=OP.is_ge, op1=OP.mult),
                   r=[('bis', 0)], w=[('bis', 0)])
                op('dve', lambda e, wd=wd: e.scalar_tensor_tensor(out=A[:, 0:1], in0=A[:, 2:3], scalar=-wd, in1=A[:, 0:1],
                                                                  op0=OP.add, op1=OP.add),
                   r=[('bis', 0)], w=[('bis', 0)])
            if doB:
                op('dve', lambda e, wd=wd: e.tensor_scalar(out=B[:, 2:3], in0=B[:, 1:2], scalar1=511.0 - SB,
                                                           scalar2=-2.0 * wd, op0=OP.is_ge, op1=OP.mult),
                   r=[('bis', 1)], w=[('bis', 1)])
                op('dve', lambda e, wd=wd: e.scalar_tensor_tensor(out=B[:, 0:1], in0=B[:, 2:3], scalar=wd, in1=B[:, 0:1],
                                                                  op0=OP.add, op1=OP.add),
                   r=[('bis', 1)], w=[('bis', 1)])
        wl = RNG / (2 ** NIT)
        if doA:
            op('dve', lambda e: e.tensor_scalar(out=A[:, 3:4], in0=A[:, 0:1], scalar1=-wl, scalar2=None, op0=OP.add),
               r=[('bis', 0)], w=[('bis', 0)])
        if doB:
            op('dve', lambda e: e.tensor_scalar(out=B[:, 3:4], in0=B[:, 0:1], scalar1=-1.0, scalar2=-wl,
                                                op0=OP.mult, op1=OP.add),
               r=[('bis', 1)], w=[('bis', 1)])

    def att_finish(j, b):
        if j == 0:
            op('pq', lambda e: e.dma_start(out=wpqb_d.rearrange("c p k e -> (c p) (k e)"),
                                           in_=wpq_d.rearrange("c p k e -> (c p) (k e)")), w=['wpqb'])
        Sj = (j + 1) * 128
        tq = slice(j * 128, (j + 1) * 128)
        sc_ = scoreB[b]
        mk = maskbB[b]
        tcol = bisB[b][:, 3:4]
        rdn = bisB[b]
        op('dve', lambda e: e.tensor_scalar(out=mk[:, 0:Sj], in0=sc_[:, 0:Sj], scalar1=tcol, scalar2=None,
                                            op0=OP.is_ge),
           r=[('score', b), ('bis', b)], w=[('maskb', b)])
        for g8 in range((j + 8) // 8):
            bM = 2 + g8
            lo = g8 * 8
            hi = min(j + 1, lo + 8)
            for sc in range(lo, hi):
                op('pe', lambda e, sc=sc, bM=bM, lo=lo: e.transpose(out=psb[bM][:, (sc - lo) * 128:(sc - lo + 1) * 128],
                                                                    in_=mk[:, sc * 128:(sc + 1) * 128],
                                                                    identity=ident),
                   r=[('maskb', b), 'ident'], w=[('ps', bM)])
            op('act', lambda e, bM=bM, lo=lo, hi=hi: e.activation(
                out=maskT[:, lo:hi, :], in_=psb[bM][:, 0:(hi - lo) * 128].rearrange("p (a b) -> p a b", b=128),
                func=AF.Copy),
               r=[('ps', bM)], w=['maskT'])
        for sc in range(j + 1):
            sk = slice(sc * 128, (sc + 1) * 128)
            bL = 2 + (ectr[0] % 2)
            sl = ectr[0] % 2
            ectr[0] += 1
            op('pe', lambda e, sk=sk, bL=bL: e.matmul(ps[bL], kv_latT[:, sk], q_latT[:, :, tq], start=True, stop=False),
               r=[('kvT', sc)] + [('qlat', h) for h in range(4)], w=[('ps', bL)])
            op('pe', lambda e, sk=sk, bL=bL: e.matmul(ps[bL], k_rotT[0:32, sk], q_rotT[0:32, :, tq],
                                                      start=False, stop=True),
               r=['krot'] + [('qrot', h) for h in range(4)], w=[('ps', bL)])
            op('act', lambda e, bL=bL, sl=sl: e.activation(out=Eb[sl], in_=ps[bL], func=AF.Exp, scale=ATT_SCALE),
               r=[('ps', bL)], w=[('E', sl)])
            op('dve', lambda e, sl=sl, sc=sc: e.tensor_tensor(
                out=PTb[sl], in0=Eb[sl].rearrange("p (h t) -> p h t", h=4), in1=bc_mid(maskT[:, sc, :], 1, 4),
                op=OP.mult),
               r=[('E', sl), 'maskT'], w=[('PT', sl)])
            for h in range(4):
                bO = 4 + h
                op('pe', lambda e, h=h, bO=bO, sl=sl, sc=sc: e.matmul(
                    ps[bO][:, 0:129], PTb[sl][:, h, :], kv1[:, sc, 0:129], start=(sc == 0), stop=(sc == j)),
                   r=[('PT', sl), ('kv1', sc)], w=[('ps', bO)])
        for h in range(4):
            op('dve', lambda e, h=h: e.reciprocal(out=rdn[:, 4 + h:5 + h], in_=ps[4 + h][:, 128:129]),
               r=[('ps', 4 + h)], w=[('rden', b, h)])
            op('dve', lambda e, h=h: e.tensor_scalar(out=o_n[:, h, :], in0=ps[4 + h][:, 0:128],
                                                     scalar1=rdn[:, 4 + h:5 + h], scalar2=None, op0=OP.mult),
               r=[('ps', 4 + h), ('rden', b, h)], w=[('o_n', h)])
        for h in range(4):
            op('pe', lambda e, h=h: e.transpose(out=psb[2][:, h * 128:(h + 1) * 128], in_=o_n[:, h, :], identity=ident),
               r=[('o_n', h), 'ident'], w=[('ps', 2)])
        op('act', lambda e: e.activation(out=o_nT[:, :, tq], in_=psb[2][:, 0:512].rearrange("p (h t) -> p h t", h=4),
                                         func=AF.Copy),
           r=[('ps', 2)], w=[('o_nT', j)])

    NIT = 22
    RNG = 16.0
    for jp in range(NT // 2):
        jA, jB = 2 * jp, 2 * jp + 1
        att_scores(jA, 0)
        att_scores(jB, 1)
        att_bisect_pair(jA, jB)
        att_finish(jA, 0)
        att_finish(jB, 1)

    for h in range(4):
        for n in range(4):
            bank = next_bank()
            op('pe', lambda e, h=h, n=n, bank=bank: e.matmul(ps[bank], wuv[:, h, :], o_nT[:, h, cols(n)],
                                                             start=True, stop=True),
               r=['wuv'] + [('o_nT', jj) for jj in range(n * 4, n * 4 + 4)], w=[('ps', bank)])
            op('act', lambda e, h=h, n=n, bank=bank: e.activation(out=catT[:, h, cols(n)], in_=ps[bank], func=AF.Copy),
               r=[('ps', bank)], w=[('cat', h)])
    if stop == 'E':
        tmp = ar.alloc([128, S], F32)
        op('dve', lambda e: e.tensor_copy(out=tmp, in_=catT[:, 1, :]), r=[('cat', 1)], w=['tmp'])
        finish_dbg(tmp, ['tmp'], 128, S)
        with nc.Block() as block:
            sch.emit(block)
        return nc

    ar.reset(mAtt)
    sch.fence()
    mF = ar.mark()
    x_sb = ar.alloc([128, NT, D], F32)
    wbig = ar.alloc([128, 8, 1024], BF16)
    for q4 in range(4):
        op('sp', lambda e, q4=q4: e.dma_start(out=x_sb[:, q4 * 4:(q4 + 1) * 4, :],
                                             in_=x_d[q4 * 512:(q4 + 1) * 512, :].rearrange("(n p) m -> p n m", p=128)),
           w=[('x', i) for i in range(q4 * 4, q4 * 4 + 4)])
    op('pq', lambda e: e.dma_start(out=wbig, in_=wout_d), w=['wbig'])
    cat_keys = [('cat', k) for k in range(8)]
    for i in range(NT):
        for hf in range(2):
            bank = next_bank()
            for k in range(8):
                op('pe', lambda e, i=i, hf=hf, k=k, bank=bank: e.matmul(
                    ps[bank], catT[:, k, i * 128:(i + 1) * 128], wbig[:, k, hf * 512:(hf + 1) * 512],
                    start=(k == 0), stop=(k == 7)),
                   r=['wbig'] + cat_keys, w=[('ps', bank)])
            op('dve', lambda e, i=i, hf=hf, bank=bank: e.tensor_tensor(
                out=x_sb[:, i, hf * 512:(hf + 1) * 512], in0=ps[bank], in1=x_sb[:, i, hf * 512:(hf + 1) * 512], op=OP.add),
               r=[('ps', bank), ('x', i)], w=[('x', i)])
    dbg_point('F', x_sb[:, 3, :], [('x', 3)], ncols=D)

    def perm(ap, order):
        l = [list(x) for x in ap.ap]
        return bass.AP(ap.tensor, ap.offset, [l[i] for i in order])

    def srcG(i):
        return x_sb[:, i, :], ('x', i)
    rmsnorm_to_T(srcG, 8, hT, 'G')
    h2T = hT
    h2_keys = [('G', 'T', i) for i in range(NT)]
    for q4 in range(4):
        op('sp', lambda e, q4=q4: e.dma_start(
            out=xmid_d[q4 * 512:(q4 + 1) * 512, :].rearrange("(n p) m -> p n m", p=128),
            in_=x_sb[:, q4 * 4:(q4 + 1) * 4, :]),
           r=[('x', i) for i in range(q4 * 4, q4 * 4 + 4)], w=[('xmid', q4 * 2), ('xmid', q4 * 2 + 1)])
    dbg_point('G:h2T', hT[:, 3, :], h2_keys)
    ar.reset(mF)
    sch.fence()
    car = Arena(nc, 32768, ap=ar.ap, base=cat_off)
    qT_g = car.alloc([128, 16, 256], BF16)
    s12 = car.alloc([128, 16, 128], F32)
    cand = car.alloc([128, 8, 256], F32)
    ohtmp = car.alloc([128, 8, 16, 16], F32)
    G_sb = ar.alloc([128, 128, 256], BF16)
    kT_sb = ar.alloc([128, 16, 128], BF16)
    op('pq', lambda e: e.dma_start(out=kT_sb, in_=kT_d), w=['kT'])
    v16 = ar.alloc([128, 16, 16], F32)
    idx16 = ar.alloc([128, 16, 16], U32)
    idxf = ar.alloc([128, 16, 16], F32)
    vals = ar.alloc([128, 8, 16], F32)
    fidx = ar.alloc([128, 8, 16], U32)
    fif = ar.alloc([128, 8, 16], F32)
    r_f = ar.alloc([128, 8, 16], F32)
    c_f = ar.alloc([128, 8, 16], F32)
    e1f2 = [ar.alloc([128, 8, 16], F32) for _ in range(2)]
    e2f2 = [ar.alloc([128, 8, 16], F32) for _ in range(2)]
    exg2 = [ar.alloc([128, 8, 16], F32) for _ in range(2)]
    Zs = ar.alloc([128, 16], F32)
    E1T_b = ar.alloc([128, 128], BF16)
    E2T_b = ar.alloc([128, 128], BF16)
    gT_b = ar.alloc([128, 128], BF16)
    TB = 16
    OH1 = [ar.alloc([128, TB, 128], BF16) for _ in range(2)]
    OH2 = [ar.alloc([128, TB, 128], BF16) for _ in range(2)]
    wcb = [ar.alloc([128, 2048], BF16) for _ in range(3)]
    uTb = [w_[:, 0:1024].rearrange("p (k e) -> p k e", e=128) for w_ in wcb]
    vb = [w_[:, 1024:2048] for w_ in wcb]
    gab = [ar.alloc([128, 256], BF16) for _ in range(2)]
    Wtb = [ar.alloc([128, 256], BF16) for _ in range(2)]
    xg = ar.alloc([128, 2, D], F32)
    Gc = [ar.alloc([128, 256], BF16) for _ in range(3)]
    wq = [ar.alloc([128, 8, 128], BF16) for _ in range(4)]
    iota16 = iota_f[:, 0:16]
    iota_rep = junk.rearrange("p (a b) -> p a b", b=128)
    op('dve', lambda e: e.tensor_copy(out=iota_rep, in_=bc_mid(iota_b, 1, 16)), r=['iota_b'], w=['junk'])
    MAGIC2 = 12582912.0
    gctr = [0]
    wq_ptr = [0]
    vbf_keys = [('vbf', j) for j in range(16)]

    def gbuild(g):
        gq = []

        def gop(eng, fn, r=(), w=(), cost=None):
            if cost is None:
                cost = {'dve': 0.6, 'pe': 0.15, 'act': 0.5}.get(eng, 0.0)
            gq.append((eng, fn, r, w, cost))
        def wq_load_upto(L):
            while wq_ptr[0] <= L and wq_ptr[0] < 8 * 16:
                Lq = wq_ptr[0]
                wq_ptr[0] += 1
                gop('pq', lambda e, Lq=Lq: e.dma_start(out=wq[Lq % 4], in_=wpqb_d[Lq % 16]), r=['wpqb'], w=[('wq', Lq % 4)], cost=0.0)

        def qT_stage(gg):
            qcols = slice(gg * 256, (gg + 1) * 256)
            wq_load_upto(gg * 16 + 3)
            for ch in range(16):
                wt = wq[ch % 4]
                bank = 6 + ch % 2
                for k in range(8):
                    gop('pe', lambda e, k=k, wt=wt, bank=bank: e.matmul(ps[bank][:, 0:256], wt[:, k, :], h2T[:, k, qcols],
                                                                        start=(k == 0), stop=(k == 7)),
                        r=[('wq', ch % 4)] + h2_keys[2 * gg:2 * gg + 2], w=[('ps', bank)])
                gop('act', lambda e, ch=ch, bank=bank: e.activation(out=qT_g[:, ch, :], in_=ps[bank][:, 0:256], func=AF.Copy),
                    r=[('ps', bank)], w=['qT_g'])
                wq_load_upto(gg * 16 + ch + 4)
        if g == 0:
            qT_stage(0)
        def stage_x(tt):
            tcol = slice(tt * 128, (tt + 1) * 128)
            e1f, e2f, exg = e1f2[tt], e2f2[tt], exg2[tt]
            ek = 'e1f%d' % tt
            e2k = 'e2f%d' % tt
            xk = 'exg%d' % tt
            for q in range(4):
                bank = 6 + q % 2
                for cq in range(4):
                    ch = q * 4 + cq
                    gop('pe', lambda e, ch=ch, cq=cq, bank=bank, tcol=tcol: e.matmul(ps[bank][:, cq * 128:(cq + 1) * 128],
                                                                         qT_g[:, ch, tcol], kT_sb[:, ch, :],
                                                                         start=True, stop=True),
                       r=['qT_g', 'kT'], w=[('ps', bank)])
                gop('act', lambda e, q=q, bank=bank: e.activation(
                    out=s12[:, q * 4:(q + 1) * 4, :], in_=ps[bank].rearrange("p (a b) -> p a b", b=128), func=AF.Copy),
                   r=[('ps', bank)], w=[('s12', q * 4 + i_) for i_ in range(4)])
            for ch in range(16):
                row = s12[:, ch, :]
                gop('dve', lambda e, ch=ch, row=row: e.max(out=v16[:, ch, 0:8], in_=row), r=[('s12', ch)], w=[('v16', ch)])
                gop('dve', lambda e, ch=ch, row=row: e.max_index(out=idx16[:, ch, 0:8], in_max=v16[:, ch, 0:8],
                                                                in_values=row),
                   r=[('s12', ch), ('v16', ch)], w=[('idx16', ch)])
                gop('dve', lambda e, ch=ch, row=row: e.match_replace(out=row, in_to_replace=v16[:, ch, 0:8],
                                                                    in_values=row, imm_value=NEG),
                   r=[('s12', ch), ('v16', ch)], w=[('s12', ch)])
                gop('dve', lambda e, ch=ch, row=row: e.max(out=v16[:, ch, 8:16], in_=row), r=[('s12', ch)], w=[('v16', ch)])
                gop('dve', lambda e, ch=ch, row=row: e.max_index(out=idx16[:, ch, 8:16], in_max=v16[:, ch, 8:16],
                                                                in_values=row),
                   r=[('s12', ch), ('v16', ch)], w=[('idx16', ch)])
            gop('dve', lambda e: e.tensor_copy(out=idxf, in_=idx16), r=[('idx16', c_) for c_ in range(16)], w=['idxf'])
            v4 = v16.rearrange("p (h two) k -> p h two k", two=2)
            i4 = idxf.rearrange("p (h two) k -> p h two k", two=2)
            cand4 = cand.rearrange("p h (i j) -> p h i j", j=16)
            gop('dve', lambda e: e.tensor_tensor(out=cand4, in0=bc_last(v4[:, :, 0, :], 16), in1=bc_mid(v4[:, :, 1, :], 2, 16),
                                                op=OP.add),
               r=[('v16', c_) for c_ in range(16)], w=[('cand', h_) for h_ in range(8)], cost=2.4)
            for h in range(8):
                row = cand[:, h, :]
                gop('dve', lambda e, h=h, row=row: e.max(out=vals[:, h, 0:8], in_=row), r=[('cand', h)], w=[('vals', h)])
                gop('dve', lambda e, h=h, row=row: e.max_index(out=fidx[:, h, 0:8], in_max=vals[:, h, 0:8], in_values=row),
                   r=[('cand', h), ('vals', h)], w=[('fidx', h)])
                gop('dve', lambda e, h=h, row=row: e.match_replace(out=row, in_to_replace=vals[:, h, 0:8], in_values=row,
                                                                  imm_value=NEG),
                   r=[('cand', h), ('vals', h)], w=[('cand', h)])
                gop('dve', lambda e, h=h, row=row: e.max(out=vals[:, h, 8:16], in_=row), r=[('cand', h)], w=[('vals', h)])
                gop('dve', lambda e, h=h, row=row: e.max_index(out=fidx[:, h, 8:16], in_max=vals[:, h, 8:16],
                                                              in_values=row),
                   r=[('cand', h), ('vals', h)], w=[('fidx', h)])
            gop('dve', lambda e: e.tensor_copy(out=fif, in_=fidx), r=[('fidx', h_) for h_ in range(8)], w=['fif'])
            gop('dve', lambda e: e.tensor_scalar(out=r_f, in0=fif, scalar1=0.0625, scalar2=-0.46875, op0=OP.mult, op1=OP.add),
               r=['fif'], w=['r_f'])
            gop('dve', lambda e: e.tensor_scalar(out=r_f, in0=r_f, scalar1=MAGIC2, scalar2=None, op0=OP.add),
               r=['r_f'], w=['r_f'])
            gop('dve', lambda e: e.tensor_scalar(out=r_f, in0=r_f, scalar1=-MAGIC2, scalar2=None, op0=OP.add),
               r=['r_f'], w=['r_f'])
            gop('dve', lambda e: e.scalar_tensor_tensor(out=c_f, in0=r_f, scalar=-16.0, in1=fif, op0=OP.mult, op1=OP.add),
               r=['r_f', 'fif'], w=['c_f'])
            for (sel, two, dst, dkey) in ((r_f, 0, e1f, ek), (c_f, 1, e2f, e2k)):
                gop('dve', lambda e, sel=sel: e.tensor_tensor(
                    out=ohtmp, in0=bc_mid(bc_mid(iota16, 1, 16), 1, 8), in1=bc_last(sel, 16), op=OP.is_equal),
                   r=['r_f', 'c_f', 'iota_f'], w=['ohtmp'], cost=2.4)
                gop('dve', lambda e, two=two: e.tensor_tensor(out=ohtmp, in0=ohtmp, in1=bc_mid(i4[:, :, two, :], 2, 16),
                                                             op=OP.mult),
                   r=['ohtmp', 'idxf'], w=['ohtmp'], cost=2.4)
                gop('dve', lambda e, dst=dst: e.tensor_reduce(out=dst, in_=ohtmp, axis=AX.X, op=OP.add),
                   r=['ohtmp'], w=[dkey], cost=2.4)
            gop('dve', lambda e: e.tensor_tensor(out=exg, in0=vals, in1=bc_last(vals[:, :, 0], 16), op=OP.subtract),
               r=[('vals', h_) for h_ in range(8)], w=[xk])
            gop('act', lambda e: e.activation(out=exg, in_=exg, func=AF.Exp), r=[xk], w=[xk])
            gop('dve', lambda e: e.tensor_reduce(out=Zs[:, 0:8], in_=exg, axis=AX.X, op=OP.add), r=[xk], w=['Zs'])
            gop('dve', lambda e: e.reciprocal(out=Zs[:, 8:16], in_=Zs[:, 0:8]), r=['Zs'], w=['Zs'])
            gop('dve', lambda e: e.tensor_tensor(out=exg, in0=exg, in1=bc_last(Zs[:, 8:16], 16), op=OP.mult),
               r=[xk, 'Zs'], w=[xk])
        def stage_y(tt):
            e1f, e2f, exg = e1f2[tt], e2f2[tt], exg2[tt]
            for ti, (src, skey, dst, dkey) in enumerate(((e1f, 'e1f%d' % tt, E1T_b, 'E1T'), (e2f, 'e2f%d' % tt, E2T_b, 'E2T'),
                                                          (exg, 'exg%d' % tt, gT_b, 'gT'))):
                bank = 6 + ti % 2
                gop('pe', lambda e, src=src, bank=bank: e.transpose(out=ps[bank][:, 0:128],
                                                                   in_=src.rearrange("p h k -> p (h k)"),
                                                                   identity=ident_f),
                   r=[skey, 'ident_f'], w=[('ps', bank)])
                gop('act', lambda e, dst=dst, bank=bank: e.activation(out=dst, in_=ps[bank][:, 0:128], func=AF.Copy),
                   r=[('ps', bank)], w=[dkey])
            ohs = {}

            def oh_make(tb):
                    ts = slice(tb * TB, (tb + 1) * TB)
                    sl = gctr[0] % 2
                    gctr[0] += 1
                    o1, o2 = OH1[sl], OH2[sl]
                    ohs[tb] = (sl, o1, o2)
                    gop('dve', lambda e, o2=o2, ts=ts: e.tensor_tensor(out=o2, in0=iota_rep,
                                                                      in1=bc_last(E2T_b[:, ts], 128), op=OP.is_equal),
                       r=['junk', 'E2T'], w=[('OH2', sl)], cost=1.4)
                    gop('dve', lambda e, o1=o1, ts=ts: e.tensor_tensor(out=o1, in0=iota_rep,
                                                                      in1=bc_last(E1T_b[:, ts], 128), op=OP.is_equal),
                       r=['junk', 'E1T'], w=[('OH1', sl)], cost=1.4)
                    gop('dve', lambda e, o1=o1, ts=ts: e.tensor_tensor(out=o1, in0=o1, in1=bc_last(gT_b[:, ts], 128),
                                                                       op=OP.mult),
                       r=[('OH1', sl), 'gT'], w=[('OH1', sl)], cost=1.4)

            def oh_use(tb):
                    sl, o1, o2 = ohs[tb]
                    for q in range(TB // 4):
                        bank = 6 + (tb * (TB // 4) + q) % 2
                        for t4 in range(4):
                            t = q * 4 + t4
                            gop('pe', lambda e, t=t, t4=t4, bank=bank, o1=o1, o2=o2: e.matmul(
                                ps[bank][:, t4 * 128:(t4 + 1) * 128], o1[:, t, :], o2[:, t, :],
                                start=True, stop=True),
                               r=[('OH1', sl), ('OH2', sl)], w=[('ps', bank)])
                        t0 = tt * 128 + tb * TB + q * 4
                        dstp = G_sb[:, :, t0:t0 + 4]
                        eng = 'act'
                        if eng == 'act':
                            gop('act', lambda e, bank=bank, dstp=dstp: e.activation(
                                out=dstp, in_=ps[bank].rearrange("p (t e) -> p e t", t=4), func=AF.Copy),
                               r=[('ps', bank)], w=['G_sb'])
                        else:
                            gop('dve', lambda e, bank=bank, dstp=dstp: e.tensor_copy(
                                out=dstp, in_=ps[bank].rearrange("p (a b) -> p a b", b=128)),
                               r=[('ps', bank)], w=['G_sb'])

            oh_make(0)
            for tb in range(128 // TB):
                if tb + 1 < 128 // TB:
                    oh_make(tb + 1)
                oh_use(tb)

        stage_x(0)
        stage_x(1)
        if g < 7:
            qT_stage(g + 1)
        stage_y(0)
        stage_y(1)
        for q4 in range(4):
            gop('pq', lambda e, q4=q4: e.dma_start(out=gd_d[g][:, q4 * 32:(q4 + 1) * 32, :],
                                                   in_=G_sb[:, q4 * 32:(q4 + 1) * 32, :]),
                r=['G_sb'], w=[('gd', g, q4)], cost=0.0)
        return gq

    def peer_main(g, pending):
        gcols = slice(g * 256, (g + 1) * 256)
        def chunk_front(c):
            sl3 = c % 3
            op('sp', lambda e: e.dma_start(out=wcb[sl3], in_=wc_d[c]), r=[('ub', c // 8)] + vbf_keys,
               w=[('uTb', sl3), ('vb', sl3)])
            op('sp', lambda e: e.dma_start(out=Gc[sl3], in_=gd_d[g][:, c, :]), r=[('gd', g, c // 32)], w=[('Gc', sl3)])
            bank = 4 + c % 2
            for k in range(8):
                op('pe', lambda e, k=k: e.matmul(ps[bank][:, 0:256], uTb[sl3][:, k, :], h2T[:, k, gcols],
                                                 start=(k == 0), stop=(k == 7)),
                   r=[('uTb', sl3)] + h2_keys[2 * g:2 * g + 2], w=[('ps', bank)])
            s2 = c % 2
            op('act', lambda e: e.activation(out=gab[s2], in_=ps[bank][:, 0:256], func=AF.Gelu),
               r=[('ps', bank)], w=[('ga', s2)])
            op('pool', lambda e: e.tensor_tensor(out=Wtb[s2], in0=gab[s2], in1=Gc[sl3], op=OP.mult),
               r=[('ga', s2), ('Gc', sl3)], w=[('Wt', s2)])

        def chunk_back(c):
            sl3 = c % 3
            s2 = c % 2
            for t2 in range(2):
                for hf in range(2):
                    ob = t2 * 2 + hf
                    op('pe', lambda e, t2=t2, hf=hf, ob=ob: e.matmul(
                        ps[ob], Wtb[s2][:, t2 * 128:(t2 + 1) * 128], vb[sl3][:, hf * 512:(hf + 1) * 512],
                        start=(c == 0), stop=(c == 127)),
                       r=[('Wt', s2), ('vb', sl3)], w=[('ps', ob)])

        total_cost = sum(t_[4] for t_ in pending)
        per = total_cost / 127.0
        released = 0.0
        for c in range(129):
            if c < 128:
                chunk_front(c)
            if c >= 1:
                chunk_back(c - 1)
            while pending and released < per * (c + 1):
                t_ = pending.pop(0)
                op(*t_[:4])
                released += t_[4]
        while pending:
            t_ = pending.pop(0)
            op(*t_[:4])
        op('sp', lambda e: e.dma_start(out=xg, in_=xmid_d[g * 256:(g + 1) * 256, :].rearrange("(n p) m -> p n m", p=128)),
           r=[('xmid', g)], w=['xg'])
        for t2 in range(2):
            for hf in range(2):
                ob = t2 * 2 + hf
                op('dve', lambda e, t2=t2, hf=hf, ob=ob: e.tensor_tensor(
                    out=xg[:, t2, hf * 512:(hf + 1) * 512], in0=ps[ob], in1=xg[:, t2, hf * 512:(hf + 1) * 512], op=OP.add),
                   r=[('ps', ob), 'xg'], w=['xg'])
        op('sp', lambda e: e.dma_start(out=xmid_d[g * 256:(g + 1) * 256, :].rearrange("(n p) m -> p n m", p=128), in_=xg),
           r=['xg'], w=[('xmid', g)])

    for t_ in gbuild(0):
        op(*t_[:4])
    for g in range(8):
        peer_main(g, gbuild(g + 1) if g < 7 else [])

    ar.reset(mF)
    sch.fence()
    x_sb = ar.alloc([128, NT, D], F32)
    wbig = ar.alloc([128, 8, 1024], BF16)
    for q4 in range(4):
        op('sp', lambda e, q4=q4: e.dma_start(out=x_sb[:, q4 * 4:(q4 + 1) * 4, :],
                                             in_=xmid_d[q4 * 512:(q4 + 1) * 512, :].rearrange("(n p) m -> p n m", p=128)),
           r=[('xmid', q4 * 2), ('xmid', q4 * 2 + 1)], w=[('x', i) for i in range(q4 * 4, q4 * 4 + 4)])
    dbg_point('G:x2', x_sb[:, 3, :], [('x', 3)], ncols=D)

    sch.fence()

    def srcH(i):
        return x_sb[:, i, :], ('x', i)
    xT = hT
    xbf = [ar.alloc([128, D], BF16) for _ in range(2)]
    for i in range(NT):
        xb2 = xbf[i % 2]
        op('act', lambda e, i=i, xb2=xb2: e.activation(out=xb2, in_=x_sb[:, i, :], func=AF.Copy),
           r=[('x', i)], w=[('xbf', i % 2)])
        bank = next_bank()
        for k in range(8):
            op('pe', lambda e, k=k, xb2=xb2, bank=bank: e.transpose(out=psb[bank][:, k * 128:(k + 1) * 128],
                                                                     in_=xb2[:, k * 128:(k + 1) * 128], identity=ident),
               r=[('xbf', i % 2), 'ident'], w=[('ps', bank)])
        op('act', lambda e, i=i, bank=bank: e.activation(out=xT[:, :, i * 128:(i + 1) * 128],
                                                         in_=psb[bank].rearrange("p (k t) -> p k t", k=8), func=AF.Copy),
           r=[('ps', bank)], w=[('xT', i)])
    wple = ar.alloc([128, 2, 1024], BF16)
    pT_sb = ar.alloc([128, 2, S], BF16)
    gfin_bc = ar.alloc([128, D], F32)
    sgt = [ar.alloc([128, 512], F32) for _ in range(2)]
    outt = [ar.alloc([128, D], F32) for _ in range(2)]
    op('pq', lambda e: e.dma_start(out=wbig, in_=wpg_d), w=['wbig'])
    op('pq', lambda e: e.dma_start(out=wple, in_=wple_d), w=['wple'])
    op('pq', lambda e: e.dma_start(out=pT_sb, in_=pT_d.rearrange("(k p) s -> p k s", p=128)), w=['pT'])
    op('sp', lambda e: e.dma_start(out=gfin_bc, in_=bass.AP(gfin_d.tensor, 0, [[0, 128], [1, D]])), w=['gfin'])
    for i in range(NT):
        for hf in range(2):
            bG = next_bank()
            for k in range(8):
                op('pe', lambda e, i=i, hf=hf, k=k, bG=bG: e.matmul(
                    ps[bG], xT[:, k, i * 128:(i + 1) * 128], wbig[:, k, hf * 512:(hf + 1) * 512],
                    start=(k == 0), stop=(k == 7)),
                   r=['wbig', ('xT', i)], w=[('ps', bG)])
            bP = next_bank()
            for k in range(2):
                op('pe', lambda e, i=i, hf=hf, k=k, bP=bP: e.matmul(
                    ps[bP], pT_sb[:, k, i * 128:(i + 1) * 128], wple[:, k, hf * 512:(hf + 1) * 512],
                    start=(k == 0), stop=(k == 1)),
                   r=['wple', 'pT'], w=[('ps', bP)])
            sg_ = sgt[hf]
            op('act', lambda e, bG=bG, sg_=sg_: e.activation(out=sg_, in_=ps[bG], func=AF.Sigmoid),
               r=[('ps', bG)], w=[('sgt', hf)])
            op('dve', lambda e, bP=bP, sg_=sg_: e.tensor_tensor(out=sg_, in0=sg_, in1=ps[bP], op=OP.mult),
               r=[('ps', bP), ('sgt', hf)], w=[('sgt', hf)])
            op('dve', lambda e, i=i, hf=hf, sg_=sg_: e.tensor_tensor(
                out=x_sb[:, i, hf * 512:(hf + 1) * 512], in0=x_sb[:, i, hf * 512:(hf + 1) * 512], in1=sg_, op=OP.add),
               r=[('sgt', hf), ('x', i)], w=[('x', i)])
        ss = stat[:, i:i + 1]
        rs = stat[:, 16 + i:17 + i]
        op('act', lambda e, i=i, ss=ss: e.activation(out=junk[:, 0:D], in_=x_sb[:, i, :], func=AF.Square, accum_out=ss),
           r=[('x', i)], w=['junk', ('Hss', i)])
        op('act', lambda e, ss=ss, rs=rs: e.activation(out=rs, in_=ss, func=AF.Sqrt, scale=1.0 / D, bias=epsc),
           r=[('Hss', i), 'epsc'], w=[('Hrs', i)])
        op('dve', lambda e, rs=rs: e.reciprocal(out=rs, in_=rs), r=[('Hrs', i)], w=[('Hrs', i)])
        ot = outt[i % 2]
        op('dve', lambda e, i=i, rs=rs, ot=ot: e.scalar_tensor_tensor(out=ot, in0=x_sb[:, i, :], scalar=rs, in1=gfin_bc,
                                                                     op0=OP.mult, op1=OP.mult),
           r=[('x', i), ('Hrs', i), 'gfin'], w=[('outt', i % 2)])
        op('sp', lambda e, i=i, ot=ot: e.dma_start(out=out_d[i * 128:(i + 1) * 128, :], in_=ot),
           r=[('outt', i % 2)], w=[('out', i)])
    with nc.Block() as block:
        sch.emit(block)
    return nc


def _host_consts():
    ident = np.eye(128, dtype=np.float32)
    t = np.arange(128)
    cmask = np.where(t[None, :] <= t[:, None], 0.0, NEG).astype(np.float32)
    iota = np.broadcast_to(np.arange(128, dtype=np.float32)[None, :], (128, 128)).copy()
    ropec = np.zeros((128, 4), np.float32)
    inv_a = ROPE_THETA ** (-np.arange(0, 32, 2, dtype=np.float32) / 32)
    inv_i = ROPE_THETA ** (-np.arange(0, 16, 2, dtype=np.float32) / 16)
    for p in range(32):
        ropec[p, 0] = inv_a[p % 16] / (2 * np.pi)
        ropec[p, 1] = -1.0 if p < 16 else 1.0
    for base in (0, 64):
        for p in range(16):
            ropec[base + p, 2] = inv_i[p % 8] / (2 * np.pi)
            ropec[base + p, 3] = -1.0 if p < 8 else 1.0
    return ident, cmask, iota, ropec


def _prep_shared(inp):
    w_in = np.asarray(inp['w_in'][0], np.float32)
    cols = np.zeros((NCH, 128), np.int64)
    qo, kro, iqo, iko, xbo, gto = 0, 640, 672, 1184, 1256, 1768
    for h in range(4):
        cols[CH_Q[h]] = qo + 128 * h + np.arange(128)
        pc = np.arange(128)
        pc[:16] = np.arange(16, 32)
        pc[16:32] = np.arange(0, 16)
        cols[CH_QP[h]] = qo + 128 * h + pc
    kc = np.arange(128) % 32
    cols[CH_K] = kro + kc
    kp = kc.copy()
    kp[kc < 16] = kc[kc < 16] + 16
    kp[kc >= 16] = kc[kc >= 16] - 16
    cols[CH_KP] = kro + kp
    for m in range(4):
        cols[CH_IQ[m]] = iqo + 128 * m + np.arange(128)
        pc = np.arange(128)
        for base in (0, 64):
            pc[base:base + 8] = base + np.arange(8, 16)
            pc[base + 8:base + 16] = base + np.arange(0, 8)
        cols[CH_IQP[m]] = iqo + 128 * m + pc
    ic = np.arange(128) % 64
    cols[CH_IK] = iko + ic
    ip = ic.copy()
    ip[ic < 8] = ic[ic < 8] + 8
    ip[(ic >= 8) & (ic < 16)] = ic[(ic >= 8) & (ic < 16)] - 8
    cols[CH_IKP] = iko + ip
    for c in range(4):
        cols[CH_XB[c]] = xbo + 128 * c + np.arange(128)
        cols[CH_GT[c]] = gto + 128 * c + np.arange(128)
    wg = w_in[:, cols.reshape(-1)].reshape(8, 128, NCH, 128)
    wfm = np.ascontiguousarray(wg.transpose(2, 1, 0, 3))
    tmc = np.concatenate([np.arange(512, 640), np.arange(1248, 1256)])
    wtm = np.ascontiguousarray(w_in[:, tmc].reshape(8, 128, 136).transpose(1, 0, 2))
    gvec = np.zeros((128, 24), np.float32)
    gvec[:, 0:8] = np.asarray(inp['g_mix'][0]).reshape(8, 128).T
    gvec[:, 8:16] = np.asarray(inp['g_ffn'][0]).reshape(8, 128).T
    ident, cmask, iota, ropec = _host_consts()
    w_uk = np.asarray(inp['w_uk'][0], np.float32)
    wukT = np.zeros((128, 4, 128), np.float32)
    wukT[32:128] = w_uk.transpose(2, 0, 1)
    wuv = np.ascontiguousarray(np.asarray(inp['w_uv'][0], np.float32).transpose(1, 0, 2))
    lruv = np.zeros((128, 36), np.float32)
    cw = np.asarray(inp['conv_w'][0], np.float32)
    lruv[:, 0:16] = cw.reshape(4, 4, 128).transpose(2, 1, 0).reshape(128, 16)
    lruv[:, 16:20] = np.asarray(inp['conv_b'][0]).reshape(4, 128).T
    lruv[:, 20:24] = np.asarray(inp['b_rg'][0]).reshape(4, 128).T
    lruv[:, 24:28] = np.asarray(inp['b_ig'][0]).reshape(4, 128).T
    lruv[:, 28:32] = np.asarray(inp['lru_lambda'][0]).reshape(4, 128).T
    wbd = np.zeros((128, 8, 128), np.float32)
    for gi, nm in enumerate(['w_rg', 'w_ig']):
        wsrc = np.asarray(inp[nm][0], np.float32)
        for c in range(4):
            wbd[0:64, gi * 4 + c, 0:64] = wsrc[2 * c]
            wbd[64:128, gi * 4 + c, 64:128] = wsrc[2 * c + 1]
    wout = np.ascontiguousarray(np.asarray(inp['w_out'][0], np.float32).reshape(8, 128, 1024).transpose(1, 0, 2))
    wpq = np.ascontiguousarray(
        np.asarray(inp['w_pq'][0], np.float32).reshape(8, 128, 16, 128).transpose(2, 1, 0, 3))
    k1 = np.asarray(inp['peer_k1'][0], np.float32)
    k2 = np.asarray(inp['peer_k2'][0], np.float32)
    kT = np.zeros((128, 16, 128), np.float32)
    for h in range(8):
        kT[:, 2 * h, :] = k1[h].T
        kT[:, 2 * h + 1, :] = k2[h].T
    u = np.asarray(inp['peer_u'][0], np.float32)
    uT = np.ascontiguousarray(u.reshape(128, 128, 8, 128).transpose(1, 3, 2, 0))
    v = np.ascontiguousarray(np.asarray(inp['peer_v'][0], np.float32))
    wple = np.ascontiguousarray(np.asarray(inp['w_ple'][0], np.float32).reshape(2, 128, 1024).transpose(1, 0, 2))
    wpg = np.ascontiguousarray(
        np.asarray(inp['w_ple_gate'][0], np.float32).reshape(8, 128, 1024).transpose(1, 0, 2))
    return dict(wfm=wfm, wtm=wtm, gvec=gvec, ropec=ropec, ident=ident, cmask=cmask, iota=iota,
                gkv=np.asarray(inp['g_kv'], np.float32).reshape(1, 128), wukT=wukT, wuv=wuv, lruv=lruv, wbd=wbd,
                wout=wout, wpq=wpq, kT=kT, uT=uT, v=v, wple=wple, wpg=wpg,
                gfin=np.asarray(inp['g_final'], np.float32).reshape(1, D))


def make_in_maps(inp, ncores=NCORES):
    shared = _prep_shared(inp)
    maps = []
    for b in range(ncores):
        m = dict(shared)
        m['x'] = np.ascontiguousarray(np.asarray(inp['x'][b], np.float32))
        m['pT'] = np.ascontiguousarray(np.asarray(inp['p'][0, b], np.float32).T)
        m['pos'] = np.ascontiguousarray(np.asarray(inp['positions'][b], np.int32).reshape(1, S))
        maps.append(m)
    return maps


def kernel(**inputs):
    nc = build()
    maps = make_in_maps(inputs)
    res = run_bass_kernel_spmd(nc, maps, core_ids=list(range(NCORES)))
    return np.stack([np.asarray(r["out"], np.float32) for r in res.results], axis=0)
```

```python
import math
import numpy as np
import concourse.bass as bass
import concourse.mybir as mybir
from concourse.bass_utils import run_bass_kernel_spmd

F32 = mybir.dt.float32
BF16 = mybir.dt.bfloat16
I32 = mybir.dt.int32
U32 = mybir.dt.uint32
U8 = mybir.dt.uint8
AF = mybir.ActivationFunctionType
OP = mybir.AluOpType
AX = mybir.AxisListType

S = 2048
D = 1024
NT = 16
EPS = 1e-6
NEG = -1.0e30
NCORES = 8
ROPE_THETA = 500000.0
ATT_SCALE = 128 ** -0.5
IW_SCALE = (8 ** -0.5) * (64 ** -0.5)

CH_Q = [0, 1, 2, 3]
CH_QP = [4, 5, 6, 7]
CH_K, CH_KP = 8, 9
CH_IQ = [10, 11, 12, 13]
CH_IQP = [14, 15, 16, 17]
CH_IK, CH_IKP = 18, 19
CH_XB = [20, 21, 22, 23]
CH_GT = [24, 25, 26, 27]
NCH = 28


def _esz(dt):
    return 2 if dt == BF16 else (1 if dt == U8 else 4)


class Sched:
    PHYS = {'pe': 'pe', 'act': 'act', 'dve': 'dve', 'pool': 'pool', 'pq': 'pool', 'sp': 'sp'}
    UNIT = {'pe': 1, 'act': 1, 'dve': 1, 'pool': 1, 'pq': 16, 'sp': 16}
    EPOCH = {'pe': 30000, 'act': 30000, 'dve': 30000, 'pool': 30000}
    DMA = ('pq', 'sp')
    NSLOT = 16

    def __init__(self, nc):
        self.nc = nc
        self.streams = {'pe': [], 'act': [], 'dve': [], 'pool': [], 'sp': []}
        self.cnt = {e: 0 for e in self.PHYS}
        self.wr = {}
        self.rd = {}
        self.seen = {p: {} for p in self.streams}
        self.seen_d = {p: {e: set() for e in self.DMA} for p in self.streams}
        self.sems = {}
        self.fence_cnt = {}

    def fence(self):
        self.fence_cnt = dict(self.cnt)

    def sem(self, eng, c):
        if eng in self.DMA:
            slot = (c - 1) % self.NSLOT
            key = (eng, 'slot', slot)
            val = ((c - 1) // self.NSLOT + 1) * 16
        else:
            c = self.rank[eng][c]
            ep = (c - 1) // self.EPOCH[eng]
            key = (eng, ep)
            val = c - ep * self.EPOCH[eng]
        if key not in self.sems:
            self.sems[key] = self.nc.alloc_semaphore("s_" + "_".join(str(x) for x in key))
        return self.sems[key], val

    def op(self, eng, fn, r=(), w=()):
        phys = self.PHYS[eng]
        waits_c = {}
        waits_d = set()

        def need(ec):
            e, c = ec
            if e in self.DMA:
                waits_d.add((e, c))
            elif c > waits_c.get(e, 0):
                waits_c[e] = c
        for e, c in self.fence_cnt.items():
            if c <= 0:
                continue
            if e in self.DMA:
                for cc in range(max(1, c - self.NSLOT + 1), c + 1):
                    need((e, cc))
            else:
                need((e, c))
        for k in r:
            if k in self.wr:
                need(self.wr[k])
        for k in w:
            if k in self.wr:
                need(self.wr[k])
            for e, c in self.rd.get(k, {}).items():
                need((e, c))
        self.cnt[eng] += 1
        me = self.cnt[eng]
        if eng in self.DMA and me > self.NSLOT:
            waits_d.add((eng, me - self.NSLOT))
        final = []
        for e, c in waits_c.items():
            if e == 'pe' and eng == 'pe':
                continue
            if self.seen[phys].get(e, 0) >= c:
                continue
            self.seen[phys][e] = c
            final.append((e, c))
        for (e, c) in sorted(waits_d):
            if c in self.seen_d[phys][e]:
                continue
            self.seen_d[phys][e].add(c)
            final.append((e, c))
        self.streams[phys].append((eng, me, fn, final))
        for k in r:
            self.rd.setdefault(k, {})[eng] = me
        for k in w:
            self.wr[k] = (eng, me)
            self.rd[k] = {}

    def emit(self, block):
        final_waits = []
        for e, c in self.cnt.items():
            if c <= 0:
                continue
            if e in self.DMA:
                final_waits += [(e, cc) for cc in range(max(1, c - self.NSLOT + 1), c + 1)]
            else:
                final_waits.append((e, c))
        targets = {e: set() for e in self.PHYS if e not in self.DMA}
        for phys in self.streams:
            for (leng, me, fn, waits) in self.streams[phys]:
                for (e, c) in waits:
                    if e not in self.DMA:
                        targets[e].add(c)
        for (e, c) in final_waits:
            if e not in self.DMA:
                targets[e].add(c)
        self.rank = {e: {c: i + 1 for i, c in enumerate(sorted(t))} for e, t in targets.items()}

        def body_for(phys):
            def body(eng):
                for (leng, me, fn, waits) in self.streams[phys]:
                    for (e, c) in waits:
                        sm, v = self.sem(e, c)
                        eng.wait_ge(sm, v)
                    ins = fn(eng)
                    if leng in self.DMA:
                        sm, v = self.sem(leng, me)
                        ins.then_inc(sm, 16)
                    elif me in targets[leng]:
                        sm, v = self.sem(leng, me)
                        ins.then_inc(sm, 1)
                if phys == 'sp':
                    for (e, c) in final_waits:
                        sm, v = self.sem(e, c)
                        eng.wait_ge(sm, v)
            return body
        block.tensor(body_for('pe'))
        block.scalar(body_for('act'))
        block.vector(body_for('dve'))
        block.gpsimd(body_for('pool'))
        block.sync(body_for('sp'))


class Arena:
    def __init__(self, nc, nbytes, ap=None, base=0):
        self.ap = nc.alloc_sbuf_tensor("arena", [128, nbytes], U8).ap() if ap is None else ap
        self.n = base + nbytes
        self.off = base
        self.peak = 0

    def alloc(self, shape, dt):
        n = 1
        for d in shape[1:]:
            n *= d
        b = n * _esz(dt)
        b32 = (b + 63) // 64 * 64
        assert self.off + b32 <= self.n, f"arena overflow {self.off}+{b32}>{self.n}"
        a = self.ap[:, self.off:self.off + b].bitcast(dt)
        self.off += b32
        self.peak = max(self.peak, self.off)
        if len(shape) == 3:
            a = a.rearrange("p (a b) -> p a b", b=shape[2])
        elif len(shape) == 4:
            a = a.rearrange("p (a b c) -> p a b c", b=shape[2], c=shape[3])
        if shape[0] < 128:
            a = a[0:shape[0]]
        return a

    def mark(self):
        return self.off

    def reset(self, m):
        self.off = m


def bc_last(ap, n):
    return bass.AP(ap.tensor, ap.offset, [list(x) for x in ap.ap] + [[0, n]])


def bc_mid(ap, axis, n):
    l = [list(x) for x in ap.ap]
    return bass.AP(ap.tensor, ap.offset, l[:axis] + [[0, n]] + l[axis:])


def build(stop='all', dbg=None):
    try:
        return _build(stop, dbg)
    except _StopBuild as ex:
        return ex.nc


class _StopBuild(Exception):
    def __init__(self, nc):
        self.nc = nc


def _build(stop='all', dbg=None):
    nc = bass.Bass("TRN2", target_bir_lowering=False)
    dt_in = lambda name, shape, dt=F32: nc.dram_tensor(name, shape, dt, kind="ExternalInput").ap()
    x_d = dt_in("x", [S, D])
    pT_d = dt_in("pT", [256, S])
    pos_d = dt_in("pos", [1, S], I32)
    wfm_d = dt_in("wfm", [NCH, 128, 8, 128])
    wtm_d = dt_in("wtm", [128, 8, 136])
    gvec_d = dt_in("gvec", [128, 24])
    ropec_d = dt_in("ropec", [128, 4])
    ident_d = dt_in("ident", [128, 128])
    cmask_d = dt_in("cmask", [128, 128])
    iota_d = dt_in("iota", [128, 128])
    gkv_d = dt_in("gkv", [1, 128])
    wukT_d = dt_in("wukT", [128, 4, 128])
    wuv_d = dt_in("wuv", [128, 4, 128])
    lruv_d = dt_in("lruv", [128, 36])
    wbd_d = dt_in("wbd", [128, 8, 128])
    wout_d = dt_in("wout", [128, 8, 1024])
    wpq_d = dt_in("wpq", [16, 128, 8, 128])
    kT_d = dt_in("kT", [128, 16, 128])
    uT_d = dt_in("uT", [128, 128, 8, 128])
    v_d = dt_in("v", [16384, D])
    wple_d = dt_in("wple", [128, 2, 1024])
    wpg_d = dt_in("wpg", [128, 8, 1024])
    gfin_d = dt_in("gfin", [1, D])
    out_d = nc.dram_tensor("out", [S, D], F32, kind="ExternalOutput").ap()
    dbg_d = None
    if dbg is not None:
        dbg_d = nc.dram_tensor("dbg", list(dbg), F32, kind="ExternalOutput").ap()
    xmid_d = nc.dram_tensor("xmid", [S, D], F32).ap()
    gd_d = nc.dram_tensor("gd", [8, 128, 128, 256], BF16).ap()

    ub_d = nc.dram_tensor("ub16", [128, 128, 8, 128], BF16).ap()
    vbf_d = nc.dram_tensor("vb16", [16384, D], BF16).ap()
    wpqb_d = nc.dram_tensor("wpq16", [16, 128, 8, 128], BF16).ap()
    sch = Sched(nc)
    ar = Arena(nc, 204 * 1024)
    ps = [nc.alloc_psum_tensor(f"ps{i}", [128, 512], F32).ap() for i in range(8)]
    psb = [p.bitcast(BF16) for p in ps]
    op = sch.op

    ident_f = ar.alloc([128, 128], F32)
    ident = ar.alloc([128, 128], BF16)
    cmask = ar.alloc([128, 128], F32)
    iota_f = ar.alloc([128, 128], F32)
    iota_b = ar.alloc([128, 128], BF16)
    gvec = ar.alloc([128, 24], F32)
    ropec = ar.alloc([128, 4], F32)
    lruv = ar.alloc([128, 36], F32)
    op('sp', lambda e: e.dma_start(out=ident_f, in_=ident_d), w=['ident_f'])
    op('sp', lambda e: e.dma_start(out=cmask, in_=cmask_d), w=['cmask'])
    op('sp', lambda e: e.dma_start(out=iota_f, in_=iota_d), w=['iota_f'])
    op('sp', lambda e: e.dma_start(out=gvec, in_=gvec_d), w=['gvec'])
    op('sp', lambda e: e.dma_start(out=ropec, in_=ropec_d), w=['ropec'])
    op('sp', lambda e: e.dma_start(out=lruv, in_=lruv_d), w=['lruv'])
    op('dve', lambda e: e.tensor_copy(out=ident, in_=ident_f), r=['ident_f'], w=['ident'])
    op('dve', lambda e: e.tensor_copy(out=iota_b, in_=iota_f), r=['iota_f'], w=['iota_b'])

    junk = ar.alloc([128, 2048], BF16)
    epsc = ar.alloc([128, 1], F32)
    op('dve', lambda e: e.memset(epsc, EPS), w=['epsc'])
    onec = ar.alloc([128, 1], F32)
    op('dve', lambda e: e.memset(onec, 1.0), w=['onec'])
    negb = ar.alloc([128, 1], F32)
    op('dve', lambda e: e.memset(negb, -3.141589), w=['negb'])
    stat = ar.alloc([128, 64], F32)
    h_off = ar.off
    hT = ar.alloc([128, 8, S], BF16)
    cat_off = ar.off
    catT = ar.alloc([128, 8, S], BF16)
    cat_alias = [ar.ap[:, cat_off + i * 8192:cat_off + (i + 1) * 8192].bitcast(F32) for i in range(2)]

    def rmsnorm_to_T(src_tile_fn, gcol, dstT, tag):
        m = ar.mark()
        xn = [ar.alloc([128, D], BF16) for _ in range(2)]
        for i in range(NT):
            xt, xkey = src_tile_fn(i)
            ss = stat[:, i:i + 1]
            rs = stat[:, 16 + i:17 + i]
            op('act', lambda e, xt=xt, ss=ss: e.activation(out=junk[:, 0:D], in_=xt, func=AF.Square, accum_out=ss),
               r=[xkey], w=['junk', (tag, 'ss', i)])
            op('act', lambda e, ss=ss, rs=rs: e.activation(out=rs, in_=ss, func=AF.Sqrt, scale=1.0 / D, bias=epsc),
               r=[(tag, 'ss', i), 'epsc'], w=[(tag, 'rs', i)])
            op('dve', lambda e, rs=rs: e.reciprocal(out=rs, in_=rs),
               r=[(tag, 'rs', i)], w=[(tag, 'rs', i)])
            xb = xn[i % 2]
            op('dve', lambda e, xb=xb, xt=xt, rs=rs: e.tensor_scalar(out=xb, in0=xt, scalar1=rs, scalar2=None,
                                                                      op0=OP.mult),
               r=[xkey, (tag, 'rs', i)], w=[(tag, 'xn', i % 2)])
            bank = i % 2
            for k in range(8):
                op('pe', lambda e, k=k, xb=xb, bank=bank: e.transpose(out=psb[bank][:, k * 128:(k + 1) * 128],
                                                                       in_=xb[:, k * 128:(k + 1) * 128],
                                                                       identity=ident),
                   r=[(tag, 'xn', i % 2), 'ident'], w=[('ps', bank)])
            g3 = bc_last(gvec[:, gcol:gcol + 8], 128)
            op('dve', lambda e, i=i, bank=bank, g3=g3: e.tensor_tensor(
                out=dstT[:, :, i * 128:(i + 1) * 128],
                in0=psb[bank].rearrange("p (k t) -> p k t", k=8), in1=g3, op=OP.mult),
               r=[('ps', bank), 'gvec'], w=[(tag, 'T', i)])
        ar.reset(m)

    mA = ar.mark()
    xts = [ar.alloc([128, D], F32) for _ in range(2)]

    def srcA(i):
        xt = xts[i % 2]
        op('sp', lambda e, xt=xt, i=i: e.dma_start(out=xt, in_=x_d[i * 128:(i + 1) * 128, :]), w=[('xt', i % 2)])
        return xt, ('xt', i % 2)
    rmsnorm_to_T(srcA, 0, hT, 'A')
    ar.reset(mA)
    sch.fence()
    hT_keys = [('A', 'T', i) for i in range(NT)]

    def finish_dbg(src_ap, keys, rows, cols, conv=None):
        op('sp', lambda e: e.dma_start(out=dbg_d[0:rows, 0:cols], in_=src_ap), r=keys, w=['dbg'])

    def dbg_point(name, ap, keys, rows=128, ncols=S):
        if stop != name:
            return
        op('pq', lambda e: e.dma_start(out=dbg_d[0:rows, 0:ncols], in_=ap), r=keys, w=['dbg'])
        with nc.Block() as block:
            sch.emit(block)
        raise _StopBuild(nc)

    def dbg_multi(name, items):
        if stop != name:
            return
        for (ap, keys, c0, ncol) in items:
            op('pq', lambda e, ap=ap, c0=c0, ncol=ncol: e.dma_start(out=dbg_d[0:128, c0:c0 + ncol], in_=ap), r=keys, w=[('dbg', c0)])
        with nc.Block() as block:
            sch.emit(block)
        raise _StopBuild(nc)

    if stop == 'A':
        m = ar.mark()
        tmp = ar.alloc([128, S], F32)
        op('dve', lambda e: e.tensor_copy(out=tmp, in_=hT[:, 3, :]), r=hT_keys, w=['tmp'])
        finish_dbg(tmp, ['tmp'], 128, S)
        with nc.Block() as block:
            sch.emit(block)
        return nc

    wb = [ar.alloc([128, 8, 128], BF16) for _ in range(4)]
    wctr = [0]

    def load_w(src_ap):
        slot = wctr[0] % 4
        wctr[0] += 1
        t = wb[slot]
        op('pq', lambda e: e.dma_start(out=t, in_=src_ap), w=[('wb', slot)])
        return t, ('wb', slot)
    pctr = [0]

    def next_bank(lo=0, n=4):
        b = lo + pctr[0] % n
        pctr[0] += 1
        return b

    def cols(n, w=512):
        return slice(n * w, (n + 1) * w)

    def proj_mm(wt, wkey, n, bank, M=128, srcT=None, skeys=None):
        srcT = hT if srcT is None else srcT
        skeys = hT_keys if skeys is None else skeys
        for k in range(8):
            op('pe', lambda e, k=k: e.matmul(ps[bank][0:M, :], wt[:, k, 0:M], srcT[:, k, cols(n)],
                                             start=(k == 0), stop=(k == 7)),
               r=[wkey] + skeys[n * 4:(n + 1) * 4], w=[('ps', bank)])

    mL = ar.mark()
    wbd = ar.alloc([128, 8, 128], BF16)
    op('pq', lambda e: e.dma_start(out=wbd, in_=wbd_d), w=['wbd'])
    spc = ar.alloc([128, 8], F32)
    op('act', lambda e: e.activation(out=spc[:, 0:4], in_=lruv[:, 28:32], func=AF.Exp, scale=-1.0),
       r=['lruv'], w=['spc'])
    op('act', lambda e: e.activation(out=spc[:, 0:4], in_=spc[:, 0:4], func=AF.Ln, bias=onec),
       r=['spc', 'onec'], w=['spc'])
    op('dve', lambda e: e.tensor_scalar(out=spc[:, 4:8], in0=spc[:, 0:4], scalar1=-16.0, scalar2=None, op0=OP.mult),
       r=['spc'], w=['spc'])
    op('dve', lambda e: e.tensor_scalar(out=spc[:, 0:4], in0=spc[:, 0:4], scalar1=-8.0, scalar2=None, op0=OP.mult),
       r=['spc'], w=['spc'])
    Lb = {nm: ar.alloc([128, S], F32) for nm in ['xb', 'gate', 'xc', 'r', 'i', 'a', 'b']}
    xcb = ar.alloc([128, S], BF16)
    for c in range(4):
        for nm, chs in (('xb', CH_XB), ('gate', CH_GT)):
            wt, wk = load_w(wfm_d[chs[c]])
            for n in range(4):
                bank = next_bank()
                proj_mm(wt, wk, n, bank)
                op('act', lambda e, nm=nm, n=n, bank=bank: e.activation(out=Lb[nm][:, cols(n)], in_=ps[bank],
                                                                         func=AF.Copy),
                   r=[('ps', bank)], w=[nm])
        xb_, gate_, xc_, r_, i_, a_, b_ = (Lb[k] for k in ['xb', 'gate', 'xc', 'r', 'i', 'a', 'b'])
        cw = lambda tap, c=c: lruv[:, c * 4 + tap:c * 4 + tap + 1]
        op('dve', lambda e, c=c, cw=cw: e.tensor_scalar(out=xc_, in0=xb_, scalar1=cw(3), scalar2=lruv[:, 16 + c:17 + c],
                                                        op0=OP.mult, op1=OP.add),
           r=['xb', 'lruv'], w=['xc'])
        for sft in (1, 2, 3):
            op('dve', lambda e, sft=sft, cw=cw: e.scalar_tensor_tensor(out=xc_[:, sft:], in0=xb_[:, :S - sft],
                                                                       scalar=cw(3 - sft), in1=xc_[:, sft:],
                                                                       op0=OP.mult, op1=OP.add),
               r=['xb', 'xc', 'lruv'], w=['xc'])
        if c == 1:
            dbg_point('C:xb', xb_, ['xb'])
            dbg_point('C:gate', gate_, ['gate'])
            dbg_point('C:xc', xc_, ['xc'])
        op('act', lambda e: e.activation(out=xcb, in_=xc_, func=AF.Copy), r=['xc'], w=['xcb'])
        for gi, (nm, bcol) in enumerate((('r', 20), ('i', 24))):
            for n in range(4):
                bank = next_bank()
                op('pe', lambda e, gi=gi, c=c, n=n, bank=bank: e.matmul(ps[bank], wbd[:, gi * 4 + c, :],
                                                                         xcb[:, cols(n)], start=True, stop=True),
                   r=['wbd', 'xcb'], w=[('ps', bank)])
                op('act', lambda e, nm=nm, n=n, bank=bank, bcol=bcol, c=c: e.activation(
                    out=Lb[nm][:, cols(n)], in_=ps[bank], func=AF.Sigmoid, bias=lruv[:, bcol + c:bcol + c + 1]),
                   r=[('ps', bank), 'lruv'], w=[nm])
        op('act', lambda e, c=c: e.activation(out=a_, in_=r_, func=AF.Exp, scale=spc[:, c:c + 1]),
           r=['r', 'spc'], w=['a'])
        op('act', lambda e, c=c: e.activation(out=b_, in_=r_, func=AF.Exp, scale=spc[:, 4 + c:5 + c]),
           r=['r', 'spc'], w=['b'])
        op('act', lambda e: e.activation(out=b_, in_=b_, func=AF.Sqrt, scale=-1.0, bias=onec),
           r=['b', 'onec'], w=['b'])
        if c == 1:
            dbg_point('C:r', r_, ['r'])
            dbg_point('C:i', i_, ['i'])
            dbg_point('C:a', a_, ['a'])
            dbg_point('C:m', b_, ['b'])
        op('pool', lambda e: e.tensor_tensor(out=i_, in0=i_, in1=xc_, op=OP.mult), r=['i', 'xc'], w=['i'])
        op('dve', lambda e: e.tensor_tensor(out=b_, in0=b_, in1=i_, op=OP.mult), r=['b', 'i'], w=['b'])
        op('dve', lambda e: e.tensor_tensor_scan(out=r_, data0=a_, data1=b_, initial=0.0, op0=OP.mult, op1=OP.add),
           r=['a', 'b'], w=['r'])
        if c == 1:
            dbg_point('C:h', r_, ['r'])
        op('act', lambda e: e.activation(out=a_, in_=gate_, func=AF.Square), r=['gate'], w=['a'])
        op('dve', lambda e: e.tensor_scalar(out=a_, in0=a_, scalar1=0.044715, scalar2=1.0, op0=OP.mult, op1=OP.add),
           r=['a'], w=['a'])
        op('dve', lambda e: e.tensor_tensor(out=a_, in0=a_, in1=gate_, op=OP.mult), r=['a', 'gate'], w=['a'])
        op('act', lambda e: e.activation(out=a_, in_=a_, func=AF.Sigmoid, scale=1.5957691216057308),
           r=['a'], w=['a'])
        op('pool', lambda e: e.tensor_tensor(out=b_, in0=r_, in1=gate_, op=OP.mult), r=['r', 'gate'], w=['b'])
        op('dve', lambda e, c=c: e.tensor_tensor(out=catT[:, 4 + c, :], in0=b_, in1=a_, op=OP.mult),
           r=['a', 'b'], w=[('cat', 4 + c)])
    if stop == 'C':
        tmp = ar.alloc([128, S], F32)
        op('dve', lambda e: e.tensor_copy(out=tmp, in_=catT[:, 5, :]), r=[('cat', 5)], w=['tmp'])
        finish_dbg(tmp, ['tmp'], 128, S)
        with nc.Block() as block:
            sch.emit(block)
        return nc
    ar.reset(mL)
    sch.fence()

    mAtt = ar.mark()
    q_rotT = ar.alloc([128, 4, S], BF16)
    q_latT = ar.alloc([128, 4, S], BF16)
    iq_rotT = ar.alloc([128, 4, S], BF16)
    k_rotT = ar.alloc([128, S], BF16)
    ik2T = ar.alloc([128, S], BF16)
    kv_latT = ar.alloc([128, S], BF16)
    kv1 = ar.alloc([128, 16, 130], BF16)
    iw_sb = ar.alloc([128, 16, 8], F32)
    wukT = ar.alloc([128, 4, 128], BF16)
    wuv = ar.alloc([128, 4, 128], BF16)
    gkv_bc = ar.alloc([128, 128], F32)
    wtm = ar.alloc([128, 8, 136], BF16)
    op('pq', lambda e: e.dma_start(out=wukT, in_=wukT_d), w=['wukT'])
    op('pq', lambda e: e.dma_start(out=wuv, in_=wuv_d), w=['wuv'])
    op('pq', lambda e: e.dma_start(out=wtm, in_=wtm_d), w=['wtm'])
    op('sp', lambda e: e.dma_start(out=gkv_bc, in_=bass.AP(gkv_d.tensor, 0, [[0, 128], [1, 128]])), w=['gkv'])
    mTab = ar.mark()
    posf = ar.alloc([128, S], F32)
    tabs = cat_alias + [ar.alloc([128, S], F32) for _ in range(2)]
    ttmp = ar.alloc([128, S], F32)
    op('pq', lambda e: e.dma_start(out=posf, in_=bass.AP(pos_d.tensor, 0, [[0, 128], [1, S]])), w=['posf'])
    SC = 0.999999
    MAGIC = 12582912.0
    ttmp2 = t2buf = None
    for (tc_, ts_, icol, scol) in ((tabs[0], tabs[1], 0, 1), (tabs[2], tabs[3], 2, 3)):
        for (dst, shift, sg) in ((ts_, 0.0, True), (tc_, 0.25, False)):
            op('dve', lambda e, icol=icol, shift=shift: e.tensor_scalar(out=ttmp, in0=posf,
                                                                        scalar1=ropec[:, icol:icol + 1],
                                                                        scalar2=shift, op0=OP.mult, op1=OP.add),
               r=['posf', 'ropec'], w=['ttmp'])
            op('dve', lambda e, dst=dst: e.tensor_scalar(out=dst, in0=ttmp, scalar1=MAGIC, scalar2=None, op0=OP.add),
               r=['ttmp'], w=['tabs'])
            op('dve', lambda e, dst=dst: e.tensor_scalar(out=dst, in0=dst, scalar1=-MAGIC, scalar2=None, op0=OP.add),
               r=['tabs'], w=['tabs'])
            op('dve', lambda e, dst=dst: e.tensor_tensor(out=ttmp, in0=ttmp, in1=dst, op=OP.subtract),
               r=['tabs', 'ttmp'], w=['ttmp'])
            op('act', lambda e, dst=dst: e.activation(out=dst, in_=ttmp, func=AF.Sin, scale=2 * math.pi * SC),
               r=['ttmp'], w=['tabs'])
            if sg:
                op('dve', lambda e, dst=dst, scol=scol: e.tensor_scalar(out=dst, in0=dst,
                                                                        scalar1=ropec[:, scol:scol + 1],
                                                                        scalar2=None, op0=OP.mult),
                   r=['tabs', 'ropec'], w=['tabs'])

    t1s = [ar.alloc([128, 512], F32) for _ in range(2)]
    t2s = [ar.alloc([128, 512], F32) for _ in range(2)]
    rctr = [0]

    def rope_proj(chX, chP, Tc, Ts, dst_fn, dkey, M=128):
        wX, kX = load_w(wfm_d[chX])
        wP, kP = load_w(wfm_d[chP])
        for n in range(4):
            bX = next_bank()
            proj_mm(wX, kX, n, bX, M)
            bP = next_bank()
            proj_mm(wP, kP, n, bP, M)
            sl = rctr[0] % 2
            rctr[0] += 1
            t1, t2 = t1s[sl], t2s[sl]
            op('dve', lambda e, bX=bX, n=n, t1=t1: e.tensor_tensor(out=t1[0:M], in0=ps[bX][0:M], in1=Tc[0:M, cols(n)],
                                                                   op=OP.mult),
               r=[('ps', bX), 'tabs'], w=[('t1', sl)])
            op('dve', lambda e, bP=bP, n=n, t2=t2: e.tensor_tensor(out=t2[0:M], in0=ps[bP][0:M], in1=Ts[0:M, cols(n)],
                                                                   op=OP.mult),
               r=[('ps', bP), 'tabs'], w=[('t2', sl)])
            op('pool', lambda e, n=n, t1=t1, t2=t2: e.tensor_tensor(out=dst_fn(n), in0=t1[0:M], in1=t2[0:M], op=OP.add),
               r=[('t1', sl), ('t2', sl)], w=[dkey])

    for h in range(4):
        rope_proj(CH_Q[h], CH_QP[h], tabs[0], tabs[1], lambda n, h=h: q_rotT[:, h, cols(n)], ('qrot', h))
        for n in range(4):
            bank = next_bank()
            op('pe', lambda e, h=h, n=n, bank=bank: e.matmul(ps[bank], wukT[:, h, :], q_rotT[:, h, cols(n)],
                                                             start=True, stop=True),
               r=['wukT', ('qrot', h)], w=[('ps', bank)])
            op('act', lambda e, h=h, n=n, bank=bank: e.activation(out=q_latT[:, h, cols(n)], in_=ps[bank],
                                                                  func=AF.Copy),
               r=[('ps', bank)], w=[('qlat', h)])
    rope_proj(CH_K, CH_KP, tabs[0], tabs[1], lambda n: k_rotT[0:32, cols(n)], 'krot', M=32)
    for m_ in range(4):
        rope_proj(CH_IQ[m_], CH_IQP[m_], tabs[2], tabs[3], lambda n, m_=m_: iq_rotT[:, m_, cols(n)], ('iqrot', m_))
    rope_proj(CH_IK, CH_IKP, tabs[2], tabs[3], lambda n: ik2T[:, cols(n)], 'ik2')
    op('pool', lambda e: e.memset(kv1[:, :, 128:130], 1.0), w=[('kv1', i) for i in range(NT)])
    for i in range(NT):
        bank = next_bank()
        for k in range(8):
            op('pe', lambda e, k=k, i=i, bank=bank: e.matmul(ps[bank][:, 0:136], hT[:, k, i * 128:(i + 1) * 128],
                                                             wtm[:, k, :], start=(k == 0), stop=(k == 7)),
               r=['wtm', hT_keys[i]], w=[('ps', bank)])
        ssq = stat[:, 32 + i:33 + i]
        rk = stat[:, 48 + i:49 + i]
        op('act', lambda e, bank=bank, ssq=ssq: e.activation(out=junk[:, 0:128], in_=ps[bank][:, 0:128],
                                                             func=AF.Square, accum_out=ssq),
           r=[('ps', bank)], w=['junk', ('kss', i)])
        op('act', lambda e, ssq=ssq, rk=rk: e.activation(out=rk, in_=ssq, func=AF.Sqrt, scale=1.0 / 128, bias=epsc),
           r=[('kss', i), 'epsc'], w=[('krs', i)])
        op('dve', lambda e, rk=rk: e.reciprocal(out=rk, in_=rk), r=[('krs', i)], w=[('krs', i)])
        op('dve', lambda e, bank=bank, rk=rk, i=i: e.scalar_tensor_tensor(out=kv1[:, i, 0:128], in0=ps[bank][:, 0:128],
                                                                          scalar=rk, in1=gkv_bc,
                                                                          op0=OP.mult, op1=OP.mult),
           r=[('ps', bank), ('krs', i), 'gkv'], w=[('kv1', i)])
        op('dve', lambda e, bank=bank, i=i: e.tensor_scalar(out=iw_sb[:, i, :], in0=ps[bank][:, 128:136],
                                                            scalar1=IW_SCALE, scalar2=None, op0=OP.mult),
           r=[('ps', bank)], w=[('iw', i)])
        bT = next_bank()
        op('pe', lambda e, i=i, bT=bT: e.transpose(out=psb[bT][:, 0:128], in_=kv1[:, i, 0:128], identity=ident),
           r=[('kv1', i), 'ident'], w=[('ps', bT)])
        op('act', lambda e, i=i, bT=bT: e.activation(out=kv_latT[:, i * 128:(i + 1) * 128], in_=psb[bT][:, 0:128],
                                                     func=AF.Copy),
           r=[('ps', bT)], w=[('kvT', i)])
    dbg_point('E:qrot', q_rotT[:, 1, :], [('qrot', 1)])
    dbg_point('E:qlat', q_latT[:, 1, :], [('qlat', 1)])
    dbg_point('E:krot', k_rotT[0:32, :], ['krot'], rows=32)
    dbg_point('E:iqrot', iq_rotT[:, 1, :], [('iqrot', 1)])
    dbg_point('E:ik2', ik2T, ['ik2'])
    dbg_point('E:kvT', kv_latT, [('kvT', i) for i in range(NT)])
    dbg_point('E:tabs', tabs[1], ['tabs'])
    ar.reset(mTab)
    sch.fence()

    o_nT = ar.alloc([128, 4, S], BF16)
    score = ar.alloc([128, S], F32)
    relu_sb = [ar.alloc([128, 512], F32) for _ in range(2)]
    maskb = ar.alloc([128, S], BF16)
    maskT = ar.alloc([128, 16, 128], BF16)
    Eb = [ar.alloc([128, 512], BF16) for _ in range(2)]
    PTb = [ar.alloc([128, 4, 128], BF16) for _ in range(2)]
    o_n = ar.alloc([128, 4, 128], BF16)
    bis = ar.alloc([128, 8], F32)
    recyc = ['tabs', 'posf', 'posi', 'ttmp', ('t1', 0), ('t1', 1), ('t2', 0), ('t2', 1)]
    NIT = 26
    RNG = 64.0
    ectr = [0]
    har = Arena(nc, 32768, ap=ar.ap, base=h_off)
    scoreB = [score, har.alloc([128, S], F32)]
    score2 = har.alloc([128, S], F32)
    maskbB = [maskb, har.alloc([128, S], BF16)]
    junkB = har.alloc([128, S], BF16)
    reluB = relu_sb + [har.alloc([128, 512], F32) for _ in range(2)]
    bisB = [bis, har.alloc([128, 8], F32)]
    rctr2 = [0]

    def att_scores(j, b):
        op('pq', lambda e: e.dma_start(out=ub_d[8 * j:8 * j + 8].rearrange("c p k e -> (c p) (k e)"),
                                       in_=uT_d[8 * j:8 * j + 8].rearrange("c p k e -> (c p) (k e)")),
           w=[('ub', j)])
        op('pq', lambda e: e.dma_start(out=vbf_d.rearrange("(c e1) d -> c e1 d", e1=128)[:, 8 * j:8 * j + 8, :],
                                       in_=v_d[1024 * j:1024 * (j + 1), :].rearrange("(e1 c) d -> c e1 d", c=128)),
           w=[('vbf', j)])
        if j == 0:
            op('pq', lambda e: e.dma_start(out=wpqb_d.rearrange("c p k e -> (c p) (k e)"),
                                           in_=wpq_d.rearrange("c p k e -> (c p) (k e)")), w=['wpqb'])
        Sj = (j + 1) * 128
        tq = slice(j * 128, (j + 1) * 128)
        sc_ = scoreB[b]
        skey = ('score', b)
        nch = (Sj + 511) // 512
        for cc in range(nch):
            w_ = min(512, Sj - cc * 512)
            cs = slice(cc * 512, cc * 512 + w_)
            for hh in range(8):
                pb = (hh % 2) * 64
                bank = next_bank(0, 2)
                op('pe', lambda e, hh=hh, pb=pb, bank=bank, cs=cs, w_=w_: e.matmul(
                    ps[bank][:, 0:w_], iq_rotT[pb:pb + 64, hh // 2, tq], ik2T[pb:pb + 64, cs], start=True, stop=True),
                   r=[('iqrot', hh // 2), 'ik2'], w=[('ps', bank)])
                ri = rctr2[0] % 4
                rctr2[0] += 1
                rl = reluB[ri]
                op('act', lambda e, bank=bank, rl=rl, w_=w_: e.activation(out=rl[:, 0:w_], in_=ps[bank][:, 0:w_],
                                                                          func=AF.Relu),
                   r=[('ps', bank)], w=[('relu', ri)])
                eng = 'dve'
                dst = sc_
                dkey = skey
                if hh == 0:
                    op(eng, lambda e, rl=rl, cs=cs, w_=w_, hh=hh, dst=dst: e.tensor_scalar(
                        out=dst[:, cs], in0=rl[:, 0:w_], scalar1=iw_sb[:, j, hh:hh + 1], scalar2=None, op0=OP.mult),
                       r=[('relu', ri), ('iw', j)], w=[dkey])
                elif eng == 'dve':
                    op(eng, lambda e, rl=rl, cs=cs, w_=w_, hh=hh, dst=dst: e.scalar_tensor_tensor(
                        out=dst[:, cs], in0=rl[:, 0:w_], scalar=iw_sb[:, j, hh:hh + 1], in1=dst[:, cs],
                        op0=OP.mult, op1=OP.add),
                       r=[('relu', ri), ('iw', j), dkey], w=[dkey])
                else:
                    op(eng, lambda e, rl=rl, w_=w_, hh=hh: e.tensor_scalar(
                        out=rl[:, 0:w_], in0=rl[:, 0:w_], scalar1=iw_sb[:, j, hh:hh + 1], scalar2=None, op0=OP.mult),
                       r=[('relu', ri), ('iw', j)], w=[('relu', ri)])
                    op(eng, lambda e, rl=rl, cs=cs, w_=w_, dst=dst: e.tensor_tensor(
                        out=dst[:, cs], in0=dst[:, cs], in1=rl[:, 0:w_], op=OP.add),
                       r=[('relu', ri), dkey], w=[dkey])
        op('dve', lambda e: e.tensor_tensor(out=sc_[:, tq], in0=sc_[:, tq], in1=cmask, op=OP.add),
           r=[skey, 'cmask'], w=[skey])

    def att_bisect_pair(jA, jB):
        A, B = bisB[0], bisB[1]
        SA, SB = (jA + 1) * 128, (jB + 1) * 128
        doA, doB = jA >= 2, jB >= 2
        if doA:
            op('dve', lambda e: e.memset(A[:, 0:1], 0.0), w=[('bis', 0)])
        else:
            op('dve', lambda e: e.memset(A[:, 3:4], -1.0e29), w=[('bis', 0)])
        if doB:
            op('dve', lambda e: e.memset(B[:, 0:1], 0.0), w=[('bis', 1)])
        else:
            op('dve', lambda e: e.memset(B[:, 3:4], -1.0e29), w=[('bis', 1)])
        for it in range(NIT):
            wd = RNG / (2 ** (it + 1))
            if doA:
                op('dve', lambda e: e.tensor_scalar(out=junk[:, 0:SA], in0=scoreB[0][:, 0:SA], scalar1=A[:, 0:1],
                                                    scalar2=None, op0=OP.is_ge, op1=OP.add, accum_out=A[:, 1:2]),
                   r=[('score', 0), ('bis', 0)], w=['junk', ('bis', 0)])
            if doB:
                op('act', lambda e: e.activation(out=junkB[:, 0:SB], in_=scoreB[1][:, 0:SB], func=AF.Sign,
                                                 bias=B[:, 0:1], accum_out=B[:, 1:2]),
                   r=[('score', 1), ('bis', 1)], w=['junkB', ('bis', 1)])
            if doA:
                op('dve', lambda e, wd=wd: e.tensor_scalar(out=A[:, 2:3], in0=A[:, 1:2], scalar1=255.5, scalar2=2.0 * wd,
                                                           op0=OP.is_ge, op1=OP.mult),
                   r=[('bis', 0)], w=[('bis', 0)])
                op('dve', lambda e, wd=wd: e.scalar_tensor_tensor(out=A[:, 0:1], in0=A[:, 2:3], scalar=-wd, in1=A[:, 0:1],
                                                                  op0=OP.add, op1=OP.add),
                   r=[('bis', 0)], w=[('bis', 0)])
            if doB:
                op('dve', lambda e, wd=wd: e.tensor_scalar(out=B[:, 2:3], in0=B[:, 1:2], scalar1=511.0 - SB,
                                                           scalar2=-2.0 * wd, op0=OP.is_ge, op1=OP.mult),
                   r=[('bis', 1)], w=[('bis', 1)])
                op('dve', lambda e, wd=wd: e.scalar_tensor_tensor(out=B[:, 0:1], in0=B[:, 2:3], scalar=wd, in1=B[:, 0:1],
                                                                  op0=OP.add, op1=OP.add),
                   r=[('bis', 1)], w=[('bis', 1)])
        wl = RNG / (2 ** NIT)
        if doA:
            op('dve', lambda e: e.tensor_scalar(out=A[:, 3:4], in0=A[:, 0:1], scalar1=-wl, scalar2=None, op0=OP.add),
               r=[('bis', 0)], w=[('bis', 0)])
        if doB:
            op('dve', lambda e: e.tensor_scalar(out=B[:, 3:4], in0=B[:, 0:1], scalar1=-1.0, scalar2=-wl,
                                                op0=OP.mult, op1=OP.add),
               r=[('bis', 1)], w=[('bis', 1)])

    def att_finish(j, b):
        if j == 0:
            op('pq', lambda e: e.dma_start(out=wpqb_d.rearrange("c p k e -> (c p) (k e)"),
                                           in_=wpq_d.rearrange("c p k e -> (c p) (k e)")), w=['wpqb'])
        Sj = (j + 1) * 128
        tq = slice(j * 128, (j + 1) * 128)
        sc_ = scoreB[b]
        mk = maskbB[b]
        tcol = bisB[b][:, 3:4]
        rdn = bisB[b]
        op('dve', lambda e: e.tensor_scalar(out=mk[:, 0:Sj], in0=sc_[:, 0:Sj], scalar1=tcol, scalar2=None,
                                            op0=OP.is_ge),
           r=[('score', b), ('bis', b)], w=[('maskb', b)])
        for g8 in range((j + 8) // 8):
            bM = 2 + g8
            lo = g8 * 8
            hi = min(j + 1, lo + 8)
            for sc in range(lo, hi):
                op('pe', lambda e, sc=sc, bM=bM, lo=lo: e.transpose(out=psb[bM][:, (sc - lo) * 128:(sc - lo + 1) * 128],
                                                                    in_=mk[:, sc * 128:(sc + 1) * 128],
                                                                    identity=ident),
                   r=[('maskb', b), 'ident'], w=[('ps', bM)])
            op('act', lambda e, bM=bM, lo=lo, hi=hi: e.activation(
                out=maskT[:, lo:hi, :], in_=psb[bM][:, 0:(hi - lo) * 128].rearrange("p (a b) -> p a b", b=128),
                func=AF.Copy),
               r=[('ps', bM)], w=['maskT'])
        for sc in range(j + 1):
            sk = slice(sc * 128, (sc + 1) * 128)
            bL = 2 + (ectr[0] % 2)
            sl = ectr[0] % 2
            ectr[0] += 1
            op('pe', lambda e, sk=sk, bL=bL: e.matmul(ps[bL], kv_latT[:, sk], q_latT[:, :, tq], start=True, stop=False),
               r=[('kvT', sc)] + [('qlat', h) for h in range(4)], w=[('ps', bL)])
            op('pe', lambda e, sk=sk, bL=bL: e.matmul(ps[bL], k_rotT[0:32, sk], q_rotT[0:32, :, tq],
                                                      start=False, stop=True),
               r=['krot'] + [('qrot', h) for h in range(4)], w=[('ps', bL)])
            op('act', lambda e, bL=bL, sl=sl: e.activation(out=Eb[sl], in_=ps[bL], func=AF.Exp, scale=ATT_SCALE),
               r=[('ps', bL)], w=[('E', sl)])
            op('dve', lambda e, sl=sl, sc=sc: e.tensor_tensor(
                out=PTb[sl], in0=Eb[sl].rearrange("p (h t) -> p h t", h=4), in1=bc_mid(maskT[:, sc, :], 1, 4),
                op=OP.mult),
               r=[('E', sl), 'maskT'], w=[('PT', sl)])
            for h in range(4):
                bO = 4 + h
                op('pe', lambda e, h=h, bO=bO, sl=sl, sc=sc: e.matmul(
                    ps[bO][:, 0:129], PTb[sl][:, h, :], kv1[:, sc, 0:129], start=(sc == 0), stop=(sc == j)),
                   r=[('PT', sl), ('kv1', sc)], w=[('ps', bO)])
        for h in range(4):
            op('dve', lambda e, h=h: e.reciprocal(out=rdn[:, 4 + h:5 + h], in_=ps[4 + h][:, 128:129]),
               r=[('ps', 4 + h)], w=[('rden', b, h)])
            op('dve', lambda e, h=h: e.tensor_scalar(out=o_n[:, h, :], in0=ps[4 + h][:, 0:128],
                                                     scalar1=rdn[:, 4 + h:5 + h], scalar2=None, op0=OP.mult),
               r=[('ps', 4 + h), ('rden', b, h)], w=[('o_n', h)])
        for h in range(4):
            op('pe', lambda e, h=h: e.transpose(out=psb[2][:, h * 128:(h + 1) * 128], in_=o_n[:, h, :], identity=ident),
               r=[('o_n', h), 'ident'], w=[('ps', 2)])
        op('act', lambda e: e.activation(out=o_nT[:, :, tq], in_=psb[2][:, 0:512].rearrange("p (h t) -> p h t", h=4),
                                         func=AF.Copy),
           r=[('ps', 2)], w=[('o_nT', j)])

    NIT = 20
    RNG = 16.0
    for jp in range(NT // 2):
        jA, jB = 2 * jp, 2 * jp + 1
        att_scores(jA, 0)
        att_scores(jB, 1)
        att_bisect_pair(jA, jB)
        att_finish(jA, 0)
        att_finish(jB, 1)

    for h in range(4):
        for n in range(4):
            bank = next_bank()
            op('pe', lambda e, h=h, n=n, bank=bank: e.matmul(ps[bank], wuv[:, h, :], o_nT[:, h, cols(n)],
                                                             start=True, stop=True),
               r=['wuv'] + [('o_nT', jj) for jj in range(n * 4, n * 4 + 4)], w=[('ps', bank)])
            op('act', lambda e, h=h, n=n, bank=bank: e.activation(out=catT[:, h, cols(n)], in_=ps[bank], func=AF.Copy),
               r=[('ps', bank)], w=[('cat', h)])
    if stop == 'E':
        tmp = ar.alloc([128, S], F32)
        op('dve', lambda e: e.tensor_copy(out=tmp, in_=catT[:, 1, :]), r=[('cat', 1)], w=['tmp'])
        finish_dbg(tmp, ['tmp'], 128, S)
        with nc.Block() as block:
            sch.emit(block)
        return nc

    ar.reset(mAtt)
    sch.fence()
    mF = ar.mark()
    x_sb = ar.alloc([128, NT, D], F32)
    wbig = ar.alloc([128, 8, 1024], BF16)
    for q4 in range(4):
        op('sp', lambda e, q4=q4: e.dma_start(out=x_sb[:, q4 * 4:(q4 + 1) * 4, :],
                                             in_=x_d[q4 * 512:(q4 + 1) * 512, :].rearrange("(n p) m -> p n m", p=128)),
           w=[('x', i) for i in range(q4 * 4, q4 * 4 + 4)])
    op('pq', lambda e: e.dma_start(out=wbig, in_=wout_d), w=['wbig'])
    cat_keys = [('cat', k) for k in range(8)]
    for i in range(NT):
        for hf in range(2):
            bank = next_bank()
            for k in range(8):
                op('pe', lambda e, i=i, hf=hf, k=k, bank=bank: e.matmul(
                    ps[bank], catT[:, k, i * 128:(i + 1) * 128], wbig[:, k, hf * 512:(hf + 1) * 512],
                    start=(k == 0), stop=(k == 7)),
                   r=['wbig'] + cat_keys, w=[('ps', bank)])
            op('dve', lambda e, i=i, hf=hf, bank=bank: e.tensor_tensor(
                out=x_sb[:, i, hf * 512:(hf + 1) * 512], in0=ps[bank], in1=x_sb[:, i, hf * 512:(hf + 1) * 512], op=OP.add),
               r=[('ps', bank), ('x', i)], w=[('x', i)])
    dbg_point('F', x_sb[:, 3, :], [('x', 3)], ncols=D)

    def perm(ap, order):
        l = [list(x) for x in ap.ap]
        return bass.AP(ap.tensor, ap.offset, [l[i] for i in order])

    def srcG(i):
        return x_sb[:, i, :], ('x', i)
    rmsnorm_to_T(srcG, 8, hT, 'G')
    h2T = hT
    h2_keys = [('G', 'T', i) for i in range(NT)]
    for q4 in range(4):
        op('sp', lambda e, q4=q4: e.dma_start(
            out=xmid_d[q4 * 512:(q4 + 1) * 512, :].rearrange("(n p) m -> p n m", p=128),
            in_=x_sb[:, q4 * 4:(q4 + 1) * 4, :]),
           r=[('x', i) for i in range(q4 * 4, q4 * 4 + 4)], w=[('xmid', q4 * 2), ('xmid', q4 * 2 + 1)])
    dbg_point('G:h2T', hT[:, 3, :], h2_keys)
    ar.reset(mF)
    sch.fence()
    car = Arena(nc, 32768, ap=ar.ap, base=cat_off)
    qT_g = car.alloc([128, 16, 256], BF16)
    s12 = car.alloc([128, 16, 128], F32)
    cand = car.alloc([128, 8, 256], F32)
    ohtmp = car.alloc([128, 8, 16, 16], F32)
    G_sb = ar.alloc([128, 128, 256], BF16)
    kT_sb = ar.alloc([128, 16, 128], BF16)
    op('pq', lambda e: e.dma_start(out=kT_sb, in_=kT_d), w=['kT'])
    v16 = ar.alloc([128, 16, 16], F32)
    idx16 = ar.alloc([128, 16, 16], U32)
    idxf = ar.alloc([128, 16, 16], F32)
    vals = ar.alloc([128, 8, 16], F32)
    fidx = ar.alloc([128, 8, 16], U32)
    fif = ar.alloc([128, 8, 16], F32)
    r_f = ar.alloc([128, 8, 16], F32)
    c_f = ar.alloc([128, 8, 16], F32)
    e1f2 = [ar.alloc([128, 8, 16], F32) for _ in range(2)]
    e2f2 = [ar.alloc([128, 8, 16], F32) for _ in range(2)]
    exg2 = [ar.alloc([128, 8, 16], F32) for _ in range(2)]
    Zs = ar.alloc([128, 16], F32)
    E1T_b = ar.alloc([128, 128], BF16)
    E2T_b = ar.alloc([128, 128], BF16)
    gT_b = ar.alloc([128, 128], BF16)
    TB = 16
    OH1 = [ar.alloc([128, TB, 128], BF16) for _ in range(2)]
    OH2 = [ar.alloc([128, TB, 128], BF16) for _ in range(2)]
    uTb = [ar.alloc([128, 8, 128], BF16) for _ in range(3)]
    vb = [ar.alloc([128, D], BF16) for _ in range(3)]
    gab = [ar.alloc([128, 256], BF16) for _ in range(2)]
    Wtb = [ar.alloc([128, 256], BF16) for _ in range(2)]
    xg = ar.alloc([128, 2, D], F32)
    Gc = [ar.alloc([128, 256], BF16) for _ in range(3)]
    wq = [ar.alloc([128, 8, 128], BF16) for _ in range(4)]
    iota16 = iota_f[:, 0:16]
    iota_rep = junk.rearrange("p (a b) -> p a b", b=128)
    op('dve', lambda e: e.tensor_copy(out=iota_rep, in_=bc_mid(iota_b, 1, 16)), r=['iota_b'], w=['junk'])
    MAGIC2 = 12582912.0
    gctr = [0]
    wq_ptr = [0]
    v_ch = vbf_d.rearrange("(c e1) d -> c e1 d", e1=128)
    vbf_keys = [('vbf', j) for j in range(16)]

    def gbuild(g):
        gq = []

        def gop(eng, fn, r=(), w=(), cost=None):
            if cost is None:
                cost = {'dve': 0.6, 'pe': 0.15, 'act': 0.5}.get(eng, 0.0)
            gq.append((eng, fn, r, w, cost))
        def wq_load_upto(L):
            while wq_ptr[0] <= L and wq_ptr[0] < 8 * 16:
                Lq = wq_ptr[0]
                wq_ptr[0] += 1
                gop('pq', lambda e, Lq=Lq: e.dma_start(out=wq[Lq % 4], in_=wpqb_d[Lq % 16]), r=['wpqb'], w=[('wq', Lq % 4)], cost=0.0)

        def qT_stage(gg):
            qcols = slice(gg * 256, (gg + 1) * 256)
            wq_load_upto(gg * 16 + 3)
            for ch in range(16):
                wt = wq[ch % 4]
                bank = 6 + ch % 2
                for k in range(8):
                    gop('pe', lambda e, k=k, wt=wt, bank=bank: e.matmul(ps[bank][:, 0:256], wt[:, k, :], h2T[:, k, qcols],
                                                                        start=(k == 0), stop=(k == 7)),
                        r=[('wq', ch % 4)] + h2_keys[2 * gg:2 * gg + 2], w=[('ps', bank)])
                gop('act', lambda e, ch=ch, bank=bank: e.activation(out=qT_g[:, ch, :], in_=ps[bank][:, 0:256], func=AF.Copy),
                    r=[('ps', bank)], w=['qT_g'])
                wq_load_upto(gg * 16 + ch + 4)
        if g == 0:
            qT_stage(0)
        def stage_x(tt):
            tcol = slice(tt * 128, (tt + 1) * 128)
            e1f, e2f, exg = e1f2[tt], e2f2[tt], exg2[tt]
            ek = 'e1f%d' % tt
            e2k = 'e2f%d' % tt
            xk = 'exg%d' % tt
            for q in range(4):
                bank = 6 + q % 2
                for cq in range(4):
                    ch = q * 4 + cq
                    gop('pe', lambda e, ch=ch, cq=cq, bank=bank, tcol=tcol: e.matmul(ps[bank][:, cq * 128:(cq + 1) * 128],
                                                                         qT_g[:, ch, tcol], kT_sb[:, ch, :],
                                                                         start=True, stop=True),
                       r=['qT_g', 'kT'], w=[('ps', bank)])
                gop('act', lambda e, q=q, bank=bank: e.activation(
                    out=s12[:, q * 4:(q + 1) * 4, :], in_=ps[bank].rearrange("p (a b) -> p a b", b=128), func=AF.Copy),
                   r=[('ps', bank)], w=[('s12', q * 4 + i_) for i_ in range(4)])
            for ch in range(16):
                row = s12[:, ch, :]
                gop('dve', lambda e, ch=ch, row=row: e.max(out=v16[:, ch, 0:8], in_=row), r=[('s12', ch)], w=[('v16', ch)])
                gop('dve', lambda e, ch=ch, row=row: e.max_index(out=idx16[:, ch, 0:8], in_max=v16[:, ch, 0:8],
                                                                in_values=row),
                   r=[('s12', ch), ('v16', ch)], w=[('idx16', ch)])
                gop('dve', lambda e, ch=ch, row=row: e.match_replace(out=row, in_to_replace=v16[:, ch, 0:8],
                                                                    in_values=row, imm_value=NEG),
                   r=[('s12', ch), ('v16', ch)], w=[('s12', ch)])
                gop('dve', lambda e, ch=ch, row=row: e.max(out=v16[:, ch, 8:16], in_=row), r=[('s12', ch)], w=[('v16', ch)])
                gop('dve', lambda e, ch=ch, row=row: e.max_index(out=idx16[:, ch, 8:16], in_max=v16[:, ch, 8:16],
                                                                in_values=row),
                   r=[('s12', ch), ('v16', ch)], w=[('idx16', ch)])
            gop('dve', lambda e: e.tensor_copy(out=idxf, in_=idx16), r=[('idx16', c_) for c_ in range(16)], w=['idxf'])
            v4 = v16.rearrange("p (h two) k -> p h two k", two=2)
            i4 = idxf.rearrange("p (h two) k -> p h two k", two=2)
            cand4 = cand.rearrange("p h (i j) -> p h i j", j=16)
            gop('dve', lambda e: e.tensor_tensor(out=cand4, in0=bc_last(v4[:, :, 0, :], 16), in1=bc_mid(v4[:, :, 1, :], 2, 16),
                                                op=OP.add),
               r=[('v16', c_) for c_ in range(16)], w=[('cand', h_) for h_ in range(8)], cost=2.4)
            for h in range(8):
                row = cand[:, h, :]
                gop('dve', lambda e, h=h, row=row: e.max(out=vals[:, h, 0:8], in_=row), r=[('cand', h)], w=[('vals', h)])
                gop('dve', lambda e, h=h, row=row: e.max_index(out=fidx[:, h, 0:8], in_max=vals[:, h, 0:8], in_values=row),
                   r=[('cand', h), ('vals', h)], w=[('fidx', h)])
                gop('dve', lambda e, h=h, row=row: e.match_replace(out=row, in_to_replace=vals[:, h, 0:8], in_values=row,
                                                                  imm_value=NEG),
                   r=[('cand', h), ('vals', h)], w=[('cand', h)])
                gop('dve', lambda e, h=h, row=row: e.max(out=vals[:, h, 8:16], in_=row), r=[('cand', h)], w=[('vals', h)])
                gop('dve', lambda e, h=h, row=row: e.max_index(out=fidx[:, h, 8:16], in_max=vals[:, h, 8:16],
                                                              in_values=row),
                   r=[('cand', h), ('vals', h)], w=[('fidx', h)])
            gop('dve', lambda e: e.tensor_copy(out=fif, in_=fidx), r=[('fidx', h_) for h_ in range(8)], w=['fif'])
            gop('dve', lambda e: e.tensor_scalar(out=r_f, in0=fif, scalar1=0.0625, scalar2=-0.46875, op0=OP.mult, op1=OP.add),
               r=['fif'], w=['r_f'])
            gop('dve', lambda e: e.tensor_scalar(out=r_f, in0=r_f, scalar1=MAGIC2, scalar2=None, op0=OP.add),
               r=['r_f'], w=['r_f'])
            gop('dve', lambda e: e.tensor_scalar(out=r_f, in0=r_f, scalar1=-MAGIC2, scalar2=None, op0=OP.add),
               r=['r_f'], w=['r_f'])
            gop('dve', lambda e: e.scalar_tensor_tensor(out=c_f, in0=r_f, scalar=-16.0, in1=fif, op0=OP.mult, op1=OP.add),
               r=['r_f', 'fif'], w=['c_f'])
            for (sel, two, dst, dkey) in ((r_f, 0, e1f, ek), (c_f, 1, e2f, e2k)):
                gop('dve', lambda e, sel=sel: e.tensor_tensor(
                    out=ohtmp, in0=bc_mid(bc_mid(iota16, 1, 16), 1, 8), in1=bc_last(sel, 16), op=OP.is_equal),
                   r=['r_f', 'c_f', 'iota_f'], w=['ohtmp'], cost=2.4)
                gop('dve', lambda e, two=two: e.tensor_tensor(out=ohtmp, in0=ohtmp, in1=bc_mid(i4[:, :, two, :], 2, 16),
                                                             op=OP.mult),
                   r=['ohtmp', 'idxf'], w=['ohtmp'], cost=2.4)
                gop('dve', lambda e, dst=dst: e.tensor_reduce(out=dst, in_=ohtmp, axis=AX.X, op=OP.add),
                   r=['ohtmp'], w=[dkey], cost=2.4)
            gop('dve', lambda e: e.tensor_tensor(out=exg, in0=vals, in1=bc_last(vals[:, :, 0], 16), op=OP.subtract),
               r=[('vals', h_) for h_ in range(8)], w=[xk])
            gop('act', lambda e: e.activation(out=exg, in_=exg, func=AF.Exp), r=[xk], w=[xk])
            gop('dve', lambda e: e.tensor_reduce(out=Zs[:, 0:8], in_=exg, axis=AX.X, op=OP.add), r=[xk], w=['Zs'])
            gop('dve', lambda e: e.reciprocal(out=Zs[:, 8:16], in_=Zs[:, 0:8]), r=['Zs'], w=['Zs'])
            gop('dve', lambda e: e.tensor_tensor(out=exg, in0=exg, in1=bc_last(Zs[:, 8:16], 16), op=OP.mult),
               r=[xk, 'Zs'], w=[xk])
        def stage_y(tt):
            e1f, e2f, exg = e1f2[tt], e2f2[tt], exg2[tt]
            for ti, (src, skey, dst, dkey) in enumerate(((e1f, 'e1f%d' % tt, E1T_b, 'E1T'), (e2f, 'e2f%d' % tt, E2T_b, 'E2T'),
                                                          (exg, 'exg%d' % tt, gT_b, 'gT'))):
                bank = 6 + ti % 2
                gop('pe', lambda e, src=src, bank=bank: e.transpose(out=ps[bank][:, 0:128],
                                                                   in_=src.rearrange("p h k -> p (h k)"),
                                                                   identity=ident_f),
                   r=[skey, 'ident_f'], w=[('ps', bank)])
                gop('act', lambda e, dst=dst, bank=bank: e.activation(out=dst, in_=ps[bank][:, 0:128], func=AF.Copy),
                   r=[('ps', bank)], w=[dkey])
            ohs = {}

            def oh_make(tb):
                    ts = slice(tb * TB, (tb + 1) * TB)
                    sl = gctr[0] % 2
                    gctr[0] += 1
                    o1, o2 = OH1[sl], OH2[sl]
                    ohs[tb] = (sl, o1, o2)
                    gop('dve', lambda e, o2=o2, ts=ts: e.tensor_tensor(out=o2, in0=iota_rep,
                                                                      in1=bc_last(E2T_b[:, ts], 128), op=OP.is_equal),
                       r=['junk', 'E2T'], w=[('OH2', sl)], cost=1.4)
                    gop('dve', lambda e, o1=o1, ts=ts: e.tensor_tensor(out=o1, in0=iota_rep,
                                                                      in1=bc_last(E1T_b[:, ts], 128), op=OP.is_equal),
                       r=['junk', 'E1T'], w=[('OH1', sl)], cost=1.4)
                    gop('dve', lambda e, o1=o1, ts=ts: e.tensor_tensor(out=o1, in0=o1, in1=bc_last(gT_b[:, ts], 128),
                                                                       op=OP.mult),
                       r=[('OH1', sl), 'gT'], w=[('OH1', sl)], cost=1.4)

            def oh_use(tb):
                    sl, o1, o2 = ohs[tb]
                    for q in range(TB // 4):
                        bank = 6 + (tb * (TB // 4) + q) % 2
                        for t4 in range(4):
                            t = q * 4 + t4
                            gop('pe', lambda e, t=t, t4=t4, bank=bank, o1=o1, o2=o2: e.matmul(
                                ps[bank][:, t4 * 128:(t4 + 1) * 128], o1[:, t, :], o2[:, t, :],
                                start=True, stop=True),
                               r=[('OH1', sl), ('OH2', sl)], w=[('ps', bank)])
                        t0 = tt * 128 + tb * TB + q * 4
                        dstp = G_sb[:, :, t0:t0 + 4]
                        eng = 'act'
                        if eng == 'act':
                            gop('act', lambda e, bank=bank, dstp=dstp: e.activation(
                                out=dstp, in_=ps[bank].rearrange("p (t e) -> p e t", t=4), func=AF.Copy),
                               r=[('ps', bank)], w=['G_sb'])
                        else:
                            gop('dve', lambda e, bank=bank, dstp=dstp: e.tensor_copy(
                                out=dstp, in_=ps[bank].rearrange("p (a b) -> p a b", b=128)),
                               r=[('ps', bank)], w=['G_sb'])

            oh_make(0)
            for tb in range(128 // TB):
                if tb + 1 < 128 // TB:
                    oh_make(tb + 1)
                oh_use(tb)

        stage_x(0)
        stage_x(1)
        if g < 7:
            qT_stage(g + 1)
        stage_y(0)
        stage_y(1)
        for q4 in range(4):
            gop('pq', lambda e, q4=q4: e.dma_start(out=gd_d[g][:, q4 * 32:(q4 + 1) * 32, :],
                                                   in_=G_sb[:, q4 * 32:(q4 + 1) * 32, :]),
                r=['G_sb'], w=[('gd', g, q4)], cost=0.0)
        return gq

    def peer_main(g, pending):
        gcols = slice(g * 256, (g + 1) * 256)
        def chunk_front(c):
            sl3 = c % 3
            op('sp', lambda e: e.dma_start(out=uTb[sl3], in_=ub_d[c]), r=[('ub', c // 8)], w=[('uTb', sl3)])
            op('sp', lambda e: e.dma_start(out=vb[sl3], in_=v_ch[c]), r=vbf_keys, w=[('vb', sl3)])
            op('sp', lambda e: e.dma_start(out=Gc[sl3], in_=gd_d[g][:, c, :]), r=[('gd', g, c // 32)], w=[('Gc', sl3)])
            bank = 4 + c % 2
            for k in range(8):
                op('pe', lambda e, k=k: e.matmul(ps[bank][:, 0:256], uTb[sl3][:, k, :], h2T[:, k, gcols],
                                                 start=(k == 0), stop=(k == 7)),
                   r=[('uTb', sl3)] + h2_keys[2 * g:2 * g + 2], w=[('ps', bank)])
            s2 = c % 2
            op('act', lambda e: e.activation(out=gab[s2], in_=ps[bank][:, 0:256], func=AF.Gelu),
               r=[('ps', bank)], w=[('ga', s2)])
            op('pool', lambda e: e.tensor_tensor(out=Wtb[s2], in0=gab[s2], in1=Gc[sl3], op=OP.mult),
               r=[('ga', s2), ('Gc', sl3)], w=[('Wt', s2)])

        def chunk_back(c):
            sl3 = c % 3
            s2 = c % 2
            for t2 in range(2):
                for hf in range(2):
                    ob = t2 * 2 + hf
                    op('pe', lambda e, t2=t2, hf=hf, ob=ob: e.matmul(
                        ps[ob], Wtb[s2][:, t2 * 128:(t2 + 1) * 128], vb[sl3][:, hf * 512:(hf + 1) * 512],
                        start=(c == 0), stop=(c == 127)),
                       r=[('Wt', s2), ('vb', sl3)], w=[('ps', ob)])

        total_cost = sum(t_[4] for t_ in pending)
        per = total_cost / 127.0
        released = 0.0
        for c in range(129):
            if c == 64:
                op('sp', lambda e: e.dma_start(out=xg, in_=xmid_d[g * 256:(g + 1) * 256, :].rearrange("(n p) m -> p n m", p=128)),
                   r=[('xmid', g)], w=['xg'])
            if c < 128:
                chunk_front(c)
            if c >= 1:
                chunk_back(c - 1)
            while pending and released < per * (c + 1):
                t_ = pending.pop(0)
                op(*t_[:4])
                released += t_[4]
        while pending:
            t_ = pending.pop(0)
            op(*t_[:4])
        for t2 in range(2):
            for hf in range(2):
                ob = t2 * 2 + hf
                op('dve', lambda e, t2=t2, hf=hf, ob=ob: e.tensor_tensor(
                    out=xg[:, t2, hf * 512:(hf + 1) * 512], in0=ps[ob], in1=xg[:, t2, hf * 512:(hf + 1) * 512], op=OP.add),
                   r=[('ps', ob), 'xg'], w=['xg'])
        op('sp', lambda e: e.dma_start(out=xmid_d[g * 256:(g + 1) * 256, :].rearrange("(n p) m -> p n m", p=128), in_=xg),
           r=['xg'], w=[('xmid', g)])

    for t_ in gbuild(0):
        op(*t_[:4])
    for g in range(8):
        peer_main(g, gbuild(g + 1) if g < 7 else [])

    ar.reset(mF)
    sch.fence()
    x_sb = ar.alloc([128, NT, D], F32)
    wbig = ar.alloc([128, 8, 1024], BF16)
    for q4 in range(4):
        op('sp', lambda e, q4=q4: e.dma_start(out=x_sb[:, q4 * 4:(q4 + 1) * 4, :],
                                             in_=xmid_d[q4 * 512:(q4 + 1) * 512, :].rearrange("(n p) m -> p n m", p=128)),
           r=[('xmid', q4 * 2), ('xmid', q4 * 2 + 1)], w=[('x', i) for i in range(q4 * 4, q4 * 4 + 4)])
    dbg_point('G:x2', x_sb[:, 3, :], [('x', 3)], ncols=D)

    sch.fence()

    def srcH(i):
        return x_sb[:, i, :], ('x', i)
    xT = hT
    xbf = [ar.alloc([128, D], BF16) for _ in range(2)]
    for i in range(NT):
        xb2 = xbf[i % 2]
        op('act', lambda e, i=i, xb2=xb2: e.activation(out=xb2, in_=x_sb[:, i, :], func=AF.Copy),
           r=[('x', i)], w=[('xbf', i % 2)])
        bank = next_bank()
        for k in range(8):
            op('pe', lambda e, k=k, xb2=xb2, bank=bank: e.transpose(out=psb[bank][:, k * 128:(k + 1) * 128],
                                                                     in_=xb2[:, k * 128:(k + 1) * 128], identity=ident),
               r=[('xbf', i % 2), 'ident'], w=[('ps', bank)])
        op('act', lambda e, i=i, bank=bank: e.activation(out=xT[:, :, i * 128:(i + 1) * 128],
                                                         in_=psb[bank].rearrange("p (k t) -> p k t", k=8), func=AF.Copy),
           r=[('ps', bank)], w=[('xT', i)])
    wple = ar.alloc([128, 2, 1024], BF16)
    pT_sb = ar.alloc([128, 2, S], BF16)
    gfin_bc = ar.alloc([128, D], F32)
    sgt = [ar.alloc([128, 512], F32) for _ in range(2)]
    outt = [ar.alloc([128, D], F32) for _ in range(2)]
    op('pq', lambda e: e.dma_start(out=wbig, in_=wpg_d), w=['wbig'])
    op('pq', lambda e: e.dma_start(out=wple, in_=wple_d), w=['wple'])
    op('pq', lambda e: e.dma_start(out=pT_sb, in_=pT_d.rearrange("(k p) s -> p k s", p=128)), w=['pT'])
    op('sp', lambda e: e.dma_start(out=gfin_bc, in_=bass.AP(gfin_d.tensor, 0, [[0, 128], [1, D]])), w=['gfin'])
    for i in range(NT):
        for hf in range(2):
            bG = next_bank()
            for k in range(8):
                op('pe', lambda e, i=i, hf=hf, k=k, bG=bG: e.matmul(
                    ps[bG], xT[:, k, i * 128:(i + 1) * 128], wbig[:, k, hf * 512:(hf + 1) * 512],
                    start=(k == 0), stop=(k == 7)),
                   r=['wbig', ('xT', i)], w=[('ps', bG)])
            bP = next_bank()
            for k in range(2):
                op('pe', lambda e, i=i, hf=hf, k=k, bP=bP: e.matmul(
                    ps[bP], pT_sb[:, k, i * 128:(i + 1) * 128], wple[:, k, hf * 512:(hf + 1) * 512],
                    start=(k == 0), stop=(k == 1)),
                   r=['wple', 'pT'], w=[('ps', bP)])
            sg_ = sgt[hf]
            op('act', lambda e, bG=bG, sg_=sg_: e.activation(out=sg_, in_=ps[bG], func=AF.Sigmoid),
               r=[('ps', bG)], w=[('sgt', hf)])
            op('dve', lambda e, bP=bP, sg_=sg_: e.tensor_tensor(out=sg_, in0=sg_, in1=ps[bP], op=OP.mult),
               r=[('ps', bP), ('sgt', hf)], w=[('sgt', hf)])
            op('dve', lambda e, i=i, hf=hf, sg_=sg_: e.tensor_tensor(
                out=x_sb[:, i, hf * 512:(hf + 1) * 512], in0=x_sb[:, i, hf * 512:(hf + 1) * 512], in1=sg_, op=OP.add),
               r=[('sgt', hf), ('x', i)], w=[('x', i)])
        ss = stat[:, i:i + 1]
        rs = stat[:, 16 + i:17 + i]
        op('act', lambda e, i=i, ss=ss: e.activation(out=junk[:, 0:D], in_=x_sb[:, i, :], func=AF.Square, accum_out=ss),
           r=[('x', i)], w=['junk', ('Hss', i)])
        op('act', lambda e, ss=ss, rs=rs: e.activation(out=rs, in_=ss, func=AF.Sqrt, scale=1.0 / D, bias=epsc),
           r=[('Hss', i), 'epsc'], w=[('Hrs', i)])
        op('dve', lambda e, rs=rs: e.reciprocal(out=rs, in_=rs), r=[('Hrs', i)], w=[('Hrs', i)])
        ot = outt[i % 2]
        op('dve', lambda e, i=i, rs=rs, ot=ot: e.scalar_tensor_tensor(out=ot, in0=x_sb[:, i, :], scalar=rs, in1=gfin_bc,
                                                                     op0=OP.mult, op1=OP.mult),
           r=[('x', i), ('Hrs', i), 'gfin'], w=[('outt', i % 2)])
        op('sp', lambda e, i=i, ot=ot: e.dma_start(out=out_d[i * 128:(i + 1) * 128, :], in_=ot),
           r=[('outt', i % 2)], w=[('out', i)])
    with nc.Block() as block:
        sch.emit(block)
    return nc


def _host_consts():
    ident = np.eye(128, dtype=np.float32)
    t = np.arange(128)
    cmask = np.where(t[None, :] <= t[:, None], 0.0, NEG).astype(np.float32)
    iota = np.broadcast_to(np.arange(128, dtype=np.float32)[None, :], (128, 128)).copy()
    ropec = np.zeros((128, 4), np.float32)
    inv_a = ROPE_THETA ** (-np.arange(0, 32, 2, dtype=np.float32) / 32)
    inv_i = ROPE_THETA ** (-np.arange(0, 16, 2, dtype=np.float32) / 16)
    for p in range(32):
        ropec[p, 0] = inv_a[p % 16] / (2 * np.pi)
        ropec[p, 1] = -1.0 if p < 16 else 1.0
    for base in (0, 64):
        for p in range(16):
            ropec[base + p, 2] = inv_i[p % 8] / (2 * np.pi)
            ropec[base + p, 3] = -1.0 if p < 8 else 1.0
    return ident, cmask, iota, ropec


def _prep_shared(inp):
    w_in = np.asarray(inp['w_in'][0], np.float32)
    cols = np.zeros((NCH, 128), np.int64)
    qo, kro, iqo, iko, xbo, gto = 0, 640, 672, 1184, 1256, 1768
    for h in range(4):
        cols[CH_Q[h]] = qo + 128 * h + np.arange(128)
        pc = np.arange(128)
        pc[:16] = np.arange(16, 32)
        pc[16:32] = np.arange(0, 16)
        cols[CH_QP[h]] = qo + 128 * h + pc
    kc = np.arange(128) % 32
    cols[CH_K] = kro + kc
    kp = kc.copy()
    kp[kc < 16] = kc[kc < 16] + 16
    kp[kc >= 16] = kc[kc >= 16] - 16
    cols[CH_KP] = kro + kp
    for m in range(4):
        cols[CH_IQ[m]] = iqo + 128 * m + np.arange(128)
        pc = np.arange(128)
        for base in (0, 64):
            pc[base:base + 8] = base + np.arange(8, 16)
            pc[base + 8:base + 16] = base + np.arange(0, 8)
        cols[CH_IQP[m]] = iqo + 128 * m + pc
    ic = np.arange(128) % 64
    cols[CH_IK] = iko + ic
    ip = ic.copy()
    ip[ic < 8] = ic[ic < 8] + 8
    ip[(ic >= 8) & (ic < 16)] = ic[(ic >= 8) & (ic < 16)] - 8
    cols[CH_IKP] = iko + ip
    for c in range(4):
        cols[CH_XB[c]] = xbo + 128 * c + np.arange(128)
        cols[CH_GT[c]] = gto + 128 * c + np.arange(128)
    wg = w_in[:, cols.reshape(-1)].reshape(8, 128, NCH, 128)
    wfm = np.ascontiguousarray(wg.transpose(2, 1, 0, 3))
    tmc = np.concatenate([np.arange(512, 640), np.arange(1248, 1256)])
    wtm = np.ascontiguousarray(w_in[:, tmc].reshape(8, 128, 136).transpose(1, 0, 2))
    gvec = np.zeros((128, 24), np.float32)
    gvec[:, 0:8] = np.asarray(inp['g_mix'][0]).reshape(8, 128).T
    gvec[:, 8:16] = np.asarray(inp['g_ffn'][0]).reshape(8, 128).T
    ident, cmask, iota, ropec = _host_consts()
    w_uk = np.asarray(inp['w_uk'][0], np.float32)
    wukT = np.zeros((128, 4, 128), np.float32)
    wukT[32:128] = w_uk.transpose(2, 0, 1)
    wuv = np.ascontiguousarray(np.asarray(inp['w_uv'][0], np.float32).transpose(1, 0, 2))
    lruv = np.zeros((128, 36), np.float32)
    cw = np.asarray(inp['conv_w'][0], np.float32)
    lruv[:, 0:16] = cw.reshape(4, 4, 128).transpose(2, 1, 0).reshape(128, 16)
    lruv[:, 16:20] = np.asarray(inp['conv_b'][0]).reshape(4, 128).T
    lruv[:, 20:24] = np.asarray(inp['b_rg'][0]).reshape(4, 128).T
    lruv[:, 24:28] = np.asarray(inp['b_ig'][0]).reshape(4, 128).T
    lruv[:, 28:32] = np.asarray(inp['lru_lambda'][0]).reshape(4, 128).T
    wbd = np.zeros((128, 8, 128), np.float32)
    for gi, nm in enumerate(['w_rg', 'w_ig']):
        wsrc = np.asarray(inp[nm][0], np.float32)
        for c in range(4):
            wbd[0:64, gi * 4 + c, 0:64] = wsrc[2 * c]
            wbd[64:128, gi * 4 + c, 64:128] = wsrc[2 * c + 1]
    wout = np.ascontiguousarray(np.asarray(inp['w_out'][0], np.float32).reshape(8, 128, 1024).transpose(1, 0, 2))
    wpq = np.ascontiguousarray(
        np.asarray(inp['w_pq'][0], np.float32).reshape(8, 128, 16, 128).transpose(2, 1, 0, 3))
    k1 = np.asarray(inp['peer_k1'][0], np.float32)
    k2 = np.asarray(inp['peer_k2'][0], np.float32)
    kT = np.zeros((128, 16, 128), np.float32)
    for h in range(8):
        kT[:, 2 * h, :] = k1[h].T
        kT[:, 2 * h + 1, :] = k2[h].T
    u = np.asarray(inp['peer_u'][0], np.float32)
    uT = np.ascontiguousarray(u.reshape(128, 128, 8, 128).transpose(1, 3, 2, 0))
    v = np.ascontiguousarray(np.asarray(inp['peer_v'][0], np.float32))
    wple = np.ascontiguousarray(np.asarray(inp['w_ple'][0], np.float32).reshape(2, 128, 1024).transpose(1, 0, 2))
    wpg = np.ascontiguousarray(
        np.asarray(inp['w_ple_gate'][0], np.float32).reshape(8, 128, 1024).transpose(1, 0, 2))
    return dict(wfm=wfm, wtm=wtm, gvec=gvec, ropec=ropec, ident=ident, cmask=cmask, iota=iota,
                gkv=np.asarray(inp['g_kv'], np.float32).reshape(1, 128), wukT=wukT, wuv=wuv, lruv=lruv, wbd=wbd,
                wout=wout, wpq=wpq, kT=kT, uT=uT, v=v, wple=wple, wpg=wpg,
                gfin=np.asarray(inp['g_final'], np.float32).reshape(1, D))


def make_in_maps(inp, ncores=NCORES):
    shared = _prep_shared(inp)
    maps = []
    for b in range(ncores):
        m = dict(shared)
        m['x'] = np.ascontiguousarray(np.asarray(inp['x'][b], np.float32))
        m['pT'] = np.ascontiguousarray(np.asarray(inp['p'][0, b], np.float32).T)
        m['pos'] = np.ascontiguousarray(np.asarray(inp['positions'][b], np.int32).reshape(1, S))
        maps.append(m)
    return maps


def kernel(**inputs):
    nc = build()
    maps = make_in_maps(inputs)
    res = run_bass_kernel_spmd(nc, maps, core_ids=list(range(NCORES)))
    return np.stack([np.asarray(r["out"], np.float32) for r in res.results], axis=0)
```

```python
import math
import numpy as np
import concourse.bass as bass
import concourse.mybir as mybir
from concourse.bass_utils import run_bass_kernel_spmd

F32 = mybir.dt.float32
BF16 = mybir.dt.bfloat16
I32 = mybir.dt.int32
U32 = mybir.dt.uint32
U8 = mybir.dt.uint8
AF = mybir.ActivationFunctionType
OP = mybir.AluOpType
AX = mybir.AxisListType

S = 2048
D = 1024
NT = 16
EPS = 1e-6
NEG = -1.0e30
NCORES = 8
ROPE_THETA = 500000.0
ATT_SCALE = 128 ** -0.5
IW_SCALE = (8 ** -0.5) * (64 ** -0.5)

CH_Q = [0, 1, 2, 3]
CH_QP = [4, 5, 6, 7]
CH_K, CH_KP = 8, 9
CH_IQ = [10, 11, 12, 13]
CH_IQP = [14, 15, 16, 17]
CH_IK, CH_IKP = 18, 19
CH_XB = [20, 21, 22, 23]
CH_GT = [24, 25, 26, 27]
NCH = 28


def _esz(dt):
    return 2 if dt == BF16 else (1 if dt == U8 else 4)


class Sched:
    PHYS = {'pe': 'pe', 'act': 'act', 'dve': 'dve', 'pool': 'pool', 'pq': 'pool', 'sp': 'sp'}
    UNIT = {'pe': 1, 'act': 1, 'dve': 1, 'pool': 1, 'pq': 16, 'sp': 16}
    EPOCH = {'pe': 30000, 'act': 30000, 'dve': 30000, 'pool': 30000}
    DMA = ('pq', 'sp')
    NSLOT = 16

    def __init__(self, nc):
        self.nc = nc
        self.streams = {'pe': [], 'act': [], 'dve': [], 'pool': [], 'sp': []}
        self.cnt = {e: 0 for e in self.PHYS}
        self.wr = {}
        self.rd = {}
        self.seen = {p: {} for p in self.streams}
        self.seen_d = {p: {e: set() for e in self.DMA} for p in self.streams}
        self.sems = {}
        self.fence_cnt = {}

    def fence(self):
        self.fence_cnt = dict(self.cnt)

    def sem(self, eng, c):
        if eng in self.DMA:
            slot = (c - 1) % self.NSLOT
            key = (eng, 'slot', slot)
            val = ((c - 1) // self.NSLOT + 1) * 16
        else:
            c = self.rank[eng][c]
            ep = (c - 1) // self.EPOCH[eng]
            key = (eng, ep)
            val = c - ep * self.EPOCH[eng]
        if key not in self.sems:
            self.sems[key] = self.nc.alloc_semaphore("s_" + "_".join(str(x) for x in key))
        return self.sems[key], val

    def op(self, eng, fn, r=(), w=()):
        phys = self.PHYS[eng]
        waits_c = {}
        waits_d = set()

        def need(ec):
            e, c = ec
            if e in self.DMA:
                waits_d.add((e, c))
            elif c > waits_c.get(e, 0):
                waits_c[e] = c
        for e, c in self.fence_cnt.items():
            if c <= 0:
                continue
            if e in self.DMA:
                for cc in range(max(1, c - self.NSLOT + 1), c + 1):
                    need((e, cc))
            else:
                need((e, c))
        for k in r:
            if k in self.wr:
                need(self.wr[k])
        for k in w:
            if k in self.wr:
                need(self.wr[k])
            for e, c in self.rd.get(k, {}).items():
                need((e, c))
        self.cnt[eng] += 1
        me = self.cnt[eng]
        if eng in self.DMA and me > self.NSLOT:
            waits_d.add((eng, me - self.NSLOT))
        final = []
        for e, c in waits_c.items():
            if e == 'pe' and eng == 'pe':
                continue
            if self.seen[phys].get(e, 0) >= c:
                continue
            self.seen[phys][e] = c
            final.append((e, c))
        for (e, c) in sorted(waits_d):
            if c in self.seen_d[phys][e]:
                continue
            self.seen_d[phys][e].add(c)
            final.append((e, c))
        self.streams[phys].append((eng, me, fn, final))
        for k in r:
            self.rd.setdefault(k, {})[eng] = me
        for k in w:
            self.wr[k] = (eng, me)
            self.rd[k] = {}

    def emit(self, block):
        final_waits = []
        for e, c in self.cnt.items():
            if c <= 0:
                continue
            if e in self.DMA:
                final_waits += [(e, cc) for cc in range(max(1, c - self.NSLOT + 1), c + 1)]
            else:
                final_waits.append((e, c))
        targets = {e: set() for e in self.PHYS if e not in self.DMA}
        for phys in self.streams:
            for (leng, me, fn, waits) in self.streams[phys]:
                for (e, c) in waits:
                    if e not in self.DMA:
                        targets[e].add(c)
        for (e, c) in final_waits:
            if e not in self.DMA:
                targets[e].add(c)
        self.rank = {e: {c: i + 1 for i, c in enumerate(sorted(t))} for e, t in targets.items()}

        def body_for(phys):
            def body(eng):
                for (leng, me, fn, waits) in self.streams[phys]:
                    for (e, c) in waits:
                        sm, v = self.sem(e, c)
                        eng.wait_ge(sm, v)
                    ins = fn(eng)
                    if leng in self.DMA:
                        sm, v = self.sem(leng, me)
                        ins.then_inc(sm, 16)
                    elif me in targets[leng]:
                        sm, v = self.sem(leng, me)
                        ins.then_inc(sm, 1)
                if phys == 'sp':
                    for (e, c) in final_waits:
                        sm, v = self.sem(e, c)
                        eng.wait_ge(sm, v)
            return body
        block.tensor(body_for('pe'))
        block.scalar(body_for('act'))
        block.vector(body_for('dve'))
        block.gpsimd(body_for('pool'))
        block.sync(body_for('sp'))


class Arena:
    def __init__(self, nc, nbytes, ap=None, base=0):
        self.ap = nc.alloc_sbuf_tensor("arena", [128, nbytes], U8).ap() if ap is None else ap
        self.n = base + nbytes
        self.off = base
        self.peak = 0

    def alloc(self, shape, dt):
        n = 1
        for d in shape[1:]:
            n *= d
        b = n * _esz(dt)
        b32 = (b + 63) // 64 * 64
        assert self.off + b32 <= self.n, f"arena overflow {self.off}+{b32}>{self.n}"
        a = self.ap[:, self.off:self.off + b].bitcast(dt)
        self.off += b32
        self.peak = max(self.peak, self.off)
        if len(shape) == 3:
            a = a.rearrange("p (a b) -> p a b", b=shape[2])
        elif len(shape) == 4:
            a = a.rearrange("p (a b c) -> p a b c", b=shape[2], c=shape[3])
        if shape[0] < 128:
            a = a[0:shape[0]]
        return a

    def mark(self):
        return self.off

    def reset(self, m):
        self.off = m


def bc_last(ap, n):
    return bass.AP(ap.tensor, ap.offset, [list(x) for x in ap.ap] + [[0, n]])


def bc_mid(ap, axis, n):
    l = [list(x) for x in ap.ap]
    return bass.AP(ap.tensor, ap.offset, l[:axis] + [[0, n]] + l[axis:])


def build(stop='all', dbg=None):
    try:
        return _build(stop, dbg)
    except _StopBuild as ex:
        return ex.nc


class _StopBuild(Exception):
    def __init__(self, nc):
        self.nc = nc


def _build(stop='all', dbg=None):
    nc = bass.Bass("TRN2", target_bir_lowering=False)
    dt_in = lambda name, shape, dt=F32: nc.dram_tensor(name, shape, dt, kind="ExternalInput").ap()
    x_d = dt_in("x", [S, D])
    pT_d = dt_in("pT", [256, S])
    pos_d = dt_in("pos", [1, S], I32)
    wfm_d = dt_in("wfm", [NCH, 128, 8, 128])
    wtm_d = dt_in("wtm", [128, 8, 136])
    gvec_d = dt_in("gvec", [128, 24])
    ropec_d = dt_in("ropec", [128, 4])
    ident_d = dt_in("ident", [128, 128])
    cmask_d = dt_in("cmask", [128, 128])
    iota_d = dt_in("iota", [128, 128])
    gkv_d = dt_in("gkv", [1, 128])
    wukT_d = dt_in("wukT", [128, 4, 128])
    wuv_d = dt_in("wuv", [128, 4, 128])
    lruv_d = dt_in("lruv", [128, 36])
    wbd_d = dt_in("wbd", [128, 8, 128])
    wout_d = dt_in("wout", [128, 8, 1024])
    wpq_d = dt_in("wpq", [16, 128, 8, 128])
    kT_d = dt_in("kT", [128, 16, 128])
    uT_d = dt_in("uT", [128, 128, 8, 128])
    v_d = dt_in("v", [16384, D])
    wple_d = dt_in("wple", [128, 2, 1024])
    wpg_d = dt_in("wpg", [128, 8, 1024])
    gfin_d = dt_in("gfin", [1, D])
    out_d = nc.dram_tensor("out", [S, D], F32, kind="ExternalOutput").ap()
    dbg_d = None
    if dbg is not None:
        dbg_d = nc.dram_tensor("dbg", list(dbg), F32, kind="ExternalOutput").ap()
    xmid_d = nc.dram_tensor("xmid", [S, D], F32).ap()
    gd_d = nc.dram_tensor("gd", [8, 128, 128, 256], BF16).ap()

    ub_d = nc.dram_tensor("ub16", [128, 128, 8, 128], BF16).ap()
    vbf_d = nc.dram_tensor("vb16", [16384, D], BF16).ap()
    wpqb_d = nc.dram_tensor("wpq16", [16, 128, 8, 128], BF16).ap()
    sch = Sched(nc)
    ar = Arena(nc, 204 * 1024)
    ps = [nc.alloc_psum_tensor(f"ps{i}", [128, 512], F32).ap() for i in range(8)]
    psb = [p.bitcast(BF16) for p in ps]
    op = sch.op

    ident_f = ar.alloc([128, 128], F32)
    ident = ar.alloc([128, 128], BF16)
    cmask = ar.alloc([128, 128], F32)
    iota_f = ar.alloc([128, 128], F32)
    iota_b = ar.alloc([128, 128], BF16)
    gvec = ar.alloc([128, 24], F32)
    ropec = ar.alloc([128, 4], F32)
    lruv = ar.alloc([128, 36], F32)
    op('sp', lambda e: e.dma_start(out=ident_f, in_=ident_d), w=['ident_f'])
    op('sp', lambda e: e.dma_start(out=cmask, in_=cmask_d), w=['cmask'])
    op('sp', lambda e: e.dma_start(out=iota_f, in_=iota_d), w=['iota_f'])
    op('sp', lambda e: e.dma_start(out=gvec, in_=gvec_d), w=['gvec'])
    op('sp', lambda e: e.dma_start(out=ropec, in_=ropec_d), w=['ropec'])
    op('sp', lambda e: e.dma_start(out=lruv, in_=lruv_d), w=['lruv'])
    op('dve', lambda e: e.tensor_copy(out=ident, in_=ident_f), r=['ident_f'], w=['ident'])
    op('dve', lambda e: e.tensor_copy(out=iota_b, in_=iota_f), r=['iota_f'], w=['iota_b'])

    junk = ar.alloc([128, 2048], BF16)
    epsc = ar.alloc([128, 1], F32)
    op('dve', lambda e: e.memset(epsc, EPS), w=['epsc'])
    onec = ar.alloc([128, 1], F32)
    op('dve', lambda e: e.memset(onec, 1.0), w=['onec'])
    negb = ar.alloc([128, 1], F32)
    op('dve', lambda e: e.memset(negb, -3.141589), w=['negb'])
    stat = ar.alloc([128, 64], F32)
    h_off = ar.off
    hT = ar.alloc([128, 8, S], BF16)
    cat_off = ar.off
    catT = ar.alloc([128, 8, S], BF16)
    cat_alias = [ar.ap[:, cat_off + i * 8192:cat_off + (i + 1) * 8192].bitcast(F32) for i in range(2)]

    def rmsnorm_to_T(src_tile_fn, gcol, dstT, tag):
        m = ar.mark()
        xn = [ar.alloc([128, D], BF16) for _ in range(2)]
        for i in range(NT):
            xt, xkey = src_tile_fn(i)
            ss = stat[:, i:i + 1]
            rs = stat[:, 16 + i:17 + i]
            op('act', lambda e, xt=xt, ss=ss: e.activation(out=junk[:, 0:D], in_=xt, func=AF.Square, accum_out=ss),
               r=[xkey], w=['junk', (tag, 'ss', i)])
            op('act', lambda e, ss=ss, rs=rs: e.activation(out=rs, in_=ss, func=AF.Sqrt, scale=1.0 / D, bias=epsc),
               r=[(tag, 'ss', i), 'epsc'], w=[(tag, 'rs', i)])
            op('dve', lambda e, rs=rs: e.reciprocal(out=rs, in_=rs),
               r=[(tag, 'rs', i)], w=[(tag, 'rs', i)])
            xb = xn[i % 2]
            op('dve', lambda e, xb=xb, xt=xt, rs=rs: e.tensor_scalar(out=xb, in0=xt, scalar1=rs, scalar2=None,
                                                                      op0=OP.mult),
               r=[xkey, (tag, 'rs', i)], w=[(tag, 'xn', i % 2)])
            bank = i % 2
            for k in range(8):
                op('pe', lambda e, k=k, xb=xb, bank=bank: e.transpose(out=psb[bank][:, k * 128:(k + 1) * 128],
                                                                       in_=xb[:, k * 128:(k + 1) * 128],
                                                                       identity=ident),
                   r=[(tag, 'xn', i % 2), 'ident'], w=[('ps', bank)])
            g3 = bc_last(gvec[:, gcol:gcol + 8], 128)
            op('dve', lambda e, i=i, bank=bank, g3=g3: e.tensor_tensor(
                out=dstT[:, :, i * 128:(i + 1) * 128],
                in0=psb[bank].rearrange("p (k t) -> p k t", k=8), in1=g3, op=OP.mult),
               r=[('ps', bank), 'gvec'], w=[(tag, 'T', i)])
        ar.reset(m)

    mA = ar.mark()
    xts = [ar.alloc([128, D], F32) for _ in range(2)]

    def srcA(i):
        xt = xts[i % 2]
        op('sp', lambda e, xt=xt, i=i: e.dma_start(out=xt, in_=x_d[i * 128:(i + 1) * 128, :]), w=[('xt', i % 2)])
        return xt, ('xt', i % 2)
    rmsnorm_to_T(srcA, 0, hT, 'A')
    ar.reset(mA)
    sch.fence()
    hT_keys = [('A', 'T', i) for i in range(NT)]

    def finish_dbg(src_ap, keys, rows, cols, conv=None):
        op('sp', lambda e: e.dma_start(out=dbg_d[0:rows, 0:cols], in_=src_ap), r=keys, w=['dbg'])

    def dbg_point(name, ap, keys, rows=128, ncols=S):
        if stop != name:
            return
        op('pq', lambda e: e.dma_start(out=dbg_d[0:rows, 0:ncols], in_=ap), r=keys, w=['dbg'])
        with nc.Block() as block:
            sch.emit(block)
        raise _StopBuild(nc)

    def dbg_multi(name, items):
        if stop != name:
            return
        for (ap, keys, c0, ncol) in items:
            op('pq', lambda e, ap=ap, c0=c0, ncol=ncol: e.dma_start(out=dbg_d[0:128, c0:c0 + ncol], in_=ap), r=keys, w=[('dbg', c0)])
        with nc.Block() as block:
            sch.emit(block)
        raise _StopBuild(nc)

    if stop == 'A':
        m = ar.mark()
        tmp = ar.alloc([128, S], F32)
        op('dve', lambda e: e.tensor_copy(out=tmp, in_=hT[:, 3, :]), r=hT_keys, w=['tmp'])
        finish_dbg(tmp, ['tmp'], 128, S)
        with nc.Block() as block:
            sch.emit(block)
        return nc

    wb = [ar.alloc([128, 8, 128], BF16) for _ in range(4)]
    wctr = [0]

    def load_w(src_ap):
        slot = wctr[0] % 4
        wctr[0] += 1
        t = wb[slot]
        op('pq', lambda e: e.dma_start(out=t, in_=src_ap), w=[('wb', slot)])
        return t, ('wb', slot)
    pctr = [0]

    def next_bank(lo=0, n=4):
        b = lo + pctr[0] % n
        pctr[0] += 1
        return b

    def cols(n, w=512):
        return slice(n * w, (n + 1) * w)

    def proj_mm(wt, wkey, n, bank, M=128, srcT=None, skeys=None):
        srcT = hT if srcT is None else srcT
        skeys = hT_keys if skeys is None else skeys
        for k in range(8):
            op('pe', lambda e, k=k: e.matmul(ps[bank][0:M, :], wt[:, k, 0:M], srcT[:, k, cols(n)],
                                             start=(k == 0), stop=(k == 7)),
               r=[wkey] + skeys[n * 4:(n + 1) * 4], w=[('ps', bank)])

    mL = ar.mark()
    wbd = ar.alloc([128, 8, 128], BF16)
    op('pq', lambda e: e.dma_start(out=wbd, in_=wbd_d), w=['wbd'])
    spc = ar.alloc([128, 8], F32)
    op('act', lambda e: e.activation(out=spc[:, 0:4], in_=lruv[:, 28:32], func=AF.Exp, scale=-1.0),
       r=['lruv'], w=['spc'])
    op('act', lambda e: e.activation(out=spc[:, 0:4], in_=spc[:, 0:4], func=AF.Ln, bias=onec),
       r=['spc', 'onec'], w=['spc'])
    op('dve', lambda e: e.tensor_scalar(out=spc[:, 4:8], in0=spc[:, 0:4], scalar1=-16.0, scalar2=None, op0=OP.mult),
       r=['spc'], w=['spc'])
    op('dve', lambda e: e.tensor_scalar(out=spc[:, 0:4], in0=spc[:, 0:4], scalar1=-8.0, scalar2=None, op0=OP.mult),
       r=['spc'], w=['spc'])
    Lb = {nm: ar.alloc([128, S], F32) for nm in ['xb', 'gate', 'xc', 'r', 'i', 'a', 'b']}
    xcb = ar.alloc([128, S], BF16)
    for c in range(4):
        for nm, chs in (('xb', CH_XB), ('gate', CH_GT)):
            wt, wk = load_w(wfm_d[chs[c]])
            for n in range(4):
                bank = next_bank()
                proj_mm(wt, wk, n, bank)
                op('act', lambda e, nm=nm, n=n, bank=bank: e.activation(out=Lb[nm][:, cols(n)], in_=ps[bank],
                                                                         func=AF.Copy),
                   r=[('ps', bank)], w=[nm])
        xb_, gate_, xc_, r_, i_, a_, b_ = (Lb[k] for k in ['xb', 'gate', 'xc', 'r', 'i', 'a', 'b'])
        cw = lambda tap, c=c: lruv[:, c * 4 + tap:c * 4 + tap + 1]
        op('dve', lambda e, c=c, cw=cw: e.tensor_scalar(out=xc_, in0=xb_, scalar1=cw(3), scalar2=lruv[:, 16 + c:17 + c],
                                                        op0=OP.mult, op1=OP.add),
           r=['xb', 'lruv'], w=['xc'])
        for sft in (1, 2, 3):
            op('dve', lambda e, sft=sft, cw=cw: e.scalar_tensor_tensor(out=xc_[:, sft:], in0=xb_[:, :S - sft],
                                                                       scalar=cw(3 - sft), in1=xc_[:, sft:],
                                                                       op0=OP.mult, op1=OP.add),
               r=['xb', 'xc', 'lruv'], w=['xc'])
        if c == 1:
            dbg_point('C:xb', xb_, ['xb'])
            dbg_point('C:gate', gate_, ['gate'])
            dbg_point('C:xc', xc_, ['xc'])
        op('act', lambda e: e.activation(out=xcb, in_=xc_, func=AF.Copy), r=['xc'], w=['xcb'])
        for gi, (nm, bcol) in enumerate((('r', 20), ('i', 24))):
            for n in range(4):
                bank = next_bank()
                op('pe', lambda e, gi=gi, c=c, n=n, bank=bank: e.matmul(ps[bank], wbd[:, gi * 4 + c, :],
                                                                         xcb[:, cols(n)], start=True, stop=True),
                   r=['wbd', 'xcb'], w=[('ps', bank)])
                op('act', lambda e, nm=nm, n=n, bank=bank, bcol=bcol, c=c: e.activation(
                    out=Lb[nm][:, cols(n)], in_=ps[bank], func=AF.Sigmoid, bias=lruv[:, bcol + c:bcol + c + 1]),
                   r=[('ps', bank), 'lruv'], w=[nm])
        op('act', lambda e, c=c: e.activation(out=a_, in_=r_, func=AF.Exp, scale=spc[:, c:c + 1]),
           r=['r', 'spc'], w=['a'])
        op('act', lambda e, c=c: e.activation(out=b_, in_=r_, func=AF.Exp, scale=spc[:, 4 + c:5 + c]),
           r=['r', 'spc'], w=['b'])
        op('act', lambda e: e.activation(out=b_, in_=b_, func=AF.Sqrt, scale=-1.0, bias=onec),
           r=['b', 'onec'], w=['b'])
        if c == 1:
            dbg_point('C:r', r_, ['r'])
            dbg_point('C:i', i_, ['i'])
            dbg_point('C:a', a_, ['a'])
            dbg_point('C:m', b_, ['b'])
        op('dve', lambda e: e.tensor_tensor(out=i_, in0=i_, in1=xc_, op=OP.mult), r=['i', 'xc'], w=['i'])
        op('dve', lambda e: e.tensor_tensor(out=b_, in0=b_, in1=i_, op=OP.mult), r=['b', 'i'], w=['b'])
        op('dve', lambda e: e.tensor_tensor_scan(out=r_, data0=a_, data1=b_, initial=0.0, op0=OP.mult, op1=OP.add),
           r=['a', 'b'], w=['r'])
        if c == 1:
            dbg_point('C:h', r_, ['r'])
        op('act', lambda e: e.activation(out=a_, in_=gate_, func=AF.Square), r=['gate'], w=['a'])
        op('dve', lambda e: e.tensor_scalar(out=a_, in0=a_, scalar1=0.044715, scalar2=1.0, op0=OP.mult, op1=OP.add),
           r=['a'], w=['a'])
        op('dve', lambda e: e.tensor_tensor(out=a_, in0=a_, in1=gate_, op=OP.mult), r=['a', 'gate'], w=['a'])
        op('act', lambda e: e.activation(out=a_, in_=a_, func=AF.Sigmoid, scale=1.5957691216057308),
           r=['a'], w=['a'])
        op('dve', lambda e: e.tensor_tensor(out=b_, in0=r_, in1=gate_, op=OP.mult), r=['r', 'gate'], w=['b'])
        op('dve', lambda e, c=c: e.tensor_tensor(out=catT[:, 4 + c, :], in0=b_, in1=a_, op=OP.mult),
           r=['a', 'b'], w=[('cat', 4 + c)])
    if stop == 'C':
        tmp = ar.alloc([128, S], F32)
        op('dve', lambda e: e.tensor_copy(out=tmp, in_=catT[:, 5, :]), r=[('cat', 5)], w=['tmp'])
        finish_dbg(tmp, ['tmp'], 128, S)
        with nc.Block() as block:
            sch.emit(block)
        return nc
    ar.reset(mL)
    sch.fence()

    mAtt = ar.mark()
    q_rotT = ar.alloc([128, 4, S], BF16)
    q_latT = ar.alloc([128, 4, S], BF16)
    iq_rotT = ar.alloc([128, 4, S], BF16)
    k_rotT = ar.alloc([128, S], BF16)
    ik2T = ar.alloc([128, S], BF16)
    kv_latT = ar.alloc([128, S], BF16)
    kv1 = ar.alloc([128, 16, 130], BF16)
    iw_sb = ar.alloc([128, 16, 8], F32)
    wukT = ar.alloc([128, 4, 128], BF16)
    wuv = ar.alloc([128, 4, 128], BF16)
    gkv_bc = ar.alloc([128, 128], F32)
    wtm = ar.alloc([128, 8, 136], BF16)
    op('pq', lambda e: e.dma_start(out=wukT, in_=wukT_d), w=['wukT'])
    op('pq', lambda e: e.dma_start(out=wuv, in_=wuv_d), w=['wuv'])
    op('pq', lambda e: e.dma_start(out=wtm, in_=wtm_d), w=['wtm'])
    op('sp', lambda e: e.dma_start(out=gkv_bc, in_=bass.AP(gkv_d.tensor, 0, [[0, 128], [1, 128]])), w=['gkv'])
    mTab = ar.mark()
    posf = ar.alloc([128, S], F32)
    tabs = cat_alias + [ar.alloc([128, S], F32) for _ in range(2)]
    ttmp = ar.alloc([128, S], F32)
    op('pq', lambda e: e.dma_start(out=posf, in_=bass.AP(pos_d.tensor, 0, [[0, 128], [1, S]])), w=['posf'])
    SC = 0.999999
    MAGIC = 12582912.0
    ttmp2 = t2buf = None
    for (tc_, ts_, icol, scol) in ((tabs[0], tabs[1], 0, 1), (tabs[2], tabs[3], 2, 3)):
        for (dst, shift, sg) in ((ts_, 0.0, True), (tc_, 0.25, False)):
            op('dve', lambda e, icol=icol, shift=shift: e.tensor_scalar(out=ttmp, in0=posf,
                                                                        scalar1=ropec[:, icol:icol + 1],
                                                                        scalar2=shift, op0=OP.mult, op1=OP.add),
               r=['posf', 'ropec'], w=['ttmp'])
            op('dve', lambda e, dst=dst: e.tensor_scalar(out=dst, in0=ttmp, scalar1=MAGIC, scalar2=None, op0=OP.add),
               r=['ttmp'], w=['tabs'])
            op('dve', lambda e, dst=dst: e.tensor_scalar(out=dst, in0=dst, scalar1=-MAGIC, scalar2=None, op0=OP.add),
               r=['tabs'], w=['tabs'])
            op('dve', lambda e, dst=dst: e.tensor_tensor(out=ttmp, in0=ttmp, in1=dst, op=OP.subtract),
               r=['tabs', 'ttmp'], w=['ttmp'])
            op('act', lambda e, dst=dst: e.activation(out=dst, in_=ttmp, func=AF.Sin, scale=2 * math.pi * SC),
               r=['ttmp'], w=['tabs'])
            if sg:
                op('dve', lambda e, dst=dst, scol=scol: e.tensor_scalar(out=dst, in0=dst,
                                                                        scalar1=ropec[:, scol:scol + 1],
                                                                        scalar2=None, op0=OP.mult),
                   r=['tabs', 'ropec'], w=['tabs'])

    t1s = [ar.alloc([128, 512], F32) for _ in range(2)]
    t2s = [ar.alloc([128, 512], F32) for _ in range(2)]
    rctr = [0]

    def rope_proj(chX, chP, Tc, Ts, dst_fn, dkey, M=128):
        wX, kX = load_w(wfm_d[chX])
        wP, kP = load_w(wfm_d[chP])
        for n in range(4):
            bX = next_bank()
            proj_mm(wX, kX, n, bX, M)
            bP = next_bank()
            proj_mm(wP, kP, n, bP, M)
            sl = rctr[0] % 2
            rctr[0] += 1
            t1, t2 = t1s[sl], t2s[sl]
            op('dve', lambda e, bX=bX, n=n, t1=t1: e.tensor_tensor(out=t1[0:M], in0=ps[bX][0:M], in1=Tc[0:M, cols(n)],
                                                                   op=OP.mult),
               r=[('ps', bX), 'tabs'], w=[('t1', sl)])
            op('dve', lambda e, bP=bP, n=n, t2=t2: e.tensor_tensor(out=t2[0:M], in0=ps[bP][0:M], in1=Ts[0:M, cols(n)],
                                                                   op=OP.mult),
               r=[('ps', bP), 'tabs'], w=[('t2', sl)])
            op('pool', lambda e, n=n, t1=t1, t2=t2: e.tensor_tensor(out=dst_fn(n), in0=t1[0:M], in1=t2[0:M], op=OP.add),
               r=[('t1', sl), ('t2', sl)], w=[dkey])

    for h in range(4):
        rope_proj(CH_Q[h], CH_QP[h], tabs[0], tabs[1], lambda n, h=h: q_rotT[:, h, cols(n)], ('qrot', h))
        for n in range(4):
            bank = next_bank()
            op('pe', lambda e, h=h, n=n, bank=bank: e.matmul(ps[bank], wukT[:, h, :], q_rotT[:, h, cols(n)],
                                                             start=True, stop=True),
               r=['wukT', ('qrot', h)], w=[('ps', bank)])
            op('act', lambda e, h=h, n=n, bank=bank: e.activation(out=q_latT[:, h, cols(n)], in_=ps[bank],
                                                                  func=AF.Copy),
               r=[('ps', bank)], w=[('qlat', h)])
    rope_proj(CH_K, CH_KP, tabs[0], tabs[1], lambda n: k_rotT[0:32, cols(n)], 'krot', M=32)
    for m_ in range(4):
        rope_proj(CH_IQ[m_], CH_IQP[m_], tabs[2], tabs[3], lambda n, m_=m_: iq_rotT[:, m_, cols(n)], ('iqrot', m_))
    rope_proj(CH_IK, CH_IKP, tabs[2], tabs[3], lambda n: ik2T[:, cols(n)], 'ik2')
    op('pool', lambda e: e.memset(kv1[:, :, 128:130], 1.0), w=[('kv1', i) for i in range(NT)])
    for i in range(NT):
        bank = next_bank()
        for k in range(8):
            op('pe', lambda e, k=k, i=i, bank=bank: e.matmul(ps[bank][:, 0:136], hT[:, k, i * 128:(i + 1) * 128],
                                                             wtm[:, k, :], start=(k == 0), stop=(k == 7)),
               r=['wtm', hT_keys[i]], w=[('ps', bank)])
        ssq = stat[:, 32 + i:33 + i]
        rk = stat[:, 48 + i:49 + i]
        op('act', lambda e, bank=bank, ssq=ssq: e.activation(out=junk[:, 0:128], in_=ps[bank][:, 0:128],
                                                             func=AF.Square, accum_out=ssq),
           r=[('ps', bank)], w=['junk', ('kss', i)])
        op('act', lambda e, ssq=ssq, rk=rk: e.activation(out=rk, in_=ssq, func=AF.Sqrt, scale=1.0 / 128, bias=epsc),
           r=[('kss', i), 'epsc'], w=[('krs', i)])
        op('dve', lambda e, rk=rk: e.reciprocal(out=rk, in_=rk), r=[('krs', i)], w=[('krs', i)])
        op('dve', lambda e, bank=bank, rk=rk, i=i: e.scalar_tensor_tensor(out=kv1[:, i, 0:128], in0=ps[bank][:, 0:128],
                                                                          scalar=rk, in1=gkv_bc,
                                                                          op0=OP.mult, op1=OP.mult),
           r=[('ps', bank), ('krs', i), 'gkv'], w=[('kv1', i)])
        op('dve', lambda e, bank=bank, i=i: e.tensor_scalar(out=iw_sb[:, i, :], in0=ps[bank][:, 128:136],
                                                            scalar1=IW_SCALE, scalar2=None, op0=OP.mult),
           r=[('ps', bank)], w=[('iw', i)])
        bT = next_bank()
        op('pe', lambda e, i=i, bT=bT: e.transpose(out=psb[bT][:, 0:128], in_=kv1[:, i, 0:128], identity=ident),
           r=[('kv1', i), 'ident'], w=[('ps', bT)])
        op('act', lambda e, i=i, bT=bT: e.activation(out=kv_latT[:, i * 128:(i + 1) * 128], in_=psb[bT][:, 0:128],
                                                     func=AF.Copy),
           r=[('ps', bT)], w=[('kvT', i)])
    dbg_point('E:qrot', q_rotT[:, 1, :], [('qrot', 1)])
    dbg_point('E:qlat', q_latT[:, 1, :], [('qlat', 1)])
    dbg_point('E:krot', k_rotT[0:32, :], ['krot'], rows=32)
    dbg_point('E:iqrot', iq_rotT[:, 1, :], [('iqrot', 1)])
    dbg_point('E:ik2', ik2T, ['ik2'])
    dbg_point('E:kvT', kv_latT, [('kvT', i) for i in range(NT)])
    dbg_point('E:tabs', tabs[1], ['tabs'])
    ar.reset(mTab)
    sch.fence()

    o_nT = ar.alloc([128, 4, S], BF16)
    score = ar.alloc([128, S], F32)
    relu_sb = [ar.alloc([128, 512], F32) for _ in range(2)]
    maskb = ar.alloc([128, S], BF16)
    maskT = ar.alloc([128, 16, 128], BF16)
    Eb = [ar.alloc([128, 512], BF16) for _ in range(2)]
    PTb = [ar.alloc([128, 4, 128], BF16) for _ in range(2)]
    o_n = ar.alloc([128, 4, 128], BF16)
    bis = ar.alloc([128, 8], F32)
    recyc = ['tabs', 'posf', 'posi', 'ttmp', ('t1', 0), ('t1', 1), ('t2', 0), ('t2', 1)]
    NIT = 26
    RNG = 64.0
    ectr = [0]
    har = Arena(nc, 32768, ap=ar.ap, base=h_off)
    scoreB = [score, har.alloc([128, S], F32)]
    score2 = har.alloc([128, S], F32)
    maskbB = [maskb, har.alloc([128, S], BF16)]
    junkB = har.alloc([128, S], BF16)
    reluB = relu_sb + [har.alloc([128, 512], F32) for _ in range(2)]
    bisB = [bis, har.alloc([128, 8], F32)]
    rctr2 = [0]

    def att_scores(j, b):
        op('pq', lambda e: e.dma_start(out=ub_d[8 * j:8 * j + 8].rearrange("c p k e -> (c p) (k e)"),
                                       in_=uT_d[8 * j:8 * j + 8].rearrange("c p k e -> (c p) (k e)")),
           w=[('ub', j)])
        op('pq', lambda e: e.dma_start(out=vbf_d.rearrange("(c e1) d -> c e1 d", e1=128)[:, 8 * j:8 * j + 8, :],
                                       in_=v_d[1024 * j:1024 * (j + 1), :].rearrange("(e1 c) d -> c e1 d", c=128)),
           w=[('vbf', j)])
        if j == 0:
            op('pq', lambda e: e.dma_start(out=wpqb_d.rearrange("c p k e -> (c p) (k e)"),
                                           in_=wpq_d.rearrange("c p k e -> (c p) (k e)")), w=['wpqb'])
        Sj = (j + 1) * 128
        tq = slice(j * 128, (j + 1) * 128)
        sc_ = scoreB[b]
        skey = ('score', b)
        nch = (Sj + 511) // 512
        for cc in range(nch):
            w_ = min(512, Sj - cc * 512)
            cs = slice(cc * 512, cc * 512 + w_)
            for hh in range(8):
                pb = (hh % 2) * 64
                bank = next_bank(0, 2)
                op('pe', lambda e, hh=hh, pb=pb, bank=bank, cs=cs, w_=w_: e.matmul(
                    ps[bank][:, 0:w_], iq_rotT[pb:pb + 64, hh // 2, tq], ik2T[pb:pb + 64, cs], start=True, stop=True),
                   r=[('iqrot', hh // 2), 'ik2'], w=[('ps', bank)])
                ri = rctr2[0] % 4
                rctr2[0] += 1
                rl = reluB[ri]
                op('act', lambda e, bank=bank, rl=rl, w_=w_: e.activation(out=rl[:, 0:w_], in_=ps[bank][:, 0:w_],
                                                                          func=AF.Relu),
                   r=[('ps', bank)], w=[('relu', ri)])
                eng = 'dve'
                dst = sc_
                dkey = skey
                if hh == 0:
                    op(eng, lambda e, rl=rl, cs=cs, w_=w_, hh=hh, dst=dst: e.tensor_scalar(
                        out=dst[:, cs], in0=rl[:, 0:w_], scalar1=iw_sb[:, j, hh:hh + 1], scalar2=None, op0=OP.mult),
                       r=[('relu', ri), ('iw', j)], w=[dkey])
                elif eng == 'dve':
                    op(eng, lambda e, rl=rl, cs=cs, w_=w_, hh=hh, dst=dst: e.scalar_tensor_tensor(
                        out=dst[:, cs], in0=rl[:, 0:w_], scalar=iw_sb[:, j, hh:hh + 1], in1=dst[:, cs],
                        op0=OP.mult, op1=OP.add),
                       r=[('relu', ri), ('iw', j), dkey], w=[dkey])
                else:
                    op(eng, lambda e, rl=rl, w_=w_, hh=hh: e.tensor_scalar(
                        out=rl[:, 0:w_], in0=rl[:, 0:w_], scalar1=iw_sb[:, j, hh:hh + 1], scalar2=None, op0=OP.mult),
                       r=[('relu', ri), ('iw', j)], w=[('relu', ri)])
                    op(eng, lambda e, rl=rl, cs=cs, w_=w_, dst=dst: e.tensor_tensor(
                        out=dst[:, cs], in0=dst[:, cs], in1=rl[:, 0:w_], op=OP.add),
                       r=[('relu', ri), dkey], w=[dkey])
        op('dve', lambda e: e.tensor_tensor(out=sc_[:, tq], in0=sc_[:, tq], in1=cmask, op=OP.add),
           r=[skey, 'cmask'], w=[skey])

    def att_bisect_pair(jA, jB):
        A, B = bisB[0], bisB[1]
        SA, SB = (jA + 1) * 128, (jB + 1) * 128
        doA, doB = jA >= 2, jB >= 2
        if doA:
            op('dve', lambda e: e.memset(A[:, 0:1], 0.0), w=[('bis', 0)])
        else:
            op('dve', lambda e: e.memset(A[:, 3:4], -1.0e29), w=[('bis', 0)])
        if doB:
            op('dve', lambda e: e.memset(B[:, 0:1], 0.0), w=[('bis', 1)])
        else:
            op('dve', lambda e: e.memset(B[:, 3:4], -1.0e29), w=[('bis', 1)])
        for it in range(NIT):
            wd = RNG / (2 ** (it + 1))
            if doA:
                op('dve', lambda e: e.tensor_scalar(out=junk[:, 0:SA], in0=scoreB[0][:, 0:SA], scalar1=A[:, 0:1],
                                                    scalar2=None, op0=OP.is_ge, op1=OP.add, accum_out=A[:, 1:2]),
                   r=[('score', 0), ('bis', 0)], w=['junk', ('bis', 0)])
            if doB:
                op('act', lambda e: e.activation(out=junkB[:, 0:SB], in_=scoreB[1][:, 0:SB], func=AF.Sign,
                                                 bias=B[:, 0:1], accum_out=B[:, 1:2]),
                   r=[('score', 1), ('bis', 1)], w=['junkB', ('bis', 1)])
            if doA:
                op('dve', lambda e, wd=wd: e.tensor_scalar(out=A[:, 2:3], in0=A[:, 1:2], scalar1=255.5, scalar2=2.0 * wd,
                                                           op0=OP.is_ge, op1=OP.mult),
                   r=[('bis', 0)], w=[('bis', 0)])
                op('dve', lambda e, wd=wd: e.scalar_tensor_tensor(out=A[:, 0:1], in0=A[:, 2:3], scalar=-wd, in1=A[:, 0:1],
                                                                  op0=OP.add, op1=OP.add),
                   r=[('bis', 0)], w=[('bis', 0)])
            if doB:
                op('dve', lambda e, wd=wd: e.tensor_scalar(out=B[:, 2:3], in0=B[:, 1:2], scalar1=511.0 - SB,
                                                           scalar2=-2.0 * wd, op0=OP.is_ge, op1=OP.mult),
                   r=[('bis', 1)], w=[('bis', 1)])
                op('dve', lambda e, wd=wd: e.scalar_tensor_tensor(out=B[:, 0:1], in0=B[:, 2:3], scalar=wd, in1=B[:, 0:1],
                                                                  op0=OP.add, op1=OP.add),
                   r=[('bis', 1)], w=[('bis', 1)])
        wl = RNG / (2 ** NIT)
        if doA:
            op('dve', lambda e: e.tensor_scalar(out=A[:, 3:4], in0=A[:, 0:1], scalar1=-wl, scalar2=None, op0=OP.add),
               r=[('bis', 0)], w=[('bis', 0)])
        if doB:
            op('dve', lambda e: e.tensor_scalar(out=B[:, 3:4], in0=B[:, 0:1], scalar1=-1.0, scalar2=-wl,
                                                op0=OP.mult, op1=OP.add),
               r=[('bis', 1)], w=[('bis', 1)])

    def att_finish(j, b):
        if j == 0:
            op('pq', lambda e: e.dma_start(out=wpqb_d.rearrange("c p k e -> (c p) (k e)"),
                                           in_=wpq_d.rearrange("c p k e -> (c p) (k e)")), w=['wpqb'])
        Sj = (j + 1) * 128
        tq = slice(j * 128, (j + 1) * 128)
        sc_ = scoreB[b]
        mk = maskbB[b]
        tcol = bisB[b][:, 3:4]
        rdn = bisB[b]
        op('dve', lambda e: e.tensor_scalar(out=mk[:, 0:Sj], in0=sc_[:, 0:Sj], scalar1=tcol, scalar2=None,
                                            op0=OP.is_ge),
           r=[('score', b), ('bis', b)], w=[('maskb', b)])
        for g8 in range((j + 8) // 8):
            bM = 2 + g8
            lo = g8 * 8
            hi = min(j + 1, lo + 8)
            for sc in range(lo, hi):
                op('pe', lambda e, sc=sc, bM=bM, lo=lo: e.transpose(out=psb[bM][:, (sc - lo) * 128:(sc - lo + 1) * 128],
                                                                    in_=mk[:, sc * 128:(sc + 1) * 128],
                                                                    identity=ident),
                   r=[('maskb', b), 'ident'], w=[('ps', bM)])
            op('act', lambda e, bM=bM, lo=lo, hi=hi: e.activation(
                out=maskT[:, lo:hi, :], in_=psb[bM][:, 0:(hi - lo) * 128].rearrange("p (a b) -> p a b", b=128),
                func=AF.Copy),
               r=[('ps', bM)], w=['maskT'])
        for sc in range(j + 1):
            sk = slice(sc * 128, (sc + 1) * 128)
            bL = 2 + (ectr[0] % 2)
            sl = ectr[0] % 2
            ectr[0] += 1
            op('pe', lambda e, sk=sk, bL=bL: e.matmul(ps[bL], kv_latT[:, sk], q_latT[:, :, tq], start=True, stop=False),
               r=[('kvT', sc)] + [('qlat', h) for h in range(4)], w=[('ps', bL)])
            op('pe', lambda e, sk=sk, bL=bL: e.matmul(ps[bL], k_rotT[0:32, sk], q_rotT[0:32, :, tq],
                                                      start=False, stop=True),
               r=['krot'] + [('qrot', h) for h in range(4)], w=[('ps', bL)])
            op('act', lambda e, bL=bL, sl=sl: e.activation(out=Eb[sl], in_=ps[bL], func=AF.Exp, scale=ATT_SCALE),
               r=[('ps', bL)], w=[('E', sl)])
            op('dve', lambda e, sl=sl, sc=sc: e.tensor_tensor(
                out=PTb[sl], in0=Eb[sl].rearrange("p (h t) -> p h t", h=4), in1=bc_mid(maskT[:, sc, :], 1, 4),
                op=OP.mult),
               r=[('E', sl), 'maskT'], w=[('PT', sl)])
            for h in range(4):
                bO = 4 + h
                op('pe', lambda e, h=h, bO=bO, sl=sl, sc=sc: e.matmul(
                    ps[bO][:, 0:129], PTb[sl][:, h, :], kv1[:, sc, 0:129], start=(sc == 0), stop=(sc == j)),
                   r=[('PT', sl), ('kv1', sc)], w=[('ps', bO)])
        for h in range(4):
            op('dve', lambda e, h=h: e.reciprocal(out=rdn[:, 4 + h:5 + h], in_=ps[4 + h][:, 128:129]),
               r=[('ps', 4 + h)], w=[('rden', b, h)])
            op('dve', lambda e, h=h: e.tensor_scalar(out=o_n[:, h, :], in0=ps[4 + h][:, 0:128],
                                                     scalar1=rdn[:, 4 + h:5 + h], scalar2=None, op0=OP.mult),
               r=[('ps', 4 + h), ('rden', b, h)], w=[('o_n', h)])
        for h in range(4):
            op('pe', lambda e, h=h: e.transpose(out=psb[2][:, h * 128:(h + 1) * 128], in_=o_n[:, h, :], identity=ident),
               r=[('o_n', h), 'ident'], w=[('ps', 2)])
        op('act', lambda e: e.activation(out=o_nT[:, :, tq], in_=psb[2][:, 0:512].rearrange("p (h t) -> p h t", h=4),
                                         func=AF.Copy),
           r=[('ps', 2)], w=[('o_nT', j)])

    NIT = 22
    RNG = 16.0
    for jp in range(NT // 2):
        jA, jB = 2 * jp, 2 * jp + 1
        att_scores(jA, 0)
        att_scores(jB, 1)
        att_bisect_pair(jA, jB)
        att_finish(jA, 0)
        att_finish(jB, 1)

    for h in range(4):
        for n in range(4):
            bank = next_bank()
            op('pe', lambda e, h=h, n=n, bank=bank: e.matmul(ps[bank], wuv[:, h, :], o_nT[:, h, cols(n)],
                                                             start=True, stop=True),
               r=['wuv'] + [('o_nT', jj) for jj in range(n * 4, n * 4 + 4)], w=[('ps', bank)])
            op('act', lambda e, h=h, n=n, bank=bank: e.activation(out=catT[:, h, cols(n)], in_=ps[bank], func=AF.Copy),
               r=[('ps', bank)], w=[('cat', h)])
    if stop == 'E':
        tmp = ar.alloc([128, S], F32)
        op('dve', lambda e: e.tensor_copy(out=tmp, in_=catT[:, 1, :]), r=[('cat', 1)], w=['tmp'])
        finish_dbg(tmp, ['tmp'], 128, S)
        with nc.Block() as block:
            sch.emit(block)
        return nc

    ar.reset(mAtt)
    sch.fence()
    mF = ar.mark()
    x_sb = ar.alloc([128, NT, D], F32)
    wbig = ar.alloc([128, 8, 1024], BF16)
    for q4 in range(4):
        op('sp', lambda e, q4=q4: e.dma_start(out=x_sb[:, q4 * 4:(q4 + 1) * 4, :],
                                             in_=x_d[q4 * 512:(q4 + 1) * 512, :].rearrange("(n p) m -> p n m", p=128)),
           w=[('x', i) for i in range(q4 * 4, q4 * 4 + 4)])
    op('pq', lambda e: e.dma_start(out=wbig, in_=wout_d), w=['wbig'])
    cat_keys = [('cat', k) for k in range(8)]
    for i in range(NT):
        for hf in range(2):
            bank = next_bank()
            for k in range(8):
                op('pe', lambda e, i=i, hf=hf, k=k, bank=bank: e.matmul(
                    ps[bank], catT[:, k, i * 128:(i + 1) * 128], wbig[:, k, hf * 512:(hf + 1) * 512],
                    start=(k == 0), stop=(k == 7)),
                   r=['wbig'] + cat_keys, w=[('ps', bank)])
            op('dve', lambda e, i=i, hf=hf, bank=bank: e.tensor_tensor(
                out=x_sb[:, i, hf * 512:(hf + 1) * 512], in0=ps[bank], in1=x_sb[:, i, hf * 512:(hf + 1) * 512], op=OP.add),
               r=[('ps', bank), ('x', i)], w=[('x', i)])
    dbg_point('F', x_sb[:, 3, :], [('x', 3)], ncols=D)

    def perm(ap, order):
        l = [list(x) for x in ap.ap]
        return bass.AP(ap.tensor, ap.offset, [l[i] for i in order])

    def srcG(i):
        return x_sb[:, i, :], ('x', i)
    rmsnorm_to_T(srcG, 8, hT, 'G')
    h2T = hT
    h2_keys = [('G', 'T', i) for i in range(NT)]
    for q4 in range(4):
        op('sp', lambda e, q4=q4: e.dma_start(
            out=xmid_d[q4 * 512:(q4 + 1) * 512, :].rearrange("(n p) m -> p n m", p=128),
            in_=x_sb[:, q4 * 4:(q4 + 1) * 4, :]),
           r=[('x', i) for i in range(q4 * 4, q4 * 4 + 4)], w=[('xmid', q4 * 2), ('xmid', q4 * 2 + 1)])
    dbg_point('G:h2T', hT[:, 3, :], h2_keys)
    ar.reset(mF)
    sch.fence()
    car = Arena(nc, 32768, ap=ar.ap, base=cat_off)
    qT_g = car.alloc([128, 16, 256], BF16)
    s12 = car.alloc([128, 16, 128], F32)
    cand = car.alloc([128, 8, 256], F32)
    ohtmp = car.alloc([128, 8, 16, 16], F32)
    G_sb = ar.alloc([128, 128, 256], BF16)
    kT_sb = ar.alloc([128, 16, 128], BF16)
    op('pq', lambda e: e.dma_start(out=kT_sb, in_=kT_d), w=['kT'])
    v16 = ar.alloc([128, 16, 16], F32)
    idx16 = ar.alloc([128, 16, 16], U32)
    idxf = ar.alloc([128, 16, 16], F32)
    vals = ar.alloc([128, 8, 16], F32)
    fidx = ar.alloc([128, 8, 16], U32)
    fif = ar.alloc([128, 8, 16], F32)
    r_f = ar.alloc([128, 8, 16], F32)
    c_f = ar.alloc([128, 8, 16], F32)
    e1f2 = [ar.alloc([128, 8, 16], F32) for _ in range(2)]
    e2f2 = [ar.alloc([128, 8, 16], F32) for _ in range(2)]
    exg2 = [ar.alloc([128, 8, 16], F32) for _ in range(2)]
    Zs = ar.alloc([128, 16], F32)
    E1T_b = ar.alloc([128, 128], BF16)
    E2T_b = ar.alloc([128, 128], BF16)
    gT_b = ar.alloc([128, 128], BF16)
    TB = 16
    OH1 = [ar.alloc([128, TB, 128], BF16) for _ in range(2)]
    OH2 = [ar.alloc([128, TB, 128], BF16) for _ in range(2)]
    uTb = [ar.alloc([128, 8, 128], BF16) for _ in range(3)]
    vb = [ar.alloc([128, D], BF16) for _ in range(3)]
    gab = [ar.alloc([128, 256], BF16) for _ in range(2)]
    Wtb = [ar.alloc([128, 256], BF16) for _ in range(2)]
    xg = ar.alloc([128, 2, D], F32)
    Gc = [ar.alloc([128, 256], BF16) for _ in range(3)]
    wq = [ar.alloc([128, 8, 128], BF16) for _ in range(4)]
    iota16 = iota_f[:, 0:16]
    iota_rep = junk.rearrange("p (a b) -> p a b", b=128)
    op('dve', lambda e: e.tensor_copy(out=iota_rep, in_=bc_mid(iota_b, 1, 16)), r=['iota_b'], w=['junk'])
    MAGIC2 = 12582912.0
    gctr = [0]
    wq_ptr = [0]
    v_ch = vbf_d.rearrange("(c e1) d -> c e1 d", e1=128)
    vbf_keys = [('vbf', j) for j in range(16)]

    def gbuild(g):
        gq = []

        def gop(eng, fn, r=(), w=(), cost=None):
            if cost is None:
                cost = {'dve': 0.6, 'pe': 0.15, 'act': 0.5}.get(eng, 0.0)
            gq.append((eng, fn, r, w, cost))
        def wq_load_upto(L):
            while wq_ptr[0] <= L and wq_ptr[0] < 8 * 16:
                Lq = wq_ptr[0]
                wq_ptr[0] += 1
                gop('pq', lambda e, Lq=Lq: e.dma_start(out=wq[Lq % 4], in_=wpqb_d[Lq % 16]), r=['wpqb'], w=[('wq', Lq % 4)], cost=0.0)

        def qT_stage(gg):
            qcols = slice(gg * 256, (gg + 1) * 256)
            wq_load_upto(gg * 16 + 3)
            for ch in range(16):
                wt = wq[ch % 4]
                bank = 6 + ch % 2
                for k in range(8):
                    gop('pe', lambda e, k=k, wt=wt, bank=bank: e.matmul(ps[bank][:, 0:256], wt[:, k, :], h2T[:, k, qcols],
                                                                        start=(k == 0), stop=(k == 7)),
                        r=[('wq', ch % 4)] + h2_keys[2 * gg:2 * gg + 2], w=[('ps', bank)])
                gop('act', lambda e, ch=ch, bank=bank: e.activation(out=qT_g[:, ch, :], in_=ps[bank][:, 0:256], func=AF.Copy),
                    r=[('ps', bank)], w=['qT_g'])
                wq_load_upto(gg * 16 + ch + 4)
        if g == 0:
            qT_stage(0)
        def stage_x(tt):
            tcol = slice(tt * 128, (tt + 1) * 128)
            e1f, e2f, exg = e1f2[tt], e2f2[tt], exg2[tt]
            ek = 'e1f%d' % tt
            e2k = 'e2f%d' % tt
            xk = 'exg%d' % tt
            for q in range(4):
                bank = 6 + q % 2
                for cq in range(4):
                    ch = q * 4 + cq
                    gop('pe', lambda e, ch=ch, cq=cq, bank=bank, tcol=tcol: e.matmul(ps[bank][:, cq * 128:(cq + 1) * 128],
                                                                         qT_g[:, ch, tcol], kT_sb[:, ch, :],
                                                                         start=True, stop=True),
                       r=['qT_g', 'kT'], w=[('ps', bank)])
                gop('act', lambda e, q=q, bank=bank: e.activation(
                    out=s12[:, q * 4:(q + 1) * 4, :], in_=ps[bank].rearrange("p (a b) -> p a b", b=128), func=AF.Copy),
                   r=[('ps', bank)], w=[('s12', q * 4 + i_) for i_ in range(4)])
            for ch in range(16):
                row = s12[:, ch, :]
                gop('dve', lambda e, ch=ch, row=row: e.max(out=v16[:, ch, 0:8], in_=row), r=[('s12', ch)], w=[('v16', ch)])
                gop('dve', lambda e, ch=ch, row=row: e.max_index(out=idx16[:, ch, 0:8], in_max=v16[:, ch, 0:8],
                                                                in_values=row),
                   r=[('s12', ch), ('v16', ch)], w=[('idx16', ch)])
                gop('dve', lambda e, ch=ch, row=row: e.match_replace(out=row, in_to_replace=v16[:, ch, 0:8],
                                                                    in_values=row, imm_value=NEG),
                   r=[('s12', ch), ('v16', ch)], w=[('s12', ch)])
                gop('dve', lambda e, ch=ch, row=row: e.max(out=v16[:, ch, 8:16], in_=row), r=[('s12', ch)], w=[('v16', ch)])
                gop('dve', lambda e, ch=ch, row=row: e.max_index(out=idx16[:, ch, 8:16], in_max=v16[:, ch, 8:16],
                                                                in_values=row),
                   r=[('s12', ch), ('v16', ch)], w=[('idx16', ch)])
            gop('dve', lambda e: e.tensor_copy(out=idxf, in_=idx16), r=[('idx16', c_) for c_ in range(16)], w=['idxf'])
            v4 = v16.rearrange("p (h two) k -> p h two k", two=2)
            i4 = idxf.rearrange("p (h two) k -> p h two k", two=2)
            cand4 = cand.rearrange("p h (i j) -> p h i j", j=16)
            gop('dve', lambda e: e.tensor_tensor(out=cand4, in0=bc_last(v4[:, :, 0, :], 16), in1=bc_mid(v4[:, :, 1, :], 2, 16),
                                                op=OP.add),
               r=[('v16', c_) for c_ in range(16)], w=[('cand', h_) for h_ in range(8)], cost=2.4)
            for h in range(8):
                row = cand[:, h, :]
                gop('dve', lambda e, h=h, row=row: e.max(out=vals[:, h, 0:8], in_=row), r=[('cand', h)], w=[('vals', h)])
                gop('dve', lambda e, h=h, row=row: e.max_index(out=fidx[:, h, 0:8], in_max=vals[:, h, 0:8], in_values=row),
                   r=[('cand', h), ('vals', h)], w=[('fidx', h)])
                gop('dve', lambda e, h=h, row=row: e.match_replace(out=row, in_to_replace=vals[:, h, 0:8], in_values=row,
                                                                  imm_value=NEG),
                   r=[('cand', h), ('vals', h)], w=[('cand', h)])
                gop('dve', lambda e, h=h, row=row: e.max(out=vals[:, h, 8:16], in_=row), r=[('cand', h)], w=[('vals', h)])
                gop('dve', lambda e, h=h, row=row: e.max_index(out=fidx[:, h, 8:16], in_max=vals[:, h, 8:16],
                                                              in_values=row),
                   r=[('cand', h), ('vals', h)], w=[('fidx', h)])
            gop('dve', lambda e: e.tensor_copy(out=fif, in_=fidx), r=[('fidx', h_) for h_ in range(8)], w=['fif'])
            gop('dve', lambda e: e.tensor_scalar(out=r_f, in0=fif, scalar1=0.0625, scalar2=-0.46875, op0=OP.mult, op1=OP.add),
               r=['fif'], w=['r_f'])
            gop('dve', lambda e: e.tensor_scalar(out=r_f, in0=r_f, scalar1=MAGIC2, scalar2=None, op0=OP.add),
               r=['r_f'], w=['r_f'])
            gop('dve', lambda e: e.tensor_scalar(out=r_f, in0=r_f, scalar1=-MAGIC2, scalar2=None, op0=OP.add),
               r=['r_f'], w=['r_f'])
            gop('dve', lambda e: e.scalar_tensor_tensor(out=c_f, in0=r_f, scalar=-16.0, in1=fif, op0=OP.mult, op1=OP.add),
               r=['r_f', 'fif'], w=['c_f'])
            for (sel, two, dst, dkey) in ((r_f, 0, e1f, ek), (c_f, 1, e2f, e2k)):
                gop('dve', lambda e, sel=sel: e.tensor_tensor(
                    out=ohtmp, in0=bc_mid(bc_mid(iota16, 1, 16), 1, 8), in1=bc_last(sel, 16), op=OP.is_equal),
                   r=['r_f', 'c_f', 'iota_f'], w=['ohtmp'], cost=2.4)
                gop('dve', lambda e, two=two: e.tensor_tensor(out=ohtmp, in0=ohtmp, in1=bc_mid(i4[:, :, two, :], 2, 16),
                                                             op=OP.mult),
                   r=['ohtmp', 'idxf'], w=['ohtmp'], cost=2.4)
                gop('dve', lambda e, dst=dst: e.tensor_reduce(out=dst, in_=ohtmp, axis=AX.X, op=OP.add),
                   r=['ohtmp'], w=[dkey], cost=2.4)
            gop('dve', lambda e: e.tensor_tensor(out=exg, in0=vals, in1=bc_last(vals[:, :, 0], 16), op=OP.subtract),
               r=[('vals', h_) for h_ in range(8)], w=[xk])
            gop('act', lambda e: e.activation(out=exg, in_=exg, func=AF.Exp), r=[xk], w=[xk])
            gop('dve', lambda e: e.tensor_reduce(out=Zs[:, 0:8], in_=exg, axis=AX.X, op=OP.add), r=[xk], w=['Zs'])
            gop('dve', lambda e: e.reciprocal(out=Zs[:, 8:16], in_=Zs[:, 0:8]), r=['Zs'], w=['Zs'])
            gop('dve', lambda e: e.tensor_tensor(out=exg, in0=exg, in1=bc_last(Zs[:, 8:16], 16), op=OP.mult),
               r=[xk, 'Zs'], w=[xk])
        def stage_y(tt):
            e1f, e2f, exg = e1f2[tt], e2f2[tt], exg2[tt]
            for ti, (src, skey, dst, dkey) in enumerate(((e1f, 'e1f%d' % tt, E1T_b, 'E1T'), (e2f, 'e2f%d' % tt, E2T_b, 'E2T'),
                                                          (exg, 'exg%d' % tt, gT_b, 'gT'))):
                bank = 6 + ti % 2
                gop('pe', lambda e, src=src, bank=bank: e.transpose(out=ps[bank][:, 0:128],
                                                                   in_=src.rearrange("p h k -> p (h k)"),
                                                                   identity=ident_f),
                   r=[skey, 'ident_f'], w=[('ps', bank)])
                gop('act', lambda e, dst=dst, bank=bank: e.activation(out=dst, in_=ps[bank][:, 0:128], func=AF.Copy),
                   r=[('ps', bank)], w=[dkey])
            ohs = {}

            def oh_make(tb):
                    ts = slice(tb * TB, (tb + 1) * TB)
                    sl = gctr[0] % 2
                    gctr[0] += 1
                    o1, o2 = OH1[sl], OH2[sl]
                    ohs[tb] = (sl, o1, o2)
                    gop('dve', lambda e, o2=o2, ts=ts: e.tensor_tensor(out=o2, in0=iota_rep,
                                                                      in1=bc_last(E2T_b[:, ts], 128), op=OP.is_equal),
                       r=['junk', 'E2T'], w=[('OH2', sl)], cost=1.4)
                    gop('dve', lambda e, o1=o1, ts=ts: e.tensor_tensor(out=o1, in0=iota_rep,
                                                                      in1=bc_last(E1T_b[:, ts], 128), op=OP.is_equal),
                       r=['junk', 'E1T'], w=[('OH1', sl)], cost=1.4)
                    gop('dve', lambda e, o1=o1, ts=ts: e.tensor_tensor(out=o1, in0=o1, in1=bc_last(gT_b[:, ts], 128),
                                                                       op=OP.mult),
                       r=[('OH1', sl), 'gT'], w=[('OH1', sl)], cost=1.4)

            def oh_use(tb):
                    sl, o1, o2 = ohs[tb]
                    for q in range(TB // 4):
                        bank = 6 + (tb * (TB // 4) + q) % 2
                        for t4 in range(4):
                            t = q * 4 + t4
                            gop('pe', lambda e, t=t, t4=t4, bank=bank, o1=o1, o2=o2: e.matmul(
                                ps[bank][:, t4 * 128:(t4 + 1) * 128], o1[:, t, :], o2[:, t, :],
                                start=True, stop=True),
                               r=[('OH1', sl), ('OH2', sl)], w=[('ps', bank)])
                        t0 = tt * 128 + tb * TB + q * 4
                        dstp = G_sb[:, :, t0:t0 + 4]
                        eng = 'act'
                        if eng == 'act':
                            gop('act', lambda e, bank=bank, dstp=dstp: e.activation(
                                out=dstp, in_=ps[bank].rearrange("p (t e) -> p e t", t=4), func=AF.Copy),
                               r=[('ps', bank)], w=['G_sb'])
                        else:
                            gop('dve', lambda e, bank=bank, dstp=dstp: e.tensor_copy(
                                out=dstp, in_=ps[bank].rearrange("p (a b) -> p a b", b=128)),
                               r=[('ps', bank)], w=['G_sb'])

            oh_make(0)
            for tb in range(128 // TB):
                if tb + 1 < 128 // TB:
                    oh_make(tb + 1)
                oh_use(tb)

        stage_x(0)
        stage_x(1)
        if g < 7:
            qT_stage(g + 1)
        stage_y(0)
        stage_y(1)
        for q4 in range(4):
            gop('pq', lambda e, q4=q4: e.dma_start(out=gd_d[g][:, q4 * 32:(q4 + 1) * 32, :],
                                                   in_=G_sb[:, q4 * 32:(q4 + 1) * 32, :]),
                r=['G_sb'], w=[('gd', g, q4)], cost=0.0)
        return gq

    def peer_main(g, pending):
        gcols = slice(g * 256, (g + 1) * 256)
        def chunk_front(c):
            sl3 = c % 3
            op('sp', lambda e: e.dma_start(out=uTb[sl3], in_=ub_d[c]), r=[('ub', c // 8)], w=[('uTb', sl3)])
            op('sp', lambda e: e.dma_start(out=vb[sl3], in_=v_ch[c]), r=vbf_keys, w=[('vb', sl3)])
            op('sp', lambda e: e.dma_start(out=Gc[sl3], in_=gd_d[g][:, c, :]), r=[('gd', g, c // 32)], w=[('Gc', sl3)])
            bank = 4 + c % 2
            for k in range(8):
                op('pe', lambda e, k=k: e.matmul(ps[bank][:, 0:256], uTb[sl3][:, k, :], h2T[:, k, gcols],
                                                 start=(k == 0), stop=(k == 7)),
                   r=[('uTb', sl3)] + h2_keys[2 * g:2 * g + 2], w=[('ps', bank)])
            s2 = c % 2
            op('act', lambda e: e.activation(out=gab[s2], in_=ps[bank][:, 0:256], func=AF.Gelu),
               r=[('ps', bank)], w=[('ga', s2)])
            op('pool', lambda e: e.tensor_tensor(out=Wtb[s2], in0=gab[s2], in1=Gc[sl3], op=OP.mult),
               r=[('ga', s2), ('Gc', sl3)], w=[('Wt', s2)])

        def chunk_back(c):
            sl3 = c % 3
            s2 = c % 2
            for t2 in range(2):
                for hf in range(2):
                    ob = t2 * 2 + hf
                    op('pe', lambda e, t2=t2, hf=hf, ob=ob: e.matmul(
                        ps[ob], Wtb[s2][:, t2 * 128:(t2 + 1) * 128], vb[sl3][:, hf * 512:(hf + 1) * 512],
                        start=(c == 0), stop=(c == 127)),
                       r=[('Wt', s2), ('vb', sl3)], w=[('ps', ob)])

        total_cost = sum(t_[4] for t_ in pending)
        per = total_cost / 127.0
        released = 0.0
        for c in range(129):
            if c < 128:
                chunk_front(c)
            if c >= 1:
                chunk_back(c - 1)
            while pending and released < per * (c + 1):
                t_ = pending.pop(0)
                op(*t_[:4])
                released += t_[4]
        while pending:
            t_ = pending.pop(0)
            op(*t_[:4])
        op('sp', lambda e: e.dma_start(out=xg, in_=xmid_d[g * 256:(g + 1) * 256, :].rearrange("(n p) m -> p n m", p=128)),
           r=[('xmid', g)], w=['xg'])
        for t2 in range(2):
            for hf in range(2):
                ob = t2 * 2 + hf
                op('dve', lambda e, t2=t2, hf=hf, ob=ob: e.tensor_tensor(
                    out=xg[:, t2, hf * 512:(hf + 1) * 512], in0=ps[ob], in1=xg[:, t2, hf * 512:(hf + 1) * 512], op=OP.add),
                   r=[('ps', ob), 'xg'], w=['xg'])
        op('sp', lambda e: e.dma_start(out=xmid_d[g * 256:(g + 1) * 256, :].rearrange("(n p) m -> p n m", p=128), in_=xg),
           r=['xg'], w=[('xmid', g)])

    for t_ in gbuild(0):
        op(*t_[:4])
    for g in range(8):
        peer_main(g, gbuild(g + 1) if g < 7 else [])

    ar.reset(mF)
    sch.fence()
    x_sb = ar.alloc([128, NT, D], F32)
    wbig = ar.alloc([128, 8, 1024], BF16)
    for q4 in range(4):
        op('sp', lambda e, q4=q4: e.dma_start(out=x_sb[:, q4 * 4:(q4 + 1) * 4, :],
                                             in_=xmid_d[q4 * 512:(q4 + 1) * 512, :].rearrange("(n p) m -> p n m", p=128)),
           r=[('xmid', q4 * 2), ('xmid', q4 * 2 + 1)], w=[('x', i) for i in range(q4 * 4, q4 * 4 + 4)])
    dbg_point('G:x2', x_sb[:, 3, :], [('x', 3)], ncols=D)

    sch.fence()

    def srcH(i):
        return x_sb[:, i, :], ('x', i)
    xT = hT
    xbf = [ar.alloc([128, D], BF16) for _ in range(2)]
    for i in range(NT):
        xb2 = xbf[i % 2]
        op('act', lambda e, i=i, xb2=xb2: e.activation(out=xb2, in_=x_sb[:, i, :], func=AF.Copy),
           r=[('x', i)], w=[('xbf', i % 2)])
        bank = next_bank()
        for k in range(8):
            op('pe', lambda e, k=k, xb2=xb2, bank=bank: e.transpose(out=psb[bank][:, k * 128:(k + 1) * 128],
                                                                     in_=xb2[:, k * 128:(k + 1) * 128], identity=ident),
               r=[('xbf', i % 2), 'ident'], w=[('ps', bank)])
        op('act', lambda e, i=i, bank=bank: e.activation(out=xT[:, :, i * 128:(i + 1) * 128],
                                                         in_=psb[bank].rearrange("p (k t) -> p k t", k=8), func=AF.Copy),
           r=[('ps', bank)], w=[('xT', i)])
    wple = ar.alloc([128, 2, 1024], BF16)
    pT_sb = ar.alloc([128, 2, S], BF16)
    gfin_bc = ar.alloc([128, D], F32)
    sgt = [ar.alloc([128, 512], F32) for _ in range(2)]
    outt = [ar.alloc([128, D], F32) for _ in range(2)]
    op('pq', lambda e: e.dma_start(out=wbig, in_=wpg_d), w=['wbig'])
    op('pq', lambda e: e.dma_start(out=wple, in_=wple_d), w=['wple'])
    op('pq', lambda e: e.dma_start(out=pT_sb, in_=pT_d.rearrange("(k p) s -> p k s", p=128)), w=['pT'])
    op('sp', lambda e: e.dma_start(out=gfin_bc, in_=bass.AP(gfin_d.tensor, 0, [[0, 128], [1, D]])), w=['gfin'])
    for i in range(NT):
        for hf in range(2):
            bG = next_bank()
            for k in range(8):
                op('pe', lambda e, i=i, hf=hf, k=k, bG=bG: e.matmul(
                    ps[bG], xT[:, k, i * 128:(i + 1) * 128], wbig[:, k, hf * 512:(hf + 1) * 512],
                    start=(k == 0), stop=(k == 7)),
                   r=['wbig', ('xT', i)], w=[('ps', bG)])
            bP = next_bank()
            for k in range(2):
                op('pe', lambda e, i=i, hf=hf, k=k, bP=bP: e.matmul(
                    ps[bP], pT_sb[:, k, i * 128:(i + 1) * 128], wple[:, k, hf * 512:(hf + 1) * 512],
                    start=(k == 0), stop=(k == 1)),
                   r=['wple', 'pT'], w=[('ps', bP)])
            sg_ = sgt[hf]
            op('act', lambda e, bG=bG, sg_=sg_: e.activation(out=sg_, in_=ps[bG], func=AF.Sigmoid),
               r=[('ps', bG)], w=[('sgt', hf)])
            op('dve', lambda e, bP=bP, sg_=sg_: e.tensor_tensor(out=sg_, in0=sg_, in1=ps[bP], op=OP.mult),
               r=[('ps', bP), ('sgt', hf)], w=[('sgt', hf)])
            op('dve', lambda e, i=i, hf=hf, sg_=sg_: e.tensor_tensor(
                out=x_sb[:, i, hf * 512:(hf + 1) * 512], in0=x_sb[:, i, hf * 512:(hf + 1) * 512], in1=sg_, op=OP.add),
               r=[('sgt', hf), ('x', i)], w=[('x', i)])
        ss = stat[:, i:i + 1]
        rs = stat[:, 16 + i:17 + i]
        op('act', lambda e, i=i, ss=ss: e.activation(out=junk[:, 0:D], in_=x_sb[:, i, :], func=AF.Square, accum_out=ss),
           r=[('x', i)], w=['junk', ('Hss', i)])
        op('act', lambda e, ss=ss, rs=rs: e.activation(out=rs, in_=ss, func=AF.Sqrt, scale=1.0 / D, bias=epsc),
           r=[('Hss', i), 'epsc'], w=[('Hrs', i)])
        op('dve', lambda e, rs=rs: e.reciprocal(out=rs, in_=rs), r=[('Hrs', i)], w=[('Hrs', i)])
        ot = outt[i % 2]
        op('dve', lambda e, i=i, rs=rs, ot=ot: e.scalar_tensor_tensor(out=ot, in0=x_sb[:, i, :], scalar=rs, in1=gfin_bc,
                                                                     op0=OP.mult, op1=OP.mult),
           r=[('x', i), ('Hrs', i), 'gfin'], w=[('outt', i % 2)])
        op('sp', lambda e, i=i, ot=ot: e.dma_start(out=out_d[i * 128:(i + 1) * 128, :], in_=ot),
           r=[('outt', i % 2)], w=[('out', i)])
    with nc.Block() as block:
        sch.emit(block)
    return nc


def _host_consts():
    ident = np.eye(128, dtype=np.float32)
    t = np.arange(128)
    cmask = np.where(t[None, :] <= t[:, None], 0.0, NEG).astype(np.float32)
    iota = np.broadcast_to(np.arange(128, dtype=np.float32)[None, :], (128, 128)).copy()
    ropec = np.zeros((128, 4), np.float32)
    inv_a = ROPE_THETA ** (-np.arange(0, 32, 2, dtype=np.float32) / 32)
    inv_i = ROPE_THETA ** (-np.arange(0, 16, 2, dtype=np.float32) / 16)
    for p in range(32):
        ropec[p, 0] = inv_a[p % 16] / (2 * np.pi)
        ropec[p, 1] = -1.0 if p < 16 else 1.0
    for base in (0, 64):
        for p in range(16):
            ropec[base + p, 2] = inv_i[p % 8] / (2 * np.pi)
            ropec[base + p, 3] = -1.0 if p < 8 else 1.0
    return ident, cmask, iota, ropec


def _prep_shared(inp):
    w_in = np.asarray(inp['w_in'][0], np.float32)
    cols = np.zeros((NCH, 128), np.int64)
    qo, kro, iqo, iko, xbo, gto = 0, 640, 672, 1184, 1256, 1768
    for h in range(4):
        cols[CH_Q[h]] = qo + 128 * h + np.arange(128)
        pc = np.arange(128)
        pc[:16] = np.arange(16, 32)
        pc[16:32] = np.arange(0, 16)
        cols[CH_QP[h]] = qo + 128 * h + pc
    kc = np.arange(128) % 32
    cols[CH_K] = kro + kc
    kp = kc.copy()
    kp[kc < 16] = kc[kc < 16] + 16
    kp[kc >= 16] = kc[kc >= 16] - 16
    cols[CH_KP] = kro + kp
    for m in range(4):
        cols[CH_IQ[m]] = iqo + 128 * m + np.arange(128)
        pc = np.arange(128)
        for base in (0, 64):
            pc[base:base + 8] = base + np.arange(8, 16)
            pc[base + 8:base + 16] = base + np.arange(0, 8)
        cols[CH_IQP[m]] = iqo + 128 * m + pc
    ic = np.arange(128) % 64
    cols[CH_IK] = iko + ic
    ip = ic.copy()
    ip[ic < 8] = ic[ic < 8] + 8
    ip[(ic >= 8) & (ic < 16)] = ic[(ic >= 8) & (ic < 16)] - 8
    cols[CH_IKP] = iko + ip
    for c in range(4):
        cols[CH_XB[c]] = xbo + 128 * c + np.arange(128)
        cols[CH_GT[c]] = gto + 128 * c + np.arange(128)
    wg = w_in[:, cols.reshape(-1)].reshape(8, 128, NCH, 128)
    wfm = np.ascontiguousarray(wg.transpose(2, 1, 0, 3))
    tmc = np.concatenate([np.arange(512, 640), np.arange(1248, 1256)])
    wtm = np.ascontiguousarray(w_in[:, tmc].reshape(8, 128, 136).transpose(1, 0, 2))
    gvec = np.zeros((128, 24), np.float32)
    gvec[:, 0:8] = np.asarray(inp['g_mix'][0]).reshape(8, 128).T
    gvec[:, 8:16] = np.asarray(inp['g_ffn'][0]).reshape(8, 128).T
    ident, cmask, iota, ropec = _host_consts()
    w_uk = np.asarray(inp['w_uk'][0], np.float32)
    wukT = np.zeros((128, 4, 128), np.float32)
    wukT[32:128] = w_uk.transpose(2, 0, 1)
    wuv = np.ascontiguousarray(np.asarray(inp['w_uv'][0], np.float32).transpose(1, 0, 2))
    lruv = np.zeros((128, 36), np.float32)
    cw = np.asarray(inp['conv_w'][0], np.float32)
    lruv[:, 0:16] = cw.reshape(4, 4, 128).transpose(2, 1, 0).reshape(128, 16)
    lruv[:, 16:20] = np.asarray(inp['conv_b'][0]).reshape(4, 128).T
    lruv[:, 20:24] = np.asarray(inp['b_rg'][0]).reshape(4, 128).T
    lruv[:, 24:28] = np.asarray(inp['b_ig'][0]).reshape(4, 128).T
    lruv[:, 28:32] = np.asarray(inp['lru_lambda'][0]).reshape(4, 128).T
    wbd = np.zeros((128, 8, 128), np.float32)
    for gi, nm in enumerate(['w_rg', 'w_ig']):
        wsrc = np.asarray(inp[nm][0], np.float32)
        for c in range(4):
            wbd[0:64, gi * 4 + c, 0:64] = wsrc[2 * c]
            wbd[64:128, gi * 4 + c, 64:128] = wsrc[2 * c + 1]
    wout = np.ascontiguousarray(np.asarray(inp['w_out'][0], np.float32).reshape(8, 128, 1024).transpose(1, 0, 2))
    wpq = np.ascontiguousarray(
        np.asarray(inp['w_pq'][0], np.float32).reshape(8, 128, 16, 128).transpose(2, 1, 0, 3))
    k1 = np.asarray(inp['peer_k1'][0], np.float32)
    k2 = np.asarray(inp['peer_k2'][0], np.float32)
    kT = np.zeros((128, 16, 128), np.float32)
    for h in range(8):
        kT[:, 2 * h, :] = k1[h].T
        kT[:, 2 * h + 1, :] = k2[h].T
    u = np.asarray(inp['peer_u'][0], np.float32)
    uT = np.ascontiguousarray(u.reshape(128, 128, 8, 128).transpose(1, 3, 2, 0))
    v = np.ascontiguousarray(np.asarray(inp['peer_v'][0], np.float32))
    wple = np.ascontiguousarray(np.asarray(inp['w_ple'][0], np.float32).reshape(2, 128, 1024).transpose(1, 0, 2))
    wpg = np.ascontiguousarray(
        np.asarray(inp['w_ple_gate'][0], np.float32).reshape(8, 128, 1024).transpose(1, 0, 2))
    return dict(wfm=wfm, wtm=wtm, gvec=gvec, ropec=ropec, ident=ident, cmask=cmask, iota=iota,
                gkv=np.asarray(inp['g_kv'], np.float32).reshape(1, 128), wukT=wukT, wuv=wuv, lruv=lruv, wbd=wbd,
                wout=wout, wpq=wpq, kT=kT, uT=uT, v=v, wple=wple, wpg=wpg,
                gfin=np.asarray(inp['g_final'], np.float32).reshape(1, D))


def make_in_maps(inp, ncores=NCORES):
    shared = _prep_shared(inp)
    maps = []
    for b in range(ncores):
        m = dict(shared)
        m['x'] = np.ascontiguousarray(np.asarray(inp['x'][b], np.float32))
        m['pT'] = np.ascontiguousarray(np.asarray(inp['p'][0, b], np.float32).T)
        m['pos'] = np.ascontiguousarray(np.asarray(inp['positions'][b], np.int32).reshape(1, S))
        maps.append(m)
    return maps


def kernel(**inputs):
    nc = build()
    maps = make_in_maps(inputs)
    res = run_bass_kernel_spmd(nc, maps, core_ids=list(range(NCORES)))
    return np.stack([np.asarray(r["out"], np.float32) for r in res.results], axis=0)
```
